# Optimizing a Trainium2 kernel written in Bass

```python
import math
import jax, jax.numpy as jnp
from jax import lax
import numpy as np

D_MODEL = 2048
BATCH = 4
SEQ = 2048
DEPTH = 2

EPS = 1e-6
N_BRANCH = 3
POOL_WINDOWS = (2, 4, 8, 16)
POOL_WIDTH = D_MODEL // 4
POOL_GROUP = POOL_WIDTH // len(POOL_WINDOWS)
SB_HEAD_DIM = 128
SB_WIDTH = 3 * D_MODEL // 8
SB_HEADS = SB_WIDTH // SB_HEAD_DIM
SB_BLOCK = 128
GDN_HEAD_DIM = 128
GDN_WIDTH = 3 * D_MODEL // 8
GDN_HEADS = GDN_WIDTH // GDN_HEAD_DIM
GDN_CONV = 4
GDN_CHUNK = 64
D_FF = 4 * D_MODEL
IN_SIZES = (POOL_WIDTH, 3 * SB_WIDTH, 3 * GDN_WIDTH, GDN_WIDTH, GDN_HEADS, GDN_HEADS, N_BRANCH * D_MODEL)
N_IN = sum(IN_SIZES)

kernel_name = 'hybrid_pool_stickbreak_gdn_block'


def rms_norm(x, gain):
    x32 = x.astype(jnp.float32)
    y = x32 * lax.rsqrt(jnp.mean(x32 * x32, axis=-1, keepdims=True) + EPS)
    return (y * gain.astype(jnp.float32)).astype(x.dtype)


def l2_normalize(x):
    return x * lax.rsqrt(jnp.sum(x * x, axis=-1, keepdims=True) + EPS)


def pool_mixer(p, w_group, scale):
    b, s, _ = p.shape
    p32 = p.astype(jnp.float32)
    csum = jnp.cumsum(p32, axis=1)
    n_seen = jnp.arange(1, s + 1, dtype=jnp.float32)[None, :, None]
    groups = []
    for g, w in enumerate(POOL_WINDOWS):
        sl = slice(g * POOL_GROUP, (g + 1) * POOL_GROUP)
        cg = csum[..., sl]
        lagged = jnp.pad(cg[:, :s - w], ((0, 0), (w, 0), (0, 0)))
        groups.append((cg - lagged) / jnp.minimum(n_seen, w) - p32[..., sl])
    d = jnp.stack(groups, axis=2)
    y = jnp.einsum('bsgc,gcd->bsgd', d, w_group.astype(jnp.float32)).reshape(b, s, POOL_WIDTH)
    return (y * scale.astype(jnp.float32)).astype(p.dtype)


def stick_breaking_attention(q, k, v):
    b, s, h, dh = q.shape
    q32 = q.astype(jnp.float32) * (dh ** -0.5)
    k32 = k.astype(jnp.float32)
    v32 = v.astype(jnp.float32)
    outs = []
    for start in range(0, s, SB_BLOCK):
        end = start + SB_BLOCK
        z = jnp.einsum('bqhd,bkhd->bhqk', q32[:, start:end], k32[:, :end])
        mask = jnp.arange(end)[None, :] < jnp.arange(start, end)[:, None]
        log_stay = jnp.where(mask, jax.nn.log_sigmoid(-z), 0.0)
        log_later = lax.cumsum(log_stay, axis=3, reverse=True) - log_stay
        a = jnp.where(mask, jnp.exp(jax.nn.log_sigmoid(z) + log_later), 0.0)
        outs.append(jnp.einsum('bhqk,bkhd->bqhd', a, v32[:, :end]))
    return jnp.concatenate(outs, axis=1).astype(q.dtype)


def short_causal_conv(x, w):
    k, c = w.shape
    y = lax.conv_general_dilated(x, w[:, None, :], window_strides=(1,), padding=[(k - 1, 0)],
                                 dimension_numbers=('NWC', 'WIO', 'NWC'), feature_group_count=c)
    return jax.nn.silu(y)


def to_chunks(t):
    b, s, h = t.shape[:3]
    t = t.reshape((b, s // GDN_CHUNK, GDN_CHUNK, h) + t.shape[3:])
    return jnp.moveaxis(t, 3, 1)


def gated_delta_rule(q, k, v, log_alpha, beta):
    b, s, h, dk = q.shape
    dv = v.shape[-1]
    qc, kc, vc = to_chunks(q), to_chunks(k), to_chunks(v)
    g = jnp.cumsum(to_chunks(log_alpha), axis=-1)
    bc = to_chunks(beta)
    incl = jnp.tril(jnp.ones((GDN_CHUNK, GDN_CHUNK), dtype=bool))
    strict = jnp.tril(jnp.ones((GDN_CHUNK, GDN_CHUNK), dtype=bool), k=-1)
    diff = g[..., :, None] - g[..., None, :]
    gamma = jnp.where(incl, jnp.exp(jnp.where(incl, diff, 0.0)), 0.0)
    kk = jnp.einsum('bhncd,bhnmd->bhncm', kc, kc)
    lower = jnp.where(strict, bc[..., :, None] * kk * gamma, 0.0)
    unit_lower = lower + jnp.eye(GDN_CHUNK, dtype=lower.dtype)
    rhs = jnp.concatenate([vc * bc[..., None], kc * (bc * jnp.exp(g))[..., None]], axis=-1)
    sol = lax.linalg.triangular_solve(unit_lower, rhs, left_side=True, lower=True, unit_diagonal=True)
    u, w = sol[..., :dv], sol[..., dv:]
    qk = jnp.einsum('bhncd,bhnmd->bhncm', qc, kc) * gamma
    q_dec = qc * jnp.exp(g)[..., None]
    k_dec = kc * jnp.exp(g[..., -1:] - g)[..., None]
    chunk_decay = jnp.exp(g[..., -1])

    def step(state, inp):
        u_n, w_n, qk_n, qd_n, kd_n, dec_n = inp
        v_new = u_n - jnp.einsum('bhck,bhkv->bhcv', w_n, state)
        o_n = jnp.einsum('bhck,bhkv->bhcv', qd_n, state) + jnp.einsum('bhcm,bhmv->bhcv', qk_n, v_new)
        state = state * dec_n[..., None, None] + jnp.einsum('bhck,bhcv->bhkv', kd_n, v_new)
        return state, o_n

    xs = tuple(jnp.moveaxis(t, 2, 0) for t in (u, w, qk, q_dec, k_dec, chunk_decay))
    state0 = jnp.zeros((b, h, dk, dv), jnp.float32)
    _, o = lax.scan(step, state0, xs)
    return o.transpose(1, 0, 3, 2, 4).reshape(b, s, h, dv)


def gdn_mixer(qkv, z, a, bg, conv_w, a_log, dt_bias, norm_gain):
    b, s, _ = qkv.shape
    qkv = short_causal_conv(qkv, conv_w).astype(jnp.float32)
    q, k, v = jnp.split(qkv, 3, axis=-1)
    q = l2_normalize(q.reshape(b, s, GDN_HEADS, GDN_HEAD_DIM)) * (GDN_HEAD_DIM ** -0.5)
    k = l2_normalize(k.reshape(b, s, GDN_HEADS, GDN_HEAD_DIM))
    v = v.reshape(b, s, GDN_HEADS, GDN_HEAD_DIM)
    log_alpha = -jnp.exp(a_log.astype(jnp.float32)) * jax.nn.softplus(a.astype(jnp.float32) + dt_bias.astype(jnp.float32))
    beta = jax.nn.sigmoid(bg.astype(jnp.float32))
    o = gated_delta_rule(q, k, v, log_alpha, beta)
    zh = z.astype(jnp.float32).reshape(b, s, GDN_HEADS, GDN_HEAD_DIM)
    o = rms_norm(o, norm_gain) * jax.nn.silu(zh)
    return o.reshape(b, s, GDN_WIDTH).astype(z.dtype)


def hybrid_mixer(u, w_in, pool_w, pool_scale, gdn_conv, gdn_a_log, gdn_dt_bias, gdn_norm,
                 w_pool_up, w_sb_up, w_gdn_up, w_out):
    b, s, _ = u.shape
    proj = u @ w_in
    offsets = [int(o) for o in np.cumsum(IN_SIZES)[:-1]]
    p, sb_qkv, gdn_qkv, gdn_z, gdn_a, gdn_b, gates = jnp.split(proj, offsets, axis=-1)
    y_pool = pool_mixer(p, pool_w, pool_scale)
    sq, sk, sv = (t.reshape(b, s, SB_HEADS, SB_HEAD_DIM) for t in jnp.split(sb_qkv, 3, axis=-1))
    y_sb = stick_breaking_attention(sq, sk, sv).reshape(b, s, SB_WIDTH)
    y_gdn = gdn_mixer(gdn_qkv, gdn_z, gdn_a, gdn_b, gdn_conv, gdn_a_log, gdn_dt_bias, gdn_norm)
    g_pool, g_sb, g_gdn = jnp.split(jax.nn.sigmoid(gates), N_BRANCH, axis=-1)
    merged = g_pool * (y_pool @ w_pool_up) + g_sb * (y_sb @ w_sb_up) + g_gdn * (y_gdn @ w_gdn_up)
    return merged @ w_out


def squared_relu_mlp(u, w_ff1, w_ff2):
    h = jax.nn.relu(u @ w_ff1)
    return (h * h) @ w_ff2


def setup_inputs(seed: int = 0) -> dict:
    key = jax.random.key(seed)
    ks = jax.random.split(key, 20)
    f32 = jnp.float32

    def nrm(k, shape, fan_in):
        return jax.random.normal(k, shape, f32) * (fan_in ** -0.5)

    def gain(k, shape):
        return 1.0 + 0.02 * jax.random.normal(k, shape, f32)

    dt = jnp.exp(jax.random.uniform(ks[7], (DEPTH, GDN_HEADS), f32, math.log(1e-3), math.log(1e-1)))
    return {
        'x': jax.random.normal(ks[0], (BATCH, SEQ, D_MODEL), f32),
        'attn_norm': gain(ks[1], (DEPTH, D_MODEL)),
        'w_in': nrm(ks[2], (DEPTH, D_MODEL, N_IN), D_MODEL),
        'pool_w': nrm(ks[3], (DEPTH, len(POOL_WINDOWS), POOL_GROUP, POOL_GROUP), POOL_GROUP),
        'pool_scale': gain(ks[4], (DEPTH, POOL_WIDTH)),
        'gdn_conv': nrm(ks[5], (DEPTH, GDN_CONV, 3 * GDN_WIDTH), GDN_CONV),
        'gdn_a_log': jnp.log(jax.random.uniform(ks[6], (DEPTH, GDN_HEADS), f32, 1.0, 16.0)),
        'gdn_dt_bias': dt + jnp.log(-jnp.expm1(-dt)),
        'gdn_norm': gain(ks[8], (DEPTH, GDN_HEAD_DIM)),
        'w_pool_up': nrm(ks[9], (DEPTH, POOL_WIDTH, D_MODEL), POOL_WIDTH),
        'w_sb_up': nrm(ks[10], (DEPTH, SB_WIDTH, D_MODEL), SB_WIDTH),
        'w_gdn_up': nrm(ks[11], (DEPTH, GDN_WIDTH, D_MODEL), GDN_WIDTH),
        'w_out': nrm(ks[12], (DEPTH, D_MODEL, D_MODEL), D_MODEL),
        'mlp_norm': gain(ks[13], (DEPTH, D_MODEL)),
        'w_ff1': nrm(ks[14], (DEPTH, D_MODEL, D_FF), D_MODEL),
        'w_ff2': nrm(ks[15], (DEPTH, D_FF, D_MODEL), D_FF),
        'final_norm': gain(ks[16], (D_MODEL,)),
    }


def reference(x, attn_norm, w_in, pool_w, pool_scale, gdn_conv, gdn_a_log, gdn_dt_bias, gdn_norm,
              w_pool_up, w_sb_up, w_gdn_up, w_out, mlp_norm, w_ff1, w_ff2, final_norm):
    for l in range(DEPTH):
        u = rms_norm(x, attn_norm[l])
        x = x + hybrid_mixer(u, w_in[l], pool_w[l], pool_scale[l], gdn_conv[l], gdn_a_log[l],
                             gdn_dt_bias[l], gdn_norm[l], w_pool_up[l], w_sb_up[l], w_gdn_up[l], w_out[l])
        x = x + squared_relu_mlp(rms_norm(x, mlp_norm[l]), w_ff1[l], w_ff2[l])
    return rms_norm(x, final_norm)
```

```python
import numpy as np
from contextlib import ExitStack
import concourse.bass as bass
import concourse.mybir as mybir
from concourse.bass_utils import run_bass_kernel_spmd

F32 = mybir.dt.float32
BF16 = mybir.dt.bfloat16
AF = mybir.ActivationFunctionType
ALU = mybir.AluOpType
AX = mybir.AxisListType

D = 2048
SEQ = 2048
NB = 4
DFF = 8192
EPS = 1e-6
NEG = -30000.0
SAME_ENGINE_SYNC = True
ENGMAP = {"pe": "tensor", "act": "scalar", "dve": "vector", "pool": "gpsimd", "sp": "sync"}


class Reg:
    __slots__ = ("w", "r")

    def __init__(self):
        self.w = {}
        self.r = {}


class Sched:
    ENGS = ("pe", "act", "dve", "pool", "sp")

    def __init__(self, nc, es):
        self.nc = nc
        self.es = es
        self.count = {e: 0 for e in self.ENGS}
        self.waited = {e: {} for e in self.ENGS}
        self.sems = {}
        self.dcount = {}
        for e in self.ENGS:
            self.sems[e] = es.enter_context(nc.semaphore("s_" + e))

    def dsem(self, name):
        if name not in self.sems:
            self.sems[name] = self.es.enter_context(self.nc.semaphore("d_" + name))
            self.dcount[name] = 0
        return self.sems[name]

    def _waits(self, eng, reads, writes):
        need = {}
        for r in reads:
            for s, v in r.w.items():
                if v > need.get(s, 0):
                    need[s] = v
        for w in writes:
            for s, v in w.w.items():
                if v > need.get(s, 0):
                    need[s] = v
            for s, v in w.r.items():
                if v > need.get(s, 0):
                    need[s] = v
        out = []
        wd = self.waited[eng]
        for s, v in need.items():
            if s == eng and (eng == "pe" or not SAME_ENGINE_SYNC):
                continue
            if s in self.count:
                assert v <= self.count[s], f"wait on unsignaled op of {s}: {v} > {self.count[s]}"
            if wd.get(s, 0) >= v:
                continue
            wd[s] = v
            out.append((s, v))
        return out

    def _emit1(self, e, waits, fn, sig):
        eng = getattr(self.nc, ENGMAP[e])
        for s, v in waits:
            eng.wait_ge(self.sems[s], v)
        if fn is not None:
            ins = fn(eng)
            if sig is not None:
                ins.then_inc(self.sems[sig[0]], sig[1])

    def op(self, eng, fn, reads=(), writes=(), signal=True):
        waits = self._waits(eng, reads, writes)
        val = self.count[eng] + 1
        if signal:
            self.count[eng] = val
        self._emit1(eng, waits, fn, (eng, 1) if signal else None)
        for w in writes:
            w.w = {eng: val}
            w.r = {}
        for r in reads:
            if r.r.get(eng, 0) < val:
                r.r[eng] = val

    def dma(self, eng, fn, reads=(), writes=(), sem="dma"):
        self.dsem(sem)
        waits = self._waits(eng, reads, writes)
        self.dcount[sem] += 16
        val = self.dcount[sem]
        self._emit1(eng, waits, fn, (sem, 16))
        for w in writes:
            w.w = {sem: val}
            w.r = {}
        for r in reads:
            if r.r.get(sem, 0) < val:
                r.r[sem] = val

    def wait_all(self, eng, regs):
        waits = self._waits(eng, regs, ())
        self._emit1(eng, waits, None, None)

    def barrier(self):
        allv = dict(self.count)
        allv.update(self.dcount)
        for e in self.ENGS:
            waits = []
            for s, v in allv.items():
                if s == e or v == 0:
                    continue
                if self.waited[e].get(s, 0) >= v:
                    continue
                self.waited[e][s] = v
                waits.append((s, v))
            self._emit1(e, waits, None, None)


class Ctx:
    def __init__(self, nc, S, es):
        self.nc, self.S, self.es = nc, S, es
        self.pb = []
        self.Rpb = []
        for i in range(8):
            self.pb.append(es.enter_context(nc.psum_tensor(f"pb{i}", [128, 512], F32)))
            self.Rpb.append(Reg())
        self.pbi = 0
        self.n = 0

    def sb(self, shape, dt, es=None, name=None):
        self.n += 1
        return (es or self.es).enter_context(self.nc.sbuf_tensor((name or "t") + f"_{self.n}", shape, dt))

    def bank(self, lo=0, hi=8):
        i = lo + (self.pbi % (hi - lo))
        self.pbi += 1
        return self.pb[i], self.Rpb[i]

    def mm(self, out, lhsT, rhs, start, stop, reads, writes, signal=True):
        self.S.op("pe", lambda e: e.matmul(out, lhsT=lhsT, rhs=rhs, start=start, stop=stop),
                  reads=reads, writes=writes, signal=signal)

    def tr(self, out, in_, ident, reads, writes, signal=True):
        self.S.op("pe", lambda e: e.transpose(out, in_, ident), reads=reads, writes=writes, signal=signal)

    def act(self, out, in_, func, reads, writes, eng="act", **kw):
        self.S.op("act", lambda e: e.activation(out=out, in_=in_, func=func, **kw), reads=reads, writes=writes)

    def tt(self, eng, out, in0, in1, op, reads, writes):
        self.S.op(eng, lambda e: e.tensor_tensor(out=out, in0=in0, in1=in1, op=op), reads=reads, writes=writes)

    def ts(self, eng, out, in0, s1, s2, op0, op1, reads, writes):
        if op1 is None:
            self.S.op(eng, lambda e: e.tensor_scalar(out=out, in0=in0, scalar1=s1, scalar2=None, op0=op0),
                      reads=reads, writes=writes)
        else:
            self.S.op(eng, lambda e: e.tensor_scalar(out=out, in0=in0, scalar1=s1, scalar2=s2, op0=op0, op1=op1),
                      reads=reads, writes=writes)

    def stt(self, eng, out, in0, scalar, in1, op0, op1, reads, writes):
        self.S.op(eng, lambda e: e.scalar_tensor_tensor(out=out, in0=in0, scalar=scalar, in1=in1, op0=op0, op1=op1),
                  reads=reads, writes=writes)

    def cp(self, eng, out, in_, reads, writes):
        if eng == "act":
            self.S.op("act", lambda e: e.activation(out=out, in_=in_, func=AF.Copy), reads=reads, writes=writes)
        else:
            self.S.op(eng, lambda e: e.tensor_copy(out=out, in_=in_), reads=reads, writes=writes)

    def load(self, q, out, in_, writes, sem):
        self.S.dma(q, lambda e: e.dma_start(out=out, in_=in_), writes=writes, sem=sem)

    def store(self, q, out, in_, reads, writes, sem):
        self.S.dma(q, lambda e: e.dma_start(out=out, in_=in_), reads=reads, writes=writes, sem=sem)


def norm_transpose(c, xt_ap, Rxt, gain_sb, Rgain, identf, Rid, uT, RuT, tok0, scr, banks=(0, 8)):
    sq, Rsq, ss, Rss, u32, Ru32 = scr
    c.S.op("act", lambda e: e.activation(out=sq[:], in_=xt_ap, func=AF.Square, accum_out=ss[:, 0:1]),
           reads=[Rxt], writes=[Rsq, Rss])
    c.act(ss[:, 1:2], ss[:, 0:1], AF.Ln, [Rss], [Rss], scale=1.0 / D, bias=EPS)
    c.act(ss[:, 2:3], ss[:, 1:2], AF.Exp, [Rss], [Rss], scale=-0.5)
    c.stt("dve", u32[:], xt_ap, ss[:, 2:3], gain_sb[:], ALU.mult, ALU.mult, [Rxt, Rss, Rgain], [Ru32])
    for j in range(4):
        pb, Rp = c.bank(*banks)
        for i in range(4):
            kc = 4 * j + i
            c.tr(pb[:, i * 128:(i + 1) * 128], u32[:, kc * 128:(kc + 1) * 128], identf[:],
                 [Ru32, Rid], [Rp], signal=(i == 3))
        eng = "act" if j % 2 == 0 else "dve"
        c.cp(eng, uT[:, 4 * j:4 * j + 4, tok0:tok0 + 128],
             pb[:].rearrange("p (a b) -> p a b", a=4), [Rp], [RuT])


NFB = 22


def build_p1():
    nc = bass.Bass("TRN2", target_bir_lowering=False)
    dt = lambda name, shape, kind="ExternalInput", dty=F32: nc.dram_tensor(name, shape, dty, kind=kind).ap()
    x_d = dt("x", [SEQ, D])
    y_d = dt("y", [1280, SEQ], kind="ExternalOutput", dty=BF16)
    wd = dict(
        gain=dt("gain", [128, D]), wF=dt("wF", [NFB, 128, 16, 128]), wV=dt("wV", [128, 16, 384]),
        wAB=dt("wAB", [128, 16, 6]), cst=dt("cst", [128, 5, 128]), mb=dt("mb", [128, 4, 512]),
        c64=dt("c64", [64, 5, 64]), poolw=dt("poolw", [128, 4, 128]), poolc=dt("poolc", [128, 4, 18]),
        convw=dt("convw", [128, 9, 4]), gsc=dt("gsc", [64, 3, 32, 3]), gn=dt("gn", [128, 1]))
    with ExitStack() as es:
        S = Sched(nc, es)
        c = Ctx(nc, S, es)
        Ry = Reg()
        emit_p1(c, S, es, x_d, [y_d], [wd], Ry)
        S.barrier()
        S.wait_all("sp", [Ry])
    return nc


class _Stop(Exception):
    pass


def emit_p1(c, S, es0, x_d, y_ds, wds, Ry):
    try:
        _emit_p1(c, S, es0, x_d, y_ds, wds, Ry)
    except _Stop:
        S.barrier()


def _chk(c, name):
    import os
    if os.environ.get("P1_STOP") == name:
        c.S.barrier()
        raise _Stop()


def _emit_p1(c, S, es0, x_d, y_ds, wds, Ry):
    wd = wds[0]
    with ExitStack() as es:
        cstf = c.sb([128, 5, 128], F32, es)
        cstb = c.sb([128, 5, 128], BF16, es)
        c64f = c.sb([64, 5, 64], F32, es)
        poolw = c.sb([128, 4, 128], BF16, es)
        poolc = c.sb([128, 4, 18], F32, es)
        convw = c.sb([128, 9, 4], F32, es)
        gsc = c.sb([64, 3, 32, 3], F32, es)
        gn = c.sb([128, 1], F32, es)
        Rc = Reg()
        c.load("sp", cstf[:], wd["cst"], [Rc], "c0")
        c.load("pool", cstb[:], wd["cst"], [Rc], "c1")
        c.load("sp", c64f[:], wd["c64"], [Rc], "c3")
        c.load("pool", poolw[:], wd["poolw"], [Rc], "c5")
        c.load("sp", poolc[:], wd["poolc"], [Rc], "c6")
        S.barrier()
        identf = cstf[:, 0, :]
        identb = cstb[:, 0, :]
        TIb = cstb[:, 1, :]
        TSb = cstb[:, 2, :]
        onesb = cstb[:, 3, :]
        Rid = Rc

        uT = c.sb([128, 16, SEQ], BF16, es, name="uT")
        RuT = Reg()

        with ExitStack() as pes:
            gain_sb = c.sb([128, D], F32, pes)
            c.load("sp", gain_sb[:], wd["gain"], [Rc], "c10")
            S.barrier()
            xt = [c.sb([128, D], F32, pes) for _ in range(2)]
            Rxt = [Reg(), Reg()]
            scrs = []
            for _ in range(2):
                sq = c.sb([128, D], BF16, pes)
                ss = c.sb([128, 4], F32, pes)
                u32 = c.sb([128, D], F32, pes)
                scrs.append((sq, Reg(), ss, Reg(), u32, Reg()))
            RuTs = [Reg() for _ in range(16)]
            for tt in range(16):
                b = tt % 2
                c.load("sp", xt[b][:], x_d[tt * 128:(tt + 1) * 128, :], [Rxt[b]], f"x{b}")
                norm_transpose(c, xt[b][:], Rxt[b], gain_sb, Rc, identf, Rid, uT, RuTs[tt], tt * 128, scrs[b])
            S.barrier()

        _chk(c, "A")
        NWB = 3
        wcount = [0]

        def proj_blocks(blocks, wblk, Rw, post):
            def load_w(k):
                s = wcount[0] % NWB
                wcount[0] += 1
                c.load("pool", wblk[s][:], wF_d[blocks[k]], [Rw[s]], f"w{s}")
                return s
            slots = {}
            for k in range(min(2, len(blocks))):
                slots[k] = load_w(k)
            for k, i in enumerate(blocks):
                if k + 2 < len(blocks):
                    slots[k + 2] = load_w(k + 2)
                s = slots[k]
                for tb in range(4):
                    pb, Rp = c.bank()
                    for kc in range(16):
                        c.mm(pb[:], wblk[s][:, kc, :], uT[:, kc, tb * 512:(tb + 1) * 512], kc == 0, kc == 15,
                             [Rw[s], RuT], [Rp], signal=(kc == 15))
                    post(i, tb, pb, Rp)
                post(i, None, None, None)

        for hi, wd in enumerate(wds):
            y_d = y_ds[hi]
            wF_d, wV_d, wAB_d = wd["wF"], wd["wV"], wd["wAB"]
            Rc2 = Reg()
            c.load("sp", convw[:], wd["convw"], [Rc2], "c7")
            c.load("sp", gsc[:], wd["gsc"], [Rc2], "c8")
            c.load("sp", gn[:], wd["gn"], [Rc2], "c9")
            S.barrier()
            with ExitStack() as ses:
                sQT = c.sb([128, 3, SEQ], BF16, ses, name="sQT")
                sKT = c.sb([128, 3, SEQ], BF16, ses, name="sKT")
                sV = c.sb([128, 16, 384], BF16, ses, name="sV")
                mbb = c.sb([128, 4, 512], BF16, ses)
                Rmb = Reg()
                c.load("pool", mbb[:], wd["mb"], [Rmb], "c2")
                RsQ = [Reg() for _ in range(3)]
                RsK = [Reg() for _ in range(3)]
                RsV = Reg()
                with ExitStack() as pes:
                    wblk = [c.sb([128, 16, 128], BF16, pes) for _ in range(NWB)]
                    Rw = [Reg() for _ in range(NWB)]
                    wV = c.sb([128, 16, 384], BF16, pes)
                    RwV = Reg()
                    c.load("pool", wV[:], wV_d, [RwV], "wv")
                    big = c.sb([128, 16 + SEQ], F32, pes, name="big")
                    Rbig = Reg()
                    t1 = c.sb([128, SEQ], F32, pes, name="t1")
                    t2 = c.sb([128, SEQ], F32, pes, name="t2")
                    Rt1, Rt2 = Reg(), Reg()
                    tb16 = c.sb([128, SEQ], BF16, pes, name="tb16")
                    Rtb = Reg()
                    yst = c.sb([128, SEQ], BF16, pes, name="yst")
                    Ryst = Reg()
                    c.S.op("dve", lambda e: e.memset(big[:, 0:16], 0.0), writes=[Rbig])

                    def post_sb(i, tb, pb, Rp):
                        if tb is not None:
                            tsl = slice(tb * 512, (tb + 1) * 512)
                            if i < 4:
                                c.cp("act" if tb % 2 == 0 else "dve", big[:, 16 + tb * 512:16 + (tb + 1) * 512], pb[:], [Rp], [Rbig])
                            elif i < 7:
                                c.cp("act", sQT[:, i - 4, tsl], pb[:], [Rp], [RsQ[i - 4]])
                            else:
                                c.cp("act", sKT[:, i - 7, tsl], pb[:], [Rp], [RsK[i - 7]])
                            return
                        if i >= 4:
                            return
                        g = i
                        p = big[:, 16:16 + SEQ]
                        bufs = [t1, t2]
                        Rb = [Rt1, Rt2]
                        sh = 1
                        for lv in range(g + 1):
                            o = bufs[lv % 2]
                            if lv == 0:
                                c.tt("dve", o[:], p, big[:, 15:15 + SEQ], ALU.add, [Rbig], [Rb[0]])
                            else:
                                prev = bufs[(lv - 1) % 2]
                                c.tt("dve", o[:, sh:], prev[:, sh:], prev[:, 0:SEQ - sh], ALU.add, [Rb[(lv - 1) % 2]], [Rb[lv % 2]])
                                c.cp("dve", o[:, 0:sh], prev[:, 0:sh], [Rb[(lv - 1) % 2]], [Rb[lv % 2]])
                            sh *= 2
                        sw, Rsw = bufs[g % 2], Rb[g % 2]
                        dd, Rdd = bufs[(g + 1) % 2], Rb[(g + 1) % 2]
                        c.stt("dve", dd[:, 16:], sw[:, 16:], poolc[:, g, 1:2], big[:, 32:16 + SEQ], ALU.mult, ALU.subtract,
                              [Rsw, Rbig, Rc], [Rdd])
                        c.tt("dve", dd[:, 0:16], sw[:, 0:16], poolc[:, g, 2:18], ALU.mult, [Rsw, Rc], [Rdd])
                        c.tt("dve", dd[:, 0:16], dd[:, 0:16], big[:, 16:32], ALU.subtract, [Rdd, Rbig], [Rdd])
                        c.cp("act", tb16[:], dd[:], [Rdd], [Rtb])
                        for tb2 in range(4):
                            pb2, Rp2 = c.bank()
                            c.mm(pb2[:], poolw[:, g, :], tb16[:, tb2 * 512:(tb2 + 1) * 512], True, True, [Rtb, Rc], [Rp2])
                            c.ts("dve", yst[:, tb2 * 512:(tb2 + 1) * 512], pb2[:], poolc[:, g, 0:1], None, ALU.mult, None,
                                 [Rp2, Rc], [Ryst])
                        c.store("sp", y_d[g * 128:(g + 1) * 128, :], yst[:], [Ryst], [Ry], "yo")

                    proj_blocks(list(range(10)) if hi == 0 else list(range(4, 10)), wblk, Rw, post_sb)
                    _chk(c, "SBP")
                    for tt in range(16):
                        pb, Rp = c.bank()
                        for kc in range(16):
                            c.mm(pb[:, 0:384], uT[:, kc, tt * 128:(tt + 1) * 128], wV[:, kc, :], kc == 0, kc == 15,
                                 [RwV, RuT], [Rp], signal=(kc == 15))
                        c.cp("act" if tt % 2 else "dve", sV[:, tt, :], pb[:, 0:384], [Rp], [RsV])
                    S.barrier()

                _chk(c, "SBV")
                with ExitStack() as pes:
                    NE = 6
                    Eb = [c.sb([128, 512], F32, pes) for _ in range(NE)]
                    SPb = [c.sb([128, 512], BF16, pes) for _ in range(NE)]
                    Xb = [c.sb([128, 512], F32, pes) for _ in range(NE)]
                    Ab = [c.sb([128, 512], BF16, pes) for _ in range(NE)]
                    RE = [Reg() for _ in range(NE)]
                    RSP = [Reg() for _ in range(NE)]
                    RX = [Reg() for _ in range(NE)]
                    RA = [Reg() for _ in range(NE)]
                    ost = [c.sb([128, 512], BF16, pes) for _ in range(2)]
                    Rost = [Reg(), Reg()]
                    scale = float(128 ** -0.5)
                    items = []
                    for QB in range(4):
                        for kb in range(4 * QB + 3, -1, -1):
                            for h in range(3):
                                items.append((QB, kb, h))
                    n_it = len(items)
                    oc = [0]

                    def stage0(idx):
                        QB, kb, h = items[idx]
                        s = idx % NE
                        zb, Rz = c.pb[idx % 2], c.Rpb[idx % 2]
                        q0 = QB * 512
                        diag = kb >= 4 * QB
                        c.mm(zb[:], sKT[:, h, kb * 128:(kb + 1) * 128], sQT[:, h, q0:q0 + 512], True, not diag,
                             [RsK[h], RsQ[h]], [Rz], signal=not diag)
                        if diag:
                            c.mm(zb[:], identb, mbb[:, kb - 4 * QB, :], False, True, [Rc, Rmb], [Rz])
                        c.act(Eb[s][:], zb[:], AF.Exp, [Rz], [RE[s]], scale=scale)
                        c.act(SPb[s][:], Eb[s][:], AF.Ln, [RE[s]], [RSP[s]], bias=1.0)

                    def stage1(idx):
                        QB, kb, h = items[idx]
                        s = idx % NE
                        cb, Rcb = c.pb[2 + h], c.Rpb[2 + h]
                        first = kb == 4 * QB + 3
                        c.mm(cb[:], TIb, SPb[s][:], first, True, [Rc, RSP[s]], [Rcb])
                        c.act(Xb[s][:], cb[:], AF.Exp, [Rcb], [RX[s]], scale=-1.0)

                    def stage2(idx):
                        QB, kb, h = items[idx]
                        s = idx % NE
                        cb, Rcb = c.pb[2 + h], c.Rpb[2 + h]
                        c.mm(cb[:], TSb, SPb[s][:], False, True, [Rc, RSP[s]], [Rcb])
                        c.tt("dve" if idx % 2 else "pool", Ab[s][:], Eb[s][:], Xb[s][:], ALU.mult, [RE[s], RX[s]], [RA[s]])

                    def stage3(idx):
                        QB, kb, h = items[idx]
                        s = idx % NE
                        ob, Rob = c.pb[5 + h], c.Rpb[5 + h]
                        first = kb == 4 * QB + 3
                        c.mm(ob[:], sV[:, kb, h * 128:(h + 1) * 128], Ab[s][:], first, kb == 0, [RsV, RA[s]], [Rob])
                        if kb == 0:
                            o = oc[0] % 2
                            oc[0] += 1
                            c.cp("dve", ost[o][:], ob[:], [Rob], [Rost[o]])
                            c.store("sp", y_d[512 + h * 128:512 + (h + 1) * 128, QB * 512:(QB + 1) * 512], ost[o][:],
                                    [Rost[o]], [Ry], f"so{o}")

                    for idx in range(n_it + 3):
                        if idx < n_it:
                            stage0(idx)
                        if 1 <= idx < n_it + 1:
                            stage1(idx - 1)
                        if 2 <= idx < n_it + 2:
                            stage2(idx - 2)
                        if 3 <= idx:
                            stage3(idx - 3)
                    S.barrier()

            _chk(c, "SB")
            with ExitStack() as ges:
                tri64 = c64f[:, 0, :]
                id64f = c64f[:, 3, :]
                onesf128 = cstf[0:64, 3, :]
                abc = c.sb([64, 32, 6], F32, ges, name="abc")
                Rabc = Reg()
                la = c.sb([64, 32, 3], F32, ges)
                beta = c.sb([64, 32, 3], F32, ges)
                gcol = c.sb([64, 32, 3], F32, ges)
                glast = c.sb([64, 32, 3], F32, ges)
                egcol = c.sb([64, 32, 3], F32, ges)
                sc_kbg = c.sb([64, 32, 3], F32, ges)
                sc_kdec = c.sb([64, 32, 3], F32, ges)
                decB = c.sb([128, 32, 3], F32, ges)
                tmp = c.sb([64, 32, 3], F32, ges)
                nA = c.sb([64, 32, 3], F32, ges)
                Rs = Reg()
                with ExitStack() as pes:
                    wAB = c.sb([128, 16, 6], BF16, pes)
                    RwAB = Reg()
                    c.load("pool", wAB[:], wAB_d, [RwAB], "wv")
                    pb, Rp = c.bank()
                    for n in range(32):
                        for kc in range(16):
                            c.mm(pb[0:64, n * 6:(n + 1) * 6], uT[:, kc, n * 64:(n + 1) * 64], wAB[:, kc, :], kc == 0, kc == 15,
                                 [RwAB, RuT], [Rp], signal=(kc == 15 and n == 31))
                    c.cp("dve", abc[:].rearrange("p a b -> p (a b)"), pb[0:64, 0:192], [Rp], [Rabc])
                    a_ap = abc[:, :, 0:3]
                    b_ap = abc[:, :, 3:6]
                    c.tt("dve", tmp[:], a_ap, gsc[:, 0, :, :], ALU.add, [Rabc, Rc], [Rs])
                    c.act(tmp[:], tmp[:], AF.Exp, [Rs], [Rs])
                    c.act(tmp[:], tmp[:], AF.Ln, [Rs], [Rs], bias=1.0)
                    c.act(nA[:], gsc[:, 1, :, :], AF.Exp, [Rc, Rs], [Rs])
                    c.stt("dve", la[:], tmp[:], -1.0, nA[:], ALU.mult, ALU.mult, [Rs], [Rs])
                    c.act(tmp[:], b_ap, AF.Exp, [Rabc, Rs], [Rs], scale=-1.0)
                    c.ts("dve", tmp[:], tmp[:], 1.0, None, ALU.add, None, [Rs], [Rs])
                    c.S.op("dve", lambda e: e.reciprocal(out=beta[:], in_=tmp[:]), reads=[Rs], writes=[Rs])
                    la2 = la[:].rearrange("p a b -> p (a b)")
                    pb, Rp = c.bank()
                    c.mm(pb[0:64, 0:96], tri64, la2, True, True, [Rc, Rs], [Rp])
                    c.cp("dve", gcol[:].rearrange("p a b -> p (a b)"), pb[0:64, 0:96], [Rp], [Rs])
                    pb, Rp = c.bank()
                    c.mm(pb[:, 0:96], onesf128, la2, True, True, [Rc, Rs], [Rp])
                    c.cp("dve", glast[:].rearrange("p a b -> p (a b)"), pb[0:64, 0:96], [Rp], [Rs])
                    c.act(decB[:].rearrange("p a b -> p (a b)"), pb[:, 0:96], AF.Exp, [Rp, Rs], [Rs])
                    c.act(egcol[:], gcol[:], AF.Exp, [Rs], [Rs])
                    c.tt("dve", sc_kbg[:], egcol[:], beta[:], ALU.mult, [Rs], [Rs])
                    c.tt("dve", tmp[:], glast[:], gcol[:], ALU.subtract, [Rs], [Rs])
                    c.act(sc_kdec[:], tmp[:], AF.Exp, [Rs], [Rs])
                    S.barrier()

                _chk(c, "GAB")
                for h in range(3):
                    with ExitStack() as hes:
                        gQT = c.sb([128, SEQ], BF16, hes, name=f"gQT{h}")
                        gKT = c.sb([128, SEQ], BF16, hes, name=f"gKT{h}")
                        gKtok = c.sb([64, 32, 128], BF16, hes, name=f"gKtok{h}")
                        gVtok = c.sb([64, 32, 128], BF16, hes, name=f"gVtok{h}")
                        gZ = c.sb([128, SEQ], BF16, hes, name=f"gZ{h}")
                        P = c.sb([64, 32, 64], BF16, hes, name=f"gP{h}")
                        vb = c.sb([64, 32, 128], BF16, hes, name=f"gvb{h}")
                        kdec = c.sb([64, 32, 128], BF16, hes, name=f"gkdec{h}")
                        nwT = c.sb([128, SEQ], BF16, hes, name=f"gnwT{h}")
                        qdT = c.sb([128, SEQ], BF16, hes, name=f"gqdT{h}")
                        QKg = c.sb([64, 32, 64], BF16, hes, name=f"gQKg{h}")
                        RgQ, RgK, RgKt, RgVt, RgZ, RP, Rvb, Rkd, Rnw, Rqd, Rqk = [Reg() for _ in range(11)]
                        with ExitStack() as pes:
                            wblk = [c.sb([128, 16, 128], BF16, pes) for _ in range(NWB)]
                            Rw = [Reg() for _ in range(NWB)]
                            big = c.sb([128, 16 + SEQ], F32, pes, name=f"gbig{h}")
                            Rbig = Reg()
                            t1 = c.sb([128, SEQ], F32, pes, name=f"gt1{h}")
                            t2 = c.sb([128, SEQ], F32, pes, name=f"gt2{h}")
                            Rt1, Rt2 = Reg(), Reg()
                            tb16 = c.sb([128, SEQ], BF16, pes, name=f"gtb16{h}")
                            Rtb = Reg()
                            c.S.op("dve", lambda e: e.memset(big[:, 0:16], 0.0), writes=[Rbig])

                            def post_g(i, tb, pb, Rp):
                                kind = (i - 10) // 3
                                if tb is not None:
                                    tsl = slice(tb * 512, (tb + 1) * 512)
                                    if kind < 3:
                                        c.cp("act" if tb % 2 == 0 else "dve", big[:, 16 + tb * 512:16 + (tb + 1) * 512], pb[:],
                                             [Rp], [Rbig])
                                    else:
                                        c.act(gZ[:, tsl], pb[:], AF.Silu, [Rp], [RgZ])
                                    return
                                if kind == 3:
                                    return
                                j = kind * 3 + h
                                c.ts("dve", t1[:], big[:, 13:13 + SEQ], convw[:, j, 0:1], None, ALU.mult, None, [Rbig, Rc], [Rt1])
                                for ci in range(1, 4):
                                    c.stt("dve", t1[:], big[:, 13 + ci:13 + ci + SEQ], convw[:, j, ci:ci + 1], t1[:],
                                          ALU.mult, ALU.add, [Rbig, Rc, Rt1], [Rt1])
                                c.act(t2[:], t1[:], AF.Silu, [Rt1], [Rt2])
                                if kind < 2:
                                    c.act(tb16[:], t2[:], AF.Square, [Rt2], [Rtb])
                                    for tb2 in range(4):
                                        pb2, Rp2 = c.bank()
                                        c.mm(pb2[:], onesb, tb16[:, tb2 * 512:(tb2 + 1) * 512], True, True, [Rtb, Rc], [Rp2])
                                        c.act(t1[:, tb2 * 512:(tb2 + 1) * 512], pb2[:], AF.Ln, [Rp2], [Rt1], bias=EPS)
                                    c.act(t1[:], t1[:], AF.Exp, [Rt1], [Rt1], scale=-0.5)
                                    if kind == 0:
                                        c.stt("dve", gQT[:], t2[:], float(128 ** -0.5), t1[:], ALU.mult, ALU.mult, [Rt1, Rt2], [RgQ])
                                    else:
                                        c.tt("dve", t2[:], t2[:], t1[:], ALU.mult, [Rt1, Rt2], [Rt2])
                                        c.cp("act", gKT[:], t2[:], [Rt2], [RgK])
                                if kind >= 1:
                                    dst, Rdst = (gKtok, RgKt) if kind == 1 else (gVtok, RgVt)
                                    for grp in range(8):
                                        pb2, Rp2 = c.bank()
                                        for q in range(4):
                                            n = grp * 4 + q
                                            c.tr(pb2[0:64, q * 128:(q + 1) * 128], t2[:, n * 64:(n + 1) * 64], identf, [Rt2, Rid],
                                                 [Rp2], signal=(q == 3))
                                        c.cp("act" if grp % 2 else "dve", dst[:, grp * 4:grp * 4 + 4, :],
                                             pb2[0:64, :].rearrange("p (a b) -> p a b", a=4), [Rp2], [Rdst])

                            proj_blocks([10 + h, 13 + h, 16 + h, 19 + h], wblk, Rw, post_g)
                            S.barrier()

                        _chk(c, "GP")
                        with ExitStack() as pes:
                            labt = c.sb([64, 32, 64], F32, pes)
                            Dm = c.sb([64, 32, 64], F32, pes)
                            gamL = c.sb([64, 32, 64], F32, pes)
                            egB = c.sb([128, SEQ], F32, pes)
                            kbg = c.sb([64, 32, 128], BF16, pes)
                            Mk = [c.sb([64, 32, 64], BF16, pes) for _ in range(2)]
                            Nk = [c.sb([64, 32, 64], BF16, pes) for _ in range(2)]
                            INk = c.sb([64, 32, 64], BF16, pes)
                            Pt = [c.sb([64, 32, 64], BF16, pes) for _ in range(2)]
                            Rl, RD, RgL, ReB, Rkbg, RIN = [Reg() for _ in range(6)]
                            RMk = [[Reg() for _ in range(4)] for _ in range(2)]
                            RNk = [[Reg() for _ in range(4)] for _ in range(2)]
                            RPt = [[Reg() for _ in range(4)] for _ in range(2)]
                            RINg = [Reg() for _ in range(4)]
                            RNfg = [Reg() for _ in range(4)]
                            c.tt("dve", labt[:], la[:, :, h:h + 1].to_broadcast([64, 32, 64]),
                                 c64f[:, 0:1, :].to_broadcast([64, 32, 64]), ALU.mult, [Rs, Rc], [Rl])
                            _chk(c, "G0a")
                            l2 = labt[:].rearrange("p a b -> p (a b)")
                            for tb in range(4):
                                pb, Rp = c.bank()
                                c.mm(pb[:], onesf128, l2[:, tb * 512:(tb + 1) * 512], True, True, [Rc, Rl], [Rp])
                                c.act(egB[:, tb * 512:(tb + 1) * 512], pb[:], AF.Exp, [Rp], [ReB])
                                c.tt("dve", Dm[:, tb * 8:(tb + 1) * 8, :],
                                     gcol[:, tb * 8:(tb + 1) * 8, h:h + 1].to_broadcast([64, 8, 64]),
                                     pb[0:64, :].rearrange("p (a b) -> p a b", a=8), ALU.subtract, [Rp, Rs, ReB], [RD])
                            _chk(c, "G0b")
                            c.ts("dve", gamL[:], Dm[:], 0.0, None, ALU.min, None, [RD], [RgL])
                            c.ts("dve", Dm[:], Dm[:], -1.0, 0.0, ALU.mult, ALU.min, [RD, RgL], [RD])
                            gamU, RgU = Dm, RD
                            c.act(gamL[:], gamL[:], AF.Exp, [RgL], [RgL])
                            c.act(gamU[:], gamU[:], AF.Exp, [RgU], [RgU])
                            c.tt("dve", gamL[:], gamL[:], c64f[:, 1:2, :].to_broadcast([64, 32, 64]), ALU.mult, [RgL, Rc], [RgL])
                            c.tt("dve", gamU[:], gamU[:], c64f[:, 2:3, :].to_broadcast([64, 32, 64]), ALU.mult, [RgU, Rc], [RgU])
                            _chk(c, "G0c")
                            c.tt("dve", qdT[:], gQT[:], egB[:], ALU.mult, [RgQ, ReB], [Rqd])
                            c.tt("dve", kbg[:], gKtok[:], sc_kbg[:, :, h:h + 1].to_broadcast([64, 32, 128]), ALU.mult,
                                 [RgKt, Rs], [Rkbg])
                            c.tt("dve", kdec[:], gKtok[:], sc_kdec[:, :, h:h + 1].to_broadcast([64, 32, 128]), ALU.mult,
                                 [RgKt, Rs], [Rkd])
                            c.tt("dve", vb[:], gVtok[:], beta[:, :, h:h + 1].to_broadcast([64, 32, 128]), ALU.mult,
                                 [RgVt, Rs], [Rvb])
                            _chk(c, "G1")
                            Nf = labt
                            S.wait_all("dve", [Rl])
                            for grp in range(4):
                                pb, Rp = c.bank()
                                for q in range(8):
                                    n = grp * 8 + q
                                    ks = gKT[:, n * 64:(n + 1) * 64]
                                    c.mm(pb[0:64, q * 64:(q + 1) * 64], ks, ks, True, True, [RgK], [Rp], signal=(q == 7))
                                gs = slice(grp * 8, grp * 8 + 8)
                                pv = pb[0:64, :].rearrange("p (a b) -> p a b", a=8)
                                c.tt("dve", Nf[:, gs, :], pv, gamL[:, gs, :], ALU.mult, [Rp, RgL, Rl], [RNfg[grp]])
                                c.tt("dve", Nf[:, gs, :], Nf[:, gs, :], beta[:, gs, h:h + 1].to_broadcast([64, 8, 64]), ALU.mult,
                                     [RNfg[grp], Rs], [RNfg[grp]])
                                c.cp("act", Nk[0][:, gs, :], Nf[:, gs, :], [RNfg[grp]], [RNk[0][grp]])
                                pb2, Rp2 = c.bank()
                                for q in range(8):
                                    n = grp * 8 + q
                                    c.mm(pb2[0:64, q * 64:(q + 1) * 64], gKT[:, n * 64:(n + 1) * 64], gQT[:, n * 64:(n + 1) * 64],
                                         True, True, [RgK, RgQ], [Rp2], signal=(q == 7))
                                c.tt("dve", QKg[:, gs, :], pb2[0:64, :].rearrange("p (a b) -> p a b", a=8), gamU[:, gs, :], ALU.mult,
                                     [Rp2, RgU], [Rqk])
                                pb3, Rp3 = c.bank()
                                for q in range(8):
                                    n = grp * 8 + q
                                    c.tr(pb3[0:64, q * 64:(q + 1) * 64], Nf[:, n, :], id64f, [RNfg[grp], Rc], [Rp3], signal=(q == 7))
                                pv3 = pb3[0:64, :].rearrange("p (a b) -> p a b", a=8)
                                c.cp("act", Mk[0][:, gs, :], pv3, [Rp3], [RMk[0][grp]])
                                c.stt("dve", Pt[0][:, gs, :], pv3, -1.0, c64f[:, 3:4, :].to_broadcast([64, 8, 64]), ALU.mult, ALU.add,
                                      [Rp3, Rc, RMk[0][grp]], [RPt[0][grp]])
                            _chk(c, "G2")
                            for lv in range(5):
                                a, b = lv % 2, (lv + 1) % 2
                                last = lv == 4
                                for grp in range(4):
                                    gs = slice(grp * 8, grp * 8 + 8)
                                    pbN, RpN = c.bank()
                                    for q in range(8):
                                        n = grp * 8 + q
                                        c.mm(pbN[0:64, q * 64:(q + 1) * 64], Mk[a][:, n, :], Nk[a][:, n, :], True, True,
                                             [RMk[a][grp], RNk[a][grp]], [RpN], signal=(q == 7))
                                    pvN = pbN[0:64, :].rearrange("p (a b) -> p a b", a=8)
                                    c.stt("dve", INk[:, gs, :], pvN, 1.0, c64f[:, 3:4, :].to_broadcast([64, 8, 64]), ALU.mult, ALU.add,
                                          [RpN, Rc], [RINg[grp]])
                                    if not last:
                                        c.cp("act", Nk[b][:, gs, :], pvN, [RpN, RINg[grp]], [RNk[b][grp]])
                                        pbM, RpM = c.bank()
                                        for q in range(8):
                                            n = grp * 8 + q
                                            c.mm(pbM[0:64, q * 64:(q + 1) * 64], Nk[a][:, n, :], Mk[a][:, n, :], True, True,
                                                 [RMk[a][grp], RNk[a][grp]], [RpM], signal=(q == 7))
                                        c.cp("act", Mk[b][:, gs, :], pbM[0:64, :].rearrange("p (a b) -> p a b", a=8), [RpM], [RMk[b][grp]])
                                    pbP, RpP = c.bank()
                                    for q in range(8):
                                        n = grp * 8 + q
                                        c.mm(pbP[0:64, q * 64:(q + 1) * 64], INk[:, n, :], Pt[a][:, n, :], True, True,
                                             [RINg[grp], RPt[a][grp]], [RpP], signal=(q == 7))
                                    pvP = pbP[0:64, :].rearrange("p (a b) -> p a b", a=8)
                                    if last:
                                        c.cp("dve", P[:, gs, :], pvP, [RpP], [RP])
                                    else:
                                        c.cp("dve", Pt[b][:, gs, :], pvP, [RpP], [RPt[b][grp]])
                            for grp in range(4):
                                pb, Rp = c.bank()
                                for q in range(8):
                                    n = grp * 8 + q
                                    c.mm(pb[:, q * 64:(q + 1) * 64], kbg[:, n, :], P[:, n, :], True, True, [Rkbg, RP], [Rp],
                                         signal=(q == 7))
                                c.ts("dve", nwT[:, grp * 512:(grp + 1) * 512], pb[:], -1.0, None, ALU.mult, None, [Rp], [Rnw])
                            S.barrier()

                        _chk(c, "GPREP")
                        with ExitStack() as res:
                            Sf = c.sb([128, 128], F32, res)
                            Sb = [c.sb([128, 128], BF16, res) for _ in range(2)]
                            vn = [c.sb([64, 128], BF16, res) for _ in range(2)]
                            oT = c.sb([128, SEQ], F32, res, name=f"goT{h}")
                            sqb = c.sb([128, SEQ], BF16, res)
                            rn = c.sb([128, SEQ], F32, res)
                            yst = c.sb([128, SEQ], BF16, res)
                            RSf, RoT, Rsq, Rrn, Ryst = [Reg() for _ in range(5)]
                            RSb = [Reg(), Reg()]
                            Rvn = [Reg(), Reg()]
                            c.S.op("dve", lambda e: e.memset(Sf[:], 0.0), writes=[RSf])
                            c.S.op("pool", lambda e: e.memset(Sb[0][:], 0.0), writes=[RSb[0]])
                            for n in range(32):
                                a, b = n % 2, (n + 1) % 2
                                vps, Rvps = c.pb[n % 2], c.Rpb[n % 2]
                                ops, Rops = c.pb[2 + (n // 8) % 2], c.Rpb[2 + (n // 8) % 2]
                                dps, Rdps = c.pb[4 + n % 2], c.Rpb[4 + n % 2]
                                q = n % 8
                                c.mm(vps[0:64, 0:128], P[:, n, :], vb[:, n, :], True, False, [RP, Rvb], [Rvps], signal=False)
                                c.mm(vps[0:64, 0:128], nwT[:, n * 64:(n + 1) * 64], Sb[a][:], False, True, [Rnw, RSb[a]], [Rvps])
                                c.cp("act", vn[a][:], vps[0:64, 0:128], [Rvps], [Rvn[a]])
                                c.mm(ops[:, q * 64:(q + 1) * 64], Sb[a][:], qdT[:, n * 64:(n + 1) * 64], True, False,
                                     [RSb[a], Rqd], [Rops], signal=False)
                                c.mm(ops[:, q * 64:(q + 1) * 64], vn[a][:], QKg[:, n, :], False, True, [Rvn[a], Rqk], [Rops])
                                c.mm(dps[:, 0:128], kdec[:, n, :], vn[a][:], True, True, [Rkd, Rvn[a]], [Rdps])
                                c.stt("dve", Sf[:], Sf[:], decB[:, n, h:h + 1], dps[:, 0:128], ALU.mult, ALU.add,
                                      [RSf, Rs, Rdps], [RSf])
                                c.cp("act", Sb[b][:], Sf[:], [RSf], [RSb[b]])
                                if q == 7:
                                    c.cp("act", oT[:, (n - 7) * 64:(n + 1) * 64], ops[:], [Rops], [RoT])
                            c.act(sqb[:], oT[:], AF.Square, [RoT], [Rsq])
                            for tb in range(4):
                                pb, Rp = c.bank()
                                c.mm(pb[:], onesb, sqb[:, tb * 512:(tb + 1) * 512], True, True, [Rc, Rsq], [Rp])
                                c.act(rn[:, tb * 512:(tb + 1) * 512], pb[:], AF.Ln, [Rp], [Rrn], scale=1.0 / 128, bias=EPS)
                            c.act(rn[:], rn[:], AF.Exp, [Rrn], [Rrn], scale=-0.5)
                            c.stt("dve", rn[:], oT[:], gn[:, 0:1], rn[:], ALU.mult, ALU.mult, [RoT, Rrn, Rc], [Rrn])
                            c.tt("dve", yst[:], rn[:], gZ[:], ALU.mult, [Rrn, RgZ], [Ryst])
                            c.store("sp", y_d[896 + h * 128:896 + (h + 1) * 128, :], yst[:], [Ryst], [Ry], "go")
                            S.barrier()
        S.barrier()


OFF_P, OFF_SQ, OFF_SK, OFF_SV = 0, 512, 1280, 2048
OFF_GQ, OFF_GK, OFF_GV, OFF_GZ, OFF_A, OFF_B, OFF_G = 2816, 3584, 4352, 5120, 5888, 5894, 5900


def _blk(w):
    n = w.shape[1]
    return np.ascontiguousarray(w.reshape(16, 128, n).transpose(1, 0, 2))


def p1_consts():
    j = np.arange(128)
    cst = np.zeros((128, 5, 128), np.float32)
    cst[:, 0, :] = np.eye(128)
    cst[:, 1, :] = (j[:, None] >= j[None, :])
    cst[:, 2, :] = (j[:, None] < j[None, :])
    cst[:, 3, :] = 1.0
    mb = np.zeros((128, 4, 512), np.float32)
    q = np.arange(512)
    for jj in range(4):
        mb[:, jj, :] = np.where((128 * jj + j[:, None]) < q[None, :], 0.0, NEG)
    i = np.arange(64)
    c64 = np.zeros((64, 5, 64), np.float32)
    c64[:, 0, :] = (i[:, None] <= i[None, :])
    c64[:, 1, :] = (i[:, None] > i[None, :])
    c64[:, 2, :] = (i[None, :] >= i[:, None])
    c64[:, 3, :] = np.eye(64)
    c64[:, 4, :] = 1.0
    return cst, mb, c64


def p1_inputs(inp, l, hh, consts):
    cst, mb, c64 = consts
    W = inp["w_in"][l]
    hg = [hh * 3 + h for h in range(3)]
    cols = [OFF_P + g * 128 for g in range(4)]
    for off in (OFF_SQ, OFF_SK, OFF_GQ, OFF_GK, OFF_GV, OFF_GZ):
        cols += [off + h * 128 for h in hg]
    wF = np.stack([_blk(W[:, c0:c0 + 128]) for c0 in cols])
    wV = _blk(np.concatenate([W[:, OFF_SV + h * 128:OFF_SV + (h + 1) * 128] for h in hg], axis=1))
    wAB = _blk(np.concatenate([W[:, [OFF_A + h for h in hg]], W[:, [OFF_B + h for h in hg]]], axis=1))
    poolw = np.ascontiguousarray(inp["pool_w"][l].transpose(1, 0, 2))
    poolc = np.zeros((128, 4, 18), np.float32)
    t = np.arange(16)
    for g in range(4):
        w = 2 ** (g + 1)
        poolc[:, g, 0] = inp["pool_scale"][l][g * 128:(g + 1) * 128]
        poolc[:, g, 1] = 1.0 / w
        poolc[:, g, 2:18] = (1.0 / np.minimum(t + 1, w))[None, :]
    convw = np.zeros((128, 9, 4), np.float32)
    for kind in range(3):
        for h in range(3):
            ch0 = kind * 768 + hg[h] * 128
            convw[:, kind * 3 + h, :] = inp["gdn_conv"][l][:, ch0:ch0 + 128].T
    gsc = np.zeros((64, 3, 32, 3), np.float32)
    for h in range(3):
        gsc[:, 0, :, h] = inp["gdn_dt_bias"][l][hg[h]]
        gsc[:, 1, :, h] = inp["gdn_a_log"][l][hg[h]]
    gn = np.ascontiguousarray(inp["gdn_norm"][l][:, None])
    gain = np.ascontiguousarray(np.broadcast_to(inp["attn_norm"][l][None, :], (128, D)))
    return dict(gain=gain, wF=wF, wV=wV, wAB=wAB, cst=cst, mb=mb, c64=c64, poolw=poolw, poolc=poolc,
                convw=convw, gsc=gsc, gn=gn)


TT = 1024
NT = TT // 128


def build_p2(final):
    nc = bass.Bass("TRN2", target_bir_lowering=False)
    dt = lambda name, shape, kind="ExternalInput", dty=F32: nc.dram_tensor(name, shape, dty, kind=kind).ap()
    x_d = dt("x", [TT, D])
    yT_d = dt("yT", [D, TT], dty=BF16)
    o_d = dt("o", [TT, D], kind="ExternalOutput")
    wd = dict(gain1=dt("gain1", [128, D]), gain2=dt("gain2", [128, D]), gainF=dt("gainF", [128, D]),
              ident=dt("ident", [128, 128]),
              wG=dt("wG", [48, 128, 16, 128]), wUp=dt("wUp", [16, 128, 16, 128]), wO=dt("wO", [4, 128, 16, 512]),
              wF1=dt("wF1", [64, 128, 16, 128]), wF2=dt("wF2", [8, 4, 128, 8, 512]))
    with ExitStack() as es:
        S = Sched(nc, es)
        c = Ctx(nc, S, es)
        Ro = Reg()
        emit_p2(c, S, x_d, lambda kc: yT_d[kc * 128:(kc + 1) * 128, :], o_d, wd, Ro, final)
        S.barrier()
        S.wait_all("sp", [Ro])
    return nc


def emit_p2(c, S, x_d, yrows, o_d, wd, Ro, final):
    with ExitStack() as es:
        identf = c.sb([128, 128], F32, es)
        Rc = Reg()
        c.load("sp", identf[:], wd["ident"], [Rc], "c0")
        xs = c.sb([128, NT, D], F32, es, name="xres")
        Rx = [Reg() for _ in range(NT)]
        for tt in range(NT):
            c.load("sp", xs[:, tt, :], x_d[tt * 128:(tt + 1) * 128, :], [Rx[tt]], f"xl{tt % 2}")
        S.barrier()
        NWB = 3
        wcount = [0]

        def stream_blocks(src_list, wblk, Rw, body):
            def load_w(k):
                s = wcount[0] % NWB
                wcount[0] += 1
                c.load("pool", wblk[s][:], src_list[k], [Rw[s]], f"w{s}")
                return s
            slots = {}
            for k in range(min(2, len(src_list))):
                slots[k] = load_w(k)
            for k in range(len(src_list)):
                if k + 2 < len(src_list):
                    slots[k + 2] = load_w(k + 2)
                s = slots[k]
                body(k, wblk[s], Rw[s])

        def do_norm(gain_key, uT, RuT, pes):
            gain_sb = c.sb([128, D], F32, pes)
            Rg = Reg()
            c.load("sp", gain_sb[:], wd[gain_key], [Rg], "gl")
            scrs = []
            for _ in range(2):
                sq = c.sb([128, D], BF16, pes)
                ss = c.sb([128, 4], F32, pes)
                u32 = c.sb([128, D], F32, pes)
                scrs.append((sq, Reg(), ss, Reg(), u32, Reg()))
            RuTs = [Reg() for _ in range(NT)]
            for tt in range(NT):
                norm_transpose(c, xs[:, tt, :], Rx[tt], gain_sb, Rg, identf, Rc, uT, RuTs[tt], tt * 128, scrs[tt % 2])
            S.barrier()

        with ExitStack() as aes:
            uT = c.sb([128, 16, TT], BF16, aes, name="uT2")
            yT = c.sb([128, 16, TT], BF16, aes, name="yT2")
            mT = c.sb([128, 16, TT], BF16, aes, name="mT2")
            RuT, RyT, RmT = Reg(), Reg(), Reg()
            for kc in range(16):
                c.load("sp", yT[:, kc, :], yrows(kc), [RyT], f"yl{kc % 2}")
            with ExitStack() as pes:
                do_norm("gain1", uT, RuT, pes)
            S.barrier()
            with ExitStack() as pes:
                wblk = [c.sb([128, 16, 128], BF16, pes) for _ in range(NWB)]
                Rw = [Reg() for _ in range(NWB)]
                sig = [c.sb([128, 512], F32, pes) for _ in range(2)]
                Rsig = [Reg(), Reg()]
                macc = [c.sb([128, 512], F32, pes) for _ in range(2)]
                Rmacc = [Reg(), Reg()]
                tmpm = c.sb([128, 512], F32, pes)
                Rtmp = Reg()
                srcs = []
                for dc in range(16):
                    srcs += [wd["wG"][dc * 3 + br] for br in range(3)] + [wd["wUp"][dc]]
                kr = [(0, 4), (4, 10), (10, 16)]
                cnt = [0]

                def body(k, wb, Rwb):
                    dc, j = k // 4, k % 4
                    if j < 3:
                        body.gw[j] = (wb, Rwb)
                        return
                    for tb in range(2):
                        tsl = slice(tb * 512, (tb + 1) * 512)
                        mi = cnt[0] % 2
                        cnt[0] += 1
                        for br in range(3):
                            gwb, Rgw = body.gw[br]
                            pg, Rpg = c.bank()
                            for kc in range(16):
                                c.mm(pg[:], gwb[:, kc, :], uT[:, kc, tsl], kc == 0, kc == 15, [Rgw, RuT], [Rpg],
                                     signal=(kc == 15))
                            si = (cnt[0] + br) % 2
                            c.act(sig[si][:], pg[:], AF.Sigmoid, [Rpg], [Rsig[si]])
                            pu, Rpu = c.bank()
                            k0, k1 = kr[br]
                            for kc in range(k0, k1):
                                c.mm(pu[:], wb[:, kc, :], yT[:, kc, tsl], kc == k0, kc == k1 - 1, [Rwb, RyT], [Rpu],
                                     signal=(kc == k1 - 1))
                            if br == 0:
                                c.tt("dve", macc[mi][:], pu[:], sig[si][:], ALU.mult, [Rpu, Rsig[si]], [Rmacc[mi]])
                            else:
                                c.tt("dve", tmpm[:], pu[:], sig[si][:], ALU.mult, [Rpu, Rsig[si]], [Rtmp])
                                if br == 1:
                                    c.tt("pool", macc[mi][:], macc[mi][:], tmpm[:], ALU.add, [Rmacc[mi], Rtmp], [Rmacc[mi]])
                                else:
                                    c.tt("pool", mT[:, dc, tsl], macc[mi][:], tmpm[:], ALU.add, [Rmacc[mi], Rtmp], [RmT])
                body.gw = {}
                wblk5 = wblk + [c.sb([128, 16, 128], BF16, pes) for _ in range(3)]
                Rw5 = Rw + [Reg() for _ in range(3)]
                NW5 = 6
                w5 = [0]

                def load5(k):
                    s = w5[0] % NW5
                    w5[0] += 1
                    c.load("pool", wblk5[s][:], srcs[k], [Rw5[s]], f"v{s}")
                    return s
                slots = {}
                for k in range(2):
                    slots[k] = load5(k)
                for k in range(len(srcs)):
                    if k + 2 < len(srcs):
                        slots[k + 2] = load5(k + 2)
                    body(k, wblk5[slots[k]], Rw5[slots[k]])
                S.barrier()
            with ExitStack() as pes:
                wo = [c.sb([128, 16, 512], BF16, pes) for _ in range(2)]
                Rwo = [Reg(), Reg()]
                c.load("pool", wo[0][:], wd["wO"][0], [Rwo[0]], "wo0")
                for ob in range(4):
                    if ob + 1 < 4:
                        c.load("pool", wo[(ob + 1) % 2][:], wd["wO"][ob + 1], [Rwo[(ob + 1) % 2]], f"wo{(ob + 1) % 2}")
                    for tt in range(NT):
                        pb, Rp = c.bank()
                        for kc in range(16):
                            c.mm(pb[:], mT[:, kc, tt * 128:(tt + 1) * 128], wo[ob % 2][:, kc, :], kc == 0, kc == 15,
                                 [RmT, Rwo[ob % 2]], [Rp], signal=(kc == 15))
                        c.tt("dve", xs[:, tt, ob * 512:(ob + 1) * 512], xs[:, tt, ob * 512:(ob + 1) * 512], pb[:], ALU.add,
                             [Rp, Rx[tt]], [Rx[tt]])
                S.barrier()

        with ExitStack() as mes:
            uT = c.sb([128, 16, TT], BF16, mes, name="u2T")
            RuT = Reg()
            with ExitStack() as pes:
                do_norm("gain2", uT, RuT, pes)
            S.barrier()
            hT = c.sb([128, 8, TT], BF16, mes, name="hT")
            RhT = Reg()
            wblk = [c.sb([128, 16, 128], BF16, mes) for _ in range(NWB)]
            Rw = [Reg() for _ in range(NWB)]
            w2 = [c.sb([128, 8, 512], BF16, mes) for _ in range(2)]
            Rw2 = [Reg(), Reg()]
            rl = [c.sb([128, 512], F32, mes) for _ in range(2)]
            Rrl = [Reg(), Reg()]
            w2c = [0]

            def load2(g, ob):
                s = w2c[0] % 2
                w2c[0] += 1
                c.load("pool", w2[s][:], wd["wF2"][g, ob], [Rw2[s]], f"w2{s}")
                return s
            rc = [0]
            for g in range(8):
                srcs = [wd["wF1"][g * 8 + cb] for cb in range(8)]

                def body(k, wb, Rwb):
                    for tb in range(2):
                        tsl = slice(tb * 512, (tb + 1) * 512)
                        pb, Rp = c.bank()
                        for kc in range(16):
                            c.mm(pb[:], wb[:, kc, :], uT[:, kc, tsl], kc == 0, kc == 15, [Rwb, RuT], [Rp], signal=(kc == 15))
                        ri = rc[0] % 2
                        rc[0] += 1
                        c.act(rl[ri][:], pb[:], AF.Relu, [Rp], [Rrl[ri]])
                        c.tt("dve" if ri else "pool", hT[:, k, tsl], rl[ri][:], rl[ri][:], ALU.mult, [Rrl[ri]], [RhT])
                stream_blocks(srcs, wblk, Rw, body)
                nxt = load2(g, 0)
                for ob in range(4):
                    cur = nxt
                    if ob + 1 < 4:
                        nxt = load2(g, ob + 1)
                    for tt in range(NT):
                        pb, Rp = c.bank()
                        for kc in range(8):
                            c.mm(pb[:], hT[:, kc, tt * 128:(tt + 1) * 128], w2[cur][:, kc, :], kc == 0, kc == 7,
                                 [RhT, Rw2[cur]], [Rp], signal=(kc == 7))
                        c.tt("dve", xs[:, tt, ob * 512:(ob + 1) * 512], xs[:, tt, ob * 512:(ob + 1) * 512], pb[:], ALU.add,
                             [Rp, Rx[tt]], [Rx[tt]])
            S.barrier()

        if final:
            with ExitStack() as pes:
                gain_sb = c.sb([128, D], F32, pes)
                Rg = Reg()
                c.load("sp", gain_sb[:], wd["gainF"], [Rg], "gl")
                sq = c.sb([128, D], F32, pes)
                ss = c.sb([128, 4], F32, pes)
                ob_ = [c.sb([128, D], F32, pes) for _ in range(2)]
                Rob = [Reg(), Reg()]
                Rsq, Rss = Reg(), Reg()
                for tt in range(NT):
                    b = tt % 2
                    c.S.op("act", lambda e, tt=tt: e.activation(out=sq[:], in_=xs[:, tt, :], func=AF.Square, accum_out=ss[:, 0:1]),
                           reads=[Rx[tt]], writes=[Rsq, Rss])
                    c.act(ss[:, 1:2], ss[:, 0:1], AF.Ln, [Rss], [Rss], scale=1.0 / D, bias=EPS)
                    c.act(ss[:, 2:3], ss[:, 1:2], AF.Exp, [Rss], [Rss], scale=-0.5)
                    c.stt("dve", ob_[b][:], xs[:, tt, :], ss[:, 2:3], gain_sb[:], ALU.mult, ALU.mult, [Rx[tt], Rss, Rg], [Rob[b]])
                    c.store("sp", o_d[tt * 128:(tt + 1) * 128, :], ob_[b][:], [Rob[b]], [Ro], f"os{b}")
                S.barrier()
        else:
            for tt in range(NT):
                c.store("sp", o_d[tt * 128:(tt + 1) * 128, :], xs[:, tt, :], [Rx[tt]], [Ro], f"os{tt % 2}")
            S.barrier()


def p2_inputs(inp, l):
    W = inp["w_in"][l]
    wG = np.stack([_blk(W[:, OFF_G + br * D + dc * 128:OFF_G + br * D + (dc + 1) * 128]) for dc in range(16) for br in range(3)])
    Wup = np.concatenate([inp["w_pool_up"][l], inp["w_sb_up"][l], inp["w_gdn_up"][l]], axis=0)
    wUp = np.stack([_blk(Wup[:, dc * 128:(dc + 1) * 128]) for dc in range(16)])
    wO = np.stack([_blk(inp["w_out"][l][:, ob * 512:(ob + 1) * 512]) for ob in range(4)])
    wF1 = np.stack([_blk(inp["w_ff1"][l][:, cb * 128:(cb + 1) * 128]) for cb in range(64)])
    W2 = inp["w_ff2"][l]
    wF2 = np.stack([np.stack([np.ascontiguousarray(
        W2[g * 1024:(g + 1) * 1024, ob * 512:(ob + 1) * 512].reshape(8, 128, 512).transpose(1, 0, 2)) for ob in range(4)])
        for g in range(8)])
    bc = lambda v: np.ascontiguousarray(np.broadcast_to(v[None, :], (128, D)))
    return dict(gain1=bc(inp["attn_norm"][l]), gain2=bc(inp["mlp_norm"][l]), gainF=bc(inp["final_norm"]),
                ident=np.eye(128, dtype=np.float32), wG=wG, wUp=wUp, wO=wO, wF1=wF1, wF2=wF2)


def kernel_unfused(**inputs):
    inp = {k: np.asarray(v) for k, v in inputs.items()}
    x = np.ascontiguousarray(inp["x"], dtype=np.float32)
    consts = p1_consts()
    cores = list(range(8))
    depth = inp["w_in"].shape[0]
    for l in range(depth):
        p1h = [p1_inputs(inp, l, hh, consts) for hh in range(2)]
        in_maps = []
        for core in cores:
            m = dict(p1h[core % 2])
            m["x"] = np.ascontiguousarray(x[core // 2])
            in_maps.append(m)
        res = run_bass_kernel_spmd(build_p1(), in_maps, core_ids=cores)
        ys = [np.asarray(res.results[core]["y"]) for core in cores]
        del in_maps, p1h
        base = p2_inputs(inp, l)
        in_maps = []
        for core in cores:
            b, half = core // 2, core % 2
            y0, y1 = ys[2 * b], ys[2 * b + 1]
            tsl = slice(half * TT, (half + 1) * TT)
            yT = np.concatenate([y0[0:512, tsl], y0[512:896, tsl], y1[512:896, tsl], y0[896:1280, tsl], y1[896:1280, tsl]], axis=0)
            m = dict(base)
            m["x"] = np.ascontiguousarray(x[b, tsl, :])
            m["yT"] = np.ascontiguousarray(yT)
            in_maps.append(m)
        res = run_bass_kernel_spmd(build_p2(l == depth - 1), in_maps, core_ids=cores)
        x = np.stack([np.asarray(res.results[core]["o"]) for core in cores]).reshape(NB, SEQ, D)
        del in_maps, base
    return np.ascontiguousarray(x, dtype=np.float32)


P1_KEYS = ("gain", "wF", "wV", "wAB", "poolw", "poolc", "convw", "gsc", "gn")
P1_SHAPES = dict(gain=[128, D], wF=[NFB, 128, 16, 128], wV=[128, 16, 384], wAB=[128, 16, 6], poolw=[128, 4, 128],
                 poolc=[128, 4, 18], convw=[128, 9, 4], gsc=[64, 3, 32, 3], gn=[128, 1])
P2_SHAPES = dict(gain1=[128, D], gain2=[128, D], wG=[48, 128, 16, 128], wUp=[16, 128, 16, 128], wO=[4, 128, 16, 512],
                 wF1=[64, 128, 16, 128], wF2=[8, 4, 128, 8, 512])


def build_fused(depth):
    nc = bass.Bass("TRN2", target_bir_lowering=False)
    dt = lambda name, shape, kind="ExternalInput", dty=F32: nc.dram_tensor(name, shape, dty, kind=kind).ap()
    x_d = dt("x", [SEQ, D])
    o_d = dt("o", [SEQ, D], kind="ExternalOutput")
    shared = dict(cst=dt("cst", [128, 5, 128]), mb=dt("mb", [128, 4, 512]), c64=dt("c64", [64, 5, 64]),
                  ident=dt("ident", [128, 128]), gainF=dt("gainF", [128, D]))
    xs = dt("xs_scratch", [SEQ, D], kind="Internal")
    ys = [dt(f"ys_scratch{hh}", [1280, SEQ], kind="Internal", dty=BF16) for hh in range(2)]
    w1 = {}
    w2 = {}
    for l in range(depth):
        for hh in range(2):
            d1 = {k: dt(f"{k}_{l}_{hh}", P1_SHAPES[k]) for k in P1_KEYS}
            d1.update(cst=shared["cst"], mb=shared["mb"], c64=shared["c64"])
            w1[(l, hh)] = d1
        d2 = {k: dt(f"{k}_{l}", P2_SHAPES[k]) for k in P2_SHAPES}
        d2.update(ident=shared["ident"], gainF=shared["gainF"])
        w2[l] = d2

    def yrows_for(t):
        tsl = slice(t * TT, (t + 1) * TT)

        def yrows(kc):
            if kc < 4:
                return ys[0][kc * 128:(kc + 1) * 128, tsl]
            if kc < 7:
                return ys[0][512 + (kc - 4) * 128:512 + (kc - 3) * 128, tsl]
            if kc < 10:
                return ys[1][512 + (kc - 7) * 128:512 + (kc - 6) * 128, tsl]
            if kc < 13:
                return ys[0][896 + (kc - 10) * 128:896 + (kc - 9) * 128, tsl]
            return ys[1][896 + (kc - 13) * 128:896 + (kc - 12) * 128, tsl]
        return yrows

    with ExitStack() as es:
        S = Sched(nc, es)
        c = Ctx(nc, S, es)
        Ro = Reg()
        for l in range(depth):
            src = x_d if l == 0 else xs
            last = l == depth - 1
            dst = o_d if last else xs
            Ry = Reg()
            emit_p1(c, S, es, src, ys, [w1[(l, 0)], w1[(l, 1)]], Ry)
            S.barrier()
            for t in range(2):
                emit_p2(c, S, src[t * TT:(t + 1) * TT, :], yrows_for(t), dst[t * TT:(t + 1) * TT, :], w2[l], Ro, last)
                S.barrier()
        S.barrier()
        S.wait_all("sp", [Ro])
        print("sem counts", S.count, max(S.dcount.values()))
    return nc


def kernel(**inputs):
    inp = {k: np.asarray(v) for k, v in inputs.items()}
    x = np.ascontiguousarray(inp["x"], dtype=np.float32)
    depth = inp["w_in"].shape[0]
    cst, mb, c64 = p1_consts()
    base = dict(cst=cst, mb=mb, c64=c64, ident=np.eye(128, dtype=np.float32))
    for l in range(depth):
        for hh in range(2):
            m = p1_inputs(inp, l, hh, (cst, mb, c64))
            for k in P1_KEYS:
                base[f"{k}_{l}_{hh}"] = m[k]
        m = p2_inputs(inp, l)
        for k in P2_SHAPES:
            base[f"{k}_{l}"] = m[k]
        base["gainF"] = m["gainF"]
    cores = list(range(NB))
    in_maps = []
    for b in cores:
        m = dict(base)
        m["x"] = np.ascontiguousarray(x[b])
        in_maps.append(m)
    res = run_bass_kernel_spmd(build_fused(depth), in_maps, core_ids=cores)
    out = np.stack([np.asarray(res.results[b]["o"]) for b in cores])
    return np.ascontiguousarray(out, dtype=np.float32)
```

```python
import numpy as np
from contextlib import ExitStack
import concourse.bass as bass
import concourse.mybir as mybir
from concourse.bass_utils import run_bass_kernel_spmd

F32 = mybir.dt.float32
BF16 = mybir.dt.bfloat16
AF = mybir.ActivationFunctionType
ALU = mybir.AluOpType
AX = mybir.AxisListType

D = 2048
SEQ = 2048
NB = 4
DFF = 8192
EPS = 1e-6
NEG = -30000.0
SAME_ENGINE_SYNC = True
ENGMAP = {"pe": "tensor", "act": "scalar", "dve": "vector", "pool": "gpsimd", "sp": "sync"}


class Reg:
    __slots__ = ("w", "r")

    def __init__(self):
        self.w = {}
        self.r = {}


class Sched:
    ENGS = ("pe", "act", "dve", "pool", "sp")

    def __init__(self, nc, es):
        self.nc = nc
        self.es = es
        self.count = {e: 0 for e in self.ENGS}
        self.waited = {e: {} for e in self.ENGS}
        self.sems = {}
        self.dcount = {}
        for e in self.ENGS:
            self.sems[e] = es.enter_context(nc.semaphore("s_" + e))

    def dsem(self, name):
        if name not in self.sems:
            self.sems[name] = self.es.enter_context(self.nc.semaphore("d_" + name))
            self.dcount[name] = 0
        return self.sems[name]

    def _waits(self, eng, reads, writes):
        need = {}
        for r in reads:
            for s, v in r.w.items():
                if v > need.get(s, 0):
                    need[s] = v
        for w in writes:
            for s, v in w.w.items():
                if v > need.get(s, 0):
                    need[s] = v
            for s, v in w.r.items():
                if v > need.get(s, 0):
                    need[s] = v
        out = []
        wd = self.waited[eng]
        for s, v in need.items():
            if s == eng and (eng == "pe" or not SAME_ENGINE_SYNC):
                continue
            if s in self.count:
                assert v <= self.count[s], f"wait on unsignaled op of {s}: {v} > {self.count[s]}"
            if wd.get(s, 0) >= v:
                continue
            wd[s] = v
            out.append((s, v))
        return out

    def _emit1(self, e, waits, fn, sig):
        eng = getattr(self.nc, ENGMAP[e])
        for s, v in waits:
            eng.wait_ge(self.sems[s], v)
        if fn is not None:
            ins = fn(eng)
            if sig is not None:
                ins.then_inc(self.sems[sig[0]], sig[1])

    def op(self, eng, fn, reads=(), writes=(), signal=True):
        waits = self._waits(eng, reads, writes)
        val = self.count[eng] + 1
        if signal:
            self.count[eng] = val
        self._emit1(eng, waits, fn, (eng, 1) if signal else None)
        for w in writes:
            w.w = {eng: val}
            w.r = {}
        for r in reads:
            if r.r.get(eng, 0) < val:
                r.r[eng] = val

    def dma(self, eng, fn, reads=(), writes=(), sem="dma"):
        self.dsem(sem)
        waits = self._waits(eng, reads, writes)
        self.dcount[sem] += 16
        val = self.dcount[sem]
        self._emit1(eng, waits, fn, (sem, 16))
        for w in writes:
            w.w = {sem: val}
            w.r = {}
        for r in reads:
            if r.r.get(sem, 0) < val:
                r.r[sem] = val

    def wait_all(self, eng, regs):
        waits = self._waits(eng, regs, ())
        self._emit1(eng, waits, None, None)

    def barrier(self):
        allv = dict(self.count)
        allv.update(self.dcount)
        for e in self.ENGS:
            waits = []
            for s, v in allv.items():
                if s == e or v == 0:
                    continue
                if self.waited[e].get(s, 0) >= v:
                    continue
                self.waited[e][s] = v
                waits.append((s, v))
            self._emit1(e, waits, None, None)


class Ctx:
    def __init__(self, nc, S, es):
        self.nc, self.S, self.es = nc, S, es
        self.pb = []
        self.Rpb = []
        for i in range(8):
            self.pb.append(es.enter_context(nc.psum_tensor(f"pb{i}", [128, 512], F32)))
            self.Rpb.append(Reg())
        self.pbi = 0
        self.n = 0

    def sb(self, shape, dt, es=None, name=None):
        self.n += 1
        return (es or self.es).enter_context(self.nc.sbuf_tensor((name or "t") + f"_{self.n}", shape, dt))

    def bank(self, lo=0, hi=8):
        i = lo + (self.pbi % (hi - lo))
        self.pbi += 1
        return self.pb[i], self.Rpb[i]

    def mm(self, out, lhsT, rhs, start, stop, reads, writes, signal=True):
        self.S.op("pe", lambda e: e.matmul(out, lhsT=lhsT, rhs=rhs, start=start, stop=stop),
                  reads=reads, writes=writes, signal=signal)

    def tr(self, out, in_, ident, reads, writes, signal=True):
        self.S.op("pe", lambda e: e.transpose(out, in_, ident), reads=reads, writes=writes, signal=signal)

    def act(self, out, in_, func, reads, writes, eng="act", **kw):
        self.S.op("act", lambda e: e.activation(out=out, in_=in_, func=func, **kw), reads=reads, writes=writes)

    def tt(self, eng, out, in0, in1, op, reads, writes):
        self.S.op(eng, lambda e: e.tensor_tensor(out=out, in0=in0, in1=in1, op=op), reads=reads, writes=writes)

    def ts(self, eng, out, in0, s1, s2, op0, op1, reads, writes):
        if op1 is None:
            self.S.op(eng, lambda e: e.tensor_scalar(out=out, in0=in0, scalar1=s1, scalar2=None, op0=op0),
                      reads=reads, writes=writes)
        else:
            self.S.op(eng, lambda e: e.tensor_scalar(out=out, in0=in0, scalar1=s1, scalar2=s2, op0=op0, op1=op1),
                      reads=reads, writes=writes)

    def stt(self, eng, out, in0, scalar, in1, op0, op1, reads, writes):
        self.S.op(eng, lambda e: e.scalar_tensor_tensor(out=out, in0=in0, scalar=scalar, in1=in1, op0=op0, op1=op1),
                  reads=reads, writes=writes)

    def cp(self, eng, out, in_, reads, writes):
        if eng == "act":
            self.S.op("act", lambda e: e.activation(out=out, in_=in_, func=AF.Copy), reads=reads, writes=writes)
        else:
            self.S.op(eng, lambda e: e.tensor_copy(out=out, in_=in_), reads=reads, writes=writes)

    def load(self, q, out, in_, writes, sem):
        self.S.dma(q, lambda e: e.dma_start(out=out, in_=in_), writes=writes, sem=sem)

    def store(self, q, out, in_, reads, writes, sem):
        self.S.dma(q, lambda e: e.dma_start(out=out, in_=in_), reads=reads, writes=writes, sem=sem)


def norm_transpose(c, xt_ap, Rxt, gain_sb, Rgain, identf, Rid, uT, RuT, tok0, scr, banks=(0, 8)):
    sq, Rsq, ss, Rss, u32, Ru32 = scr
    c.S.op("act", lambda e: e.activation(out=sq[:], in_=xt_ap, func=AF.Square, accum_out=ss[:, 0:1]),
           reads=[Rxt], writes=[Rsq, Rss])
    c.act(ss[:, 1:2], ss[:, 0:1], AF.Ln, [Rss], [Rss], scale=1.0 / D, bias=EPS)
    c.act(ss[:, 2:3], ss[:, 1:2], AF.Exp, [Rss], [Rss], scale=-0.5)
    c.stt("dve", u32[:], xt_ap, ss[:, 2:3], gain_sb[:], ALU.mult, ALU.mult, [Rxt, Rss, Rgain], [Ru32])
    for j in range(4):
        pb, Rp = c.bank(*banks)
        for i in range(4):
            kc = 4 * j + i
            c.tr(pb[:, i * 128:(i + 1) * 128], u32[:, kc * 128:(kc + 1) * 128], identf[:],
                 [Ru32, Rid], [Rp], signal=(i == 3))
        eng = "act" if j % 2 == 0 else "dve"
        c.cp(eng, uT[:, 4 * j:4 * j + 4, tok0:tok0 + 128],
             pb[:].rearrange("p (a b) -> p a b", a=4), [Rp], [RuT])


NFB = 22


def build_p1():
    nc = bass.Bass("TRN2", target_bir_lowering=False)
    dt = lambda name, shape, kind="ExternalInput", dty=F32: nc.dram_tensor(name, shape, dty, kind=kind).ap()
    x_d = dt("x", [SEQ, D])
    y_d = dt("y", [1280, SEQ], kind="ExternalOutput", dty=BF16)
    wd = dict(
        gain=dt("gain", [128, D]), wF=dt("wF", [NFB, 128, 16, 128]), wV=dt("wV", [128, 16, 384]),
        wAB=dt("wAB", [128, 16, 6]), cst=dt("cst", [128, 5, 128]), mb=dt("mb", [128, 4, 512]),
        c64=dt("c64", [64, 5, 64]), poolw=dt("poolw", [128, 4, 128]), poolc=dt("poolc", [128, 4, 18]),
        convw=dt("convw", [128, 9, 4]), gsc=dt("gsc", [64, 3, 32, 3]), gn=dt("gn", [128, 1]))
    with ExitStack() as es:
        S = Sched(nc, es)
        c = Ctx(nc, S, es)
        Ry = Reg()
        emit_p1(c, S, es, x_d, [y_d], [wd], Ry)
        S.barrier()
        S.wait_all("sp", [Ry])
    return nc


class _Stop(Exception):
    pass


def emit_p1(c, S, es0, x_d, y_ds, wds, Ry):
    try:
        _emit_p1(c, S, es0, x_d, y_ds, wds, Ry)
    except _Stop:
        S.barrier()


def _chk(c, name):
    import os
    if os.environ.get("P1_STOP") == name:
        c.S.barrier()
        raise _Stop()


def _emit_p1(c, S, es0, x_d, y_ds, wds, Ry):
    wd = wds[0]
    with ExitStack() as es:
        cstf = c.sb([128, 5, 128], F32, es)
        cstb = c.sb([128, 5, 128], BF16, es)
        c64f = c.sb([64, 5, 64], F32, es)
        poolw = c.sb([128, 4, 128], BF16, es)
        poolc = c.sb([128, 4, 18], F32, es)
        convw = c.sb([128, 9, 4], F32, es)
        gsc = c.sb([64, 3, 32, 3], F32, es)
        gn = c.sb([128, 1], F32, es)
        Rc = Reg()
        c.load("sp", cstf[:], wd["cst"], [Rc], "c0")
        c.load("pool", cstb[:], wd["cst"], [Rc], "c1")
        c.load("sp", c64f[:], wd["c64"], [Rc], "c3")
        c.load("pool", poolw[:], wd["poolw"], [Rc], "c5")
        c.load("sp", poolc[:], wd["poolc"], [Rc], "c6")
        S.barrier()
        identf = cstf[:, 0, :]
        identb = cstb[:, 0, :]
        TIb = cstb[:, 1, :]
        TSb = cstb[:, 2, :]
        onesb = cstb[:, 3, :]
        Rid = Rc

        uT = c.sb([128, 16, SEQ], BF16, es, name="uT")
        RuT = Reg()

        with ExitStack() as pes:
            gain_sb = c.sb([128, D], F32, pes)
            c.load("sp", gain_sb[:], wd["gain"], [Rc], "c10")
            S.barrier()
            xt = [c.sb([128, D], F32, pes) for _ in range(2)]
            Rxt = [Reg(), Reg()]
            scrs = []
            for _ in range(2):
                sq = c.sb([128, D], BF16, pes)
                ss = c.sb([128, 4], F32, pes)
                u32 = c.sb([128, D], F32, pes)
                scrs.append((sq, Reg(), ss, Reg(), u32, Reg()))
            RuTs = [Reg() for _ in range(16)]
            for tt in range(16):
                b = tt % 2
                c.load("sp", xt[b][:], x_d[tt * 128:(tt + 1) * 128, :], [Rxt[b]], f"x{b}")
                norm_transpose(c, xt[b][:], Rxt[b], gain_sb, Rc, identf, Rid, uT, RuTs[tt], tt * 128, scrs[b])
            S.barrier()

        _chk(c, "A")
        NWB = 3
        wcount = [0]

        def proj_blocks(blocks, wblk, Rw, post):
            def load_w(k):
                s = wcount[0] % NWB
                wcount[0] += 1
                c.load("pool", wblk[s][:], wF_d[blocks[k]], [Rw[s]], f"w{s}")
                return s
            slots = {}
            for k in range(min(2, len(blocks))):
                slots[k] = load_w(k)
            for k, i in enumerate(blocks):
                if k + 2 < len(blocks):
                    slots[k + 2] = load_w(k + 2)
                s = slots[k]
                for tb in range(4):
                    pb, Rp = c.bank()
                    for kc in range(16):
                        c.mm(pb[:], wblk[s][:, kc, :], uT[:, kc, tb * 512:(tb + 1) * 512], kc == 0, kc == 15,
                             [Rw[s], RuT], [Rp], signal=(kc == 15))
                    post(i, tb, pb, Rp)
                post(i, None, None, None)

        for hi, wd in enumerate(wds):
            y_d = y_ds[hi]
            wF_d, wV_d, wAB_d = wd["wF"], wd["wV"], wd["wAB"]
            Rc2 = Reg()
            c.load("sp", convw[:], wd["convw"], [Rc2], "c7")
            c.load("sp", gsc[:], wd["gsc"], [Rc2], "c8")
            c.load("sp", gn[:], wd["gn"], [Rc2], "c9")
            S.barrier()
            with ExitStack() as ses:
                sQT = c.sb([128, 3, SEQ], BF16, ses, name="sQT")
                sKT = c.sb([128, 3, SEQ], BF16, ses, name="sKT")
                sV = c.sb([128, 16, 384], BF16, ses, name="sV")
                mbb = c.sb([128, 4, 512], BF16, ses)
                Rmb = Reg()
                c.load("pool", mbb[:], wd["mb"], [Rmb], "c2")
                RsQ = [Reg() for _ in range(3)]
                RsK = [Reg() for _ in range(3)]
                RsV = Reg()
                with ExitStack() as pes:
                    wblk = [c.sb([128, 16, 128], BF16, pes) for _ in range(NWB)]
                    Rw = [Reg() for _ in range(NWB)]
                    wV = c.sb([128, 16, 384], BF16, pes)
                    RwV = Reg()
                    c.load("pool", wV[:], wV_d, [RwV], "wv")
                    big = c.sb([128, 16 + SEQ], F32, pes, name="big")
                    Rbig = Reg()
                    t1 = c.sb([128, SEQ], F32, pes, name="t1")
                    t2 = c.sb([128, SEQ], F32, pes, name="t2")
                    Rt1, Rt2 = Reg(), Reg()
                    tb16 = c.sb([128, SEQ], BF16, pes, name="tb16")
                    Rtb = Reg()
                    yst = c.sb([128, SEQ], BF16, pes, name="yst")
                    Ryst = Reg()
                    c.S.op("dve", lambda e: e.memset(big[:, 0:16], 0.0), writes=[Rbig])

                    def post_sb(i, tb, pb, Rp):
                        if tb is not None:
                            tsl = slice(tb * 512, (tb + 1) * 512)
                            if i < 4:
                                c.cp("act" if tb % 2 == 0 else "dve", big[:, 16 + tb * 512:16 + (tb + 1) * 512], pb[:], [Rp], [Rbig])
                            elif i < 7:
                                c.cp("act", sQT[:, i - 4, tsl], pb[:], [Rp], [RsQ[i - 4]])
                            else:
                                c.cp("act", sKT[:, i - 7, tsl], pb[:], [Rp], [RsK[i - 7]])
                            return
                        if i >= 4:
                            return
                        g = i
                        p = big[:, 16:16 + SEQ]
                        bufs = [t1, t2]
                        Rb = [Rt1, Rt2]
                        sh = 1
                        for lv in range(g + 1):
                            o = bufs[lv % 2]
                            if lv == 0:
                                c.tt("dve", o[:], p, big[:, 15:15 + SEQ], ALU.add, [Rbig], [Rb[0]])
                            else:
                                prev = bufs[(lv - 1) % 2]
                                c.tt("dve", o[:, sh:], prev[:, sh:], prev[:, 0:SEQ - sh], ALU.add, [Rb[(lv - 1) % 2]], [Rb[lv % 2]])
                                c.cp("dve", o[:, 0:sh], prev[:, 0:sh], [Rb[(lv - 1) % 2]], [Rb[lv % 2]])
                            sh *= 2
                        sw, Rsw = bufs[g % 2], Rb[g % 2]
                        dd, Rdd = bufs[(g + 1) % 2], Rb[(g + 1) % 2]
                        c.stt("dve", dd[:, 16:], sw[:, 16:], poolc[:, g, 1:2], big[:, 32:16 + SEQ], ALU.mult, ALU.subtract,
                              [Rsw, Rbig, Rc], [Rdd])
                        c.tt("dve", dd[:, 0:16], sw[:, 0:16], poolc[:, g, 2:18], ALU.mult, [Rsw, Rc], [Rdd])
                        c.tt("dve", dd[:, 0:16], dd[:, 0:16], big[:, 16:32], ALU.subtract, [Rdd, Rbig], [Rdd])
                        c.cp("act", tb16[:], dd[:], [Rdd], [Rtb])
                        for tb2 in range(4):
                            pb2, Rp2 = c.bank()
                            c.mm(pb2[:], poolw[:, g, :], tb16[:, tb2 * 512:(tb2 + 1) * 512], True, True, [Rtb, Rc], [Rp2])
                            c.ts("dve", yst[:, tb2 * 512:(tb2 + 1) * 512], pb2[:], poolc[:, g, 0:1], None, ALU.mult, None,
                                 [Rp2, Rc], [Ryst])
                        c.store("sp", y_d[g * 128:(g + 1) * 128, :], yst[:], [Ryst], [Ry], "yo")

                    proj_blocks(list(range(10)) if hi == 0 else list(range(4, 10)), wblk, Rw, post_sb)
                    _chk(c, "SBP")
                    for tt in range(16):
                        pb, Rp = c.bank()
                        for kc in range(16):
                            c.mm(pb[:, 0:384], uT[:, kc, tt * 128:(tt + 1) * 128], wV[:, kc, :], kc == 0, kc == 15,
                                 [RwV, RuT], [Rp], signal=(kc == 15))
                        c.cp("act" if tt % 2 else "dve", sV[:, tt, :], pb[:, 0:384], [Rp], [RsV])
                    S.barrier()

                _chk(c, "SBV")
                with ExitStack() as pes:
                    NE = 6
                    Eb = [c.sb([128, 512], F32, pes) for _ in range(NE)]
                    SPb = [c.sb([128, 512], BF16, pes) for _ in range(NE)]
                    Xb = [c.sb([128, 512], F32, pes) for _ in range(NE)]
                    Ab = [c.sb([128, 512], BF16, pes) for _ in range(NE)]
                    RE = [Reg() for _ in range(NE)]
                    RSP = [Reg() for _ in range(NE)]
                    RX = [Reg() for _ in range(NE)]
                    RA = [Reg() for _ in range(NE)]
                    ost = [c.sb([128, 512], BF16, pes) for _ in range(2)]
                    Rost = [Reg(), Reg()]
                    scale = float(128 ** -0.5)
                    items = []
                    for QB in range(4):
                        for kb in range(4 * QB + 3, -1, -1):
                            for h in range(3):
                                items.append((QB, kb, h))
                    n_it = len(items)
                    oc = [0]

                    def stage0(idx):
                        QB, kb, h = items[idx]
                        s = idx % NE
                        zb, Rz = c.pb[idx % 2], c.Rpb[idx % 2]
                        q0 = QB * 512
                        diag = kb >= 4 * QB
                        c.mm(zb[:], sKT[:, h, kb * 128:(kb + 1) * 128], sQT[:, h, q0:q0 + 512], True, not diag,
                             [RsK[h], RsQ[h]], [Rz], signal=not diag)
                        if diag:
                            c.mm(zb[:], identb, mbb[:, kb - 4 * QB, :], False, True, [Rc, Rmb], [Rz])
                        c.act(Eb[s][:], zb[:], AF.Exp, [Rz], [RE[s]], scale=scale)
                        c.act(SPb[s][:], Eb[s][:], AF.Ln, [RE[s]], [RSP[s]], bias=1.0)

                    def stage1(idx):
                        QB, kb, h = items[idx]
                        s = idx % NE
                        cb, Rcb = c.pb[2 + h], c.Rpb[2 + h]
                        first = kb == 4 * QB + 3
                        c.mm(cb[:], TIb, SPb[s][:], first, True, [Rc, RSP[s]], [Rcb])
                        c.act(Xb[s][:], cb[:], AF.Exp, [Rcb], [RX[s]], scale=-1.0)

                    def stage2(idx):
                        QB, kb, h = items[idx]
                        s = idx % NE
                        cb, Rcb = c.pb[2 + h], c.Rpb[2 + h]
                        c.mm(cb[:], TSb, SPb[s][:], False, True, [Rc, RSP[s]], [Rcb])
                        c.tt("dve" if idx % 2 else "pool", Ab[s][:], Eb[s][:], Xb[s][:], ALU.mult, [RE[s], RX[s]], [RA[s]])

                    def stage3(idx):
                        QB, kb, h = items[idx]
                        s = idx % NE
                        ob, Rob = c.pb[5 + h], c.Rpb[5 + h]
                        first = kb == 4 * QB + 3
                        c.mm(ob[:], sV[:, kb, h * 128:(h + 1) * 128], Ab[s][:], first, kb == 0, [RsV, RA[s]], [Rob])
                        if kb == 0:
                            o = oc[0] % 2
                            oc[0] += 1
                            c.cp("dve", ost[o][:], ob[:], [Rob], [Rost[o]])
                            c.store("sp", y_d[512 + h * 128:512 + (h + 1) * 128, QB * 512:(QB + 1) * 512], ost[o][:],
                                    [Rost[o]], [Ry], f"so{o}")

                    for idx in range(n_it + 3):
                        if idx < n_it:
                            stage0(idx)
                        if 1 <= idx < n_it + 1:
                            stage1(idx - 1)
                        if 2 <= idx < n_it + 2:
                            stage2(idx - 2)
                        if 3 <= idx:
                            stage3(idx - 3)
                    S.barrier()

            _chk(c, "SB")
            with ExitStack() as ges:
                tri64 = c64f[:, 0, :]
                id64f = c64f[:, 3, :]
                onesf128 = cstf[0:64, 3, :]
                abc = c.sb([64, 32, 6], F32, ges, name="abc")
                Rabc = Reg()
                la = c.sb([64, 32, 3], F32, ges)
                beta = c.sb([64, 32, 3], F32, ges)
                gcol = c.sb([64, 32, 3], F32, ges)
                glast = c.sb([64, 32, 3], F32, ges)
                egcol = c.sb([64, 32, 3], F32, ges)
                sc_kbg = c.sb([64, 32, 3], F32, ges)
                sc_kdec = c.sb([64, 32, 3], F32, ges)
                decB = c.sb([128, 32, 3], F32, ges)
                tmp = c.sb([64, 32, 3], F32, ges)
                nA = c.sb([64, 32, 3], F32, ges)
                Rs = Reg()
                with ExitStack() as pes:
                    wAB = c.sb([128, 16, 6], BF16, pes)
                    RwAB = Reg()
                    c.load("pool", wAB[:], wAB_d, [RwAB], "wv")
                    pb, Rp = c.bank()
                    for n in range(32):
                        for kc in range(16):
                            c.mm(pb[0:64, n * 6:(n + 1) * 6], uT[:, kc, n * 64:(n + 1) * 64], wAB[:, kc, :], kc == 0, kc == 15,
                                 [RwAB, RuT], [Rp], signal=(kc == 15 and n == 31))
                    c.cp("dve", abc[:].rearrange("p a b -> p (a b)"), pb[0:64, 0:192], [Rp], [Rabc])
                    a_ap = abc[:, :, 0:3]
                    b_ap = abc[:, :, 3:6]
                    c.tt("dve", tmp[:], a_ap, gsc[:, 0, :, :], ALU.add, [Rabc, Rc], [Rs])
                    c.act(tmp[:], tmp[:], AF.Exp, [Rs], [Rs])
                    c.act(tmp[:], tmp[:], AF.Ln, [Rs], [Rs], bias=1.0)
                    c.act(nA[:], gsc[:, 1, :, :], AF.Exp, [Rc, Rs], [Rs])
                    c.stt("dve", la[:], tmp[:], -1.0, nA[:], ALU.mult, ALU.mult, [Rs], [Rs])
                    c.act(tmp[:], b_ap, AF.Exp, [Rabc, Rs], [Rs], scale=-1.0)
                    c.ts("dve", tmp[:], tmp[:], 1.0, None, ALU.add, None, [Rs], [Rs])
                    c.S.op("dve", lambda e: e.reciprocal(out=beta[:], in_=tmp[:]), reads=[Rs], writes=[Rs])
                    la2 = la[:].rearrange("p a b -> p (a b)")
                    pb, Rp = c.bank()
                    c.mm(pb[0:64, 0:96], tri64, la2, True, True, [Rc, Rs], [Rp])
                    c.cp("dve", gcol[:].rearrange("p a b -> p (a b)"), pb[0:64, 0:96], [Rp], [Rs])
                    pb, Rp = c.bank()
                    c.mm(pb[:, 0:96], onesf128, la2, True, True, [Rc, Rs], [Rp])
                    c.cp("dve", glast[:].rearrange("p a b -> p (a b)"), pb[0:64, 0:96], [Rp], [Rs])
                    c.act(decB[:].rearrange("p a b -> p (a b)"), pb[:, 0:96], AF.Exp, [Rp, Rs], [Rs])
                    c.act(egcol[:], gcol[:], AF.Exp, [Rs], [Rs])
                    c.tt("dve", sc_kbg[:], egcol[:], beta[:], ALU.mult, [Rs], [Rs])
                    c.tt("dve", tmp[:], glast[:], gcol[:], ALU.subtract, [Rs], [Rs])
                    c.act(sc_kdec[:], tmp[:], AF.Exp, [Rs], [Rs])
                    S.barrier()

                _chk(c, "GAB")
                for h in range(3):
                    with ExitStack() as hes:
                        gQT = c.sb([128, SEQ], BF16, hes, name=f"gQT{h}")
                        gKT = c.sb([128, SEQ], BF16, hes, name=f"gKT{h}")
                        gKtok = c.sb([64, 32, 128], BF16, hes, name=f"gKtok{h}")
                        gVtok = c.sb([64, 32, 128], BF16, hes, name=f"gVtok{h}")
                        gZ = c.sb([128, SEQ], BF16, hes, name=f"gZ{h}")
                        P = c.sb([64, 32, 64], BF16, hes, name=f"gP{h}")
                        vb = c.sb([64, 32, 128], BF16, hes, name=f"gvb{h}")
                        kdec = c.sb([64, 32, 128], BF16, hes, name=f"gkdec{h}")
                        nwT = c.sb([128, SEQ], BF16, hes, name=f"gnwT{h}")
                        qdT = c.sb([128, SEQ], BF16, hes, name=f"gqdT{h}")
                        QKg = c.sb([64, 32, 64], BF16, hes, name=f"gQKg{h}")
                        RgQ, RgK, RgKt, RgVt, RgZ, RP, Rvb, Rkd, Rnw, Rqd, Rqk = [Reg() for _ in range(11)]
                        with ExitStack() as pes:
                            wblk = [c.sb([128, 16, 128], BF16, pes) for _ in range(NWB)]
                            Rw = [Reg() for _ in range(NWB)]
                            big = c.sb([128, 16 + SEQ], F32, pes, name=f"gbig{h}")
                            Rbig = Reg()
                            t1 = c.sb([128, SEQ], F32, pes, name=f"gt1{h}")
                            t2 = c.sb([128, SEQ], F32, pes, name=f"gt2{h}")
                            Rt1, Rt2 = Reg(), Reg()
                            tb16 = c.sb([128, SEQ], BF16, pes, name=f"gtb16{h}")
                            Rtb = Reg()
                            c.S.op("dve", lambda e: e.memset(big[:, 0:16], 0.0), writes=[Rbig])

                            def post_g(i, tb, pb, Rp):
                                kind = (i - 10) // 3
                                if tb is not None:
                                    tsl = slice(tb * 512, (tb + 1) * 512)
                                    if kind < 3:
                                        c.cp("act" if tb % 2 == 0 else "dve", big[:, 16 + tb * 512:16 + (tb + 1) * 512], pb[:],
                                             [Rp], [Rbig])
                                    else:
                                        c.act(gZ[:, tsl], pb[:], AF.Silu, [Rp], [RgZ])
                                    return
                                if kind == 3:
                                    return
                                j = kind * 3 + h
                                c.ts("dve", t1[:], big[:, 13:13 + SEQ], convw[:, j, 0:1], None, ALU.mult, None, [Rbig, Rc], [Rt1])
                                for ci in range(1, 4):
                                    c.stt("dve", t1[:], big[:, 13 + ci:13 + ci + SEQ], convw[:, j, ci:ci + 1], t1[:],
                                          ALU.mult, ALU.add, [Rbig, Rc, Rt1], [Rt1])
                                c.act(t2[:], t1[:], AF.Silu, [Rt1], [Rt2])
                                if kind < 2:
                                    c.act(tb16[:], t2[:], AF.Square, [Rt2], [Rtb])
                                    for tb2 in range(4):
                                        pb2, Rp2 = c.bank()
                                        c.mm(pb2[:], onesb, tb16[:, tb2 * 512:(tb2 + 1) * 512], True, True, [Rtb, Rc], [Rp2])
                                        c.act(t1[:, tb2 * 512:(tb2 + 1) * 512], pb2[:], AF.Ln, [Rp2], [Rt1], bias=EPS)
                                    c.act(t1[:], t1[:], AF.Exp, [Rt1], [Rt1], scale=-0.5)
                                    if kind == 0:
                                        c.stt("dve", gQT[:], t2[:], float(128 ** -0.5), t1[:], ALU.mult, ALU.mult, [Rt1, Rt2], [RgQ])
                                    else:
                                        c.tt("dve", t2[:], t2[:], t1[:], ALU.mult, [Rt1, Rt2], [Rt2])
                                        c.cp("act", gKT[:], t2[:], [Rt2], [RgK])
                                if kind >= 1:
                                    dst, Rdst = (gKtok, RgKt) if kind == 1 else (gVtok, RgVt)
                                    for grp in range(8):
                                        pb2, Rp2 = c.bank()
                                        for q in range(4):
                                            n = grp * 4 + q
                                            c.tr(pb2[0:64, q * 128:(q + 1) * 128], t2[:, n * 64:(n + 1) * 64], identf, [Rt2, Rid],
                                                 [Rp2], signal=(q == 3))
                                        c.cp("act" if grp % 2 else "dve", dst[:, grp * 4:grp * 4 + 4, :],
                                             pb2[0:64, :].rearrange("p (a b) -> p a b", a=4), [Rp2], [Rdst])

                            proj_blocks([10 + h, 13 + h, 16 + h, 19 + h], wblk, Rw, post_g)
                            S.barrier()

                        _chk(c, "GP")
                        with ExitStack() as pes:
                            labt = c.sb([64, 32, 64], F32, pes)
                            Dm = c.sb([64, 32, 64], F32, pes)
                            gamL = c.sb([64, 32, 64], F32, pes)
                            egB = c.sb([128, SEQ], F32, pes)
                            kbg = c.sb([64, 32, 128], BF16, pes)
                            Mk = [c.sb([64, 32, 64], BF16, pes) for _ in range(2)]
                            Nk = [c.sb([64, 32, 64], BF16, pes) for _ in range(2)]
                            INk = c.sb([64, 32, 64], BF16, pes)
                            Pt = [c.sb([64, 32, 64], BF16, pes) for _ in range(2)]
                            Rl, RD, RgL, ReB, Rkbg, RIN = [Reg() for _ in range(6)]
                            RMk = [[Reg() for _ in range(4)] for _ in range(2)]
                            RNk = [[Reg() for _ in range(4)] for _ in range(2)]
                            RPt = [[Reg() for _ in range(4)] for _ in range(2)]
                            RINg = [Reg() for _ in range(4)]
                            RNfg = [Reg() for _ in range(4)]
                            c.tt("dve", labt[:], la[:, :, h:h + 1].to_broadcast([64, 32, 64]),
                                 c64f[:, 0:1, :].to_broadcast([64, 32, 64]), ALU.mult, [Rs, Rc], [Rl])
                            _chk(c, "G0a")
                            l2 = labt[:].rearrange("p a b -> p (a b)")
                            for tb in range(4):
                                pb, Rp = c.bank()
                                c.mm(pb[:], onesf128, l2[:, tb * 512:(tb + 1) * 512], True, True, [Rc, Rl], [Rp])
                                c.act(egB[:, tb * 512:(tb + 1) * 512], pb[:], AF.Exp, [Rp], [ReB])
                                c.tt("dve", Dm[:, tb * 8:(tb + 1) * 8, :],
                                     gcol[:, tb * 8:(tb + 1) * 8, h:h + 1].to_broadcast([64, 8, 64]),
                                     pb[0:64, :].rearrange("p (a b) -> p a b", a=8), ALU.subtract, [Rp, Rs, ReB], [RD])
                            _chk(c, "G0b")
                            c.ts("dve", gamL[:], Dm[:], 0.0, None, ALU.min, None, [RD], [RgL])
                            c.ts("dve", Dm[:], Dm[:], -1.0, 0.0, ALU.mult, ALU.min, [RD, RgL], [RD])
                            gamU, RgU = Dm, RD
                            c.act(gamL[:], gamL[:], AF.Exp, [RgL], [RgL])
                            c.act(gamU[:], gamU[:], AF.Exp, [RgU], [RgU])
                            c.tt("dve", gamL[:], gamL[:], c64f[:, 1:2, :].to_broadcast([64, 32, 64]), ALU.mult, [RgL, Rc], [RgL])
                            c.tt("dve", gamU[:], gamU[:], c64f[:, 2:3, :].to_broadcast([64, 32, 64]), ALU.mult, [RgU, Rc], [RgU])
                            _chk(c, "G0c")
                            c.tt("dve", qdT[:], gQT[:], egB[:], ALU.mult, [RgQ, ReB], [Rqd])
                            c.tt("dve", kbg[:], gKtok[:], sc_kbg[:, :, h:h + 1].to_broadcast([64, 32, 128]), ALU.mult,
                                 [RgKt, Rs], [Rkbg])
                            c.tt("dve", kdec[:], gKtok[:], sc_kdec[:, :, h:h + 1].to_broadcast([64, 32, 128]), ALU.mult,
                                 [RgKt, Rs], [Rkd])
                            c.tt("dve", vb[:], gVtok[:], beta[:, :, h:h + 1].to_broadcast([64, 32, 128]), ALU.mult,
                                 [RgVt, Rs], [Rvb])
                            _chk(c, "G1")
                            Nf = labt
                            S.wait_all("dve", [Rl])
                            for grp in range(4):
                                pb, Rp = c.bank()
                                for q in range(8):
                                    n = grp * 8 + q
                                    ks = gKT[:, n * 64:(n + 1) * 64]
                                    c.mm(pb[0:64, q * 64:(q + 1) * 64], ks, ks, True, True, [RgK], [Rp], signal=(q == 7))
                                gs = slice(grp * 8, grp * 8 + 8)
                                pv = pb[0:64, :].rearrange("p (a b) -> p a b", a=8)
                                c.tt("dve", Nf[:, gs, :], pv, gamL[:, gs, :], ALU.mult, [Rp, RgL, Rl], [RNfg[grp]])
                                c.tt("dve", Nf[:, gs, :], Nf[:, gs, :], beta[:, gs, h:h + 1].to_broadcast([64, 8, 64]), ALU.mult,
                                     [RNfg[grp], Rs], [RNfg[grp]])
                                c.cp("act", Nk[0][:, gs, :], Nf[:, gs, :], [RNfg[grp]], [RNk[0][grp]])
                                pb2, Rp2 = c.bank()
                                for q in range(8):
                                    n = grp * 8 + q
                                    c.mm(pb2[0:64, q * 64:(q + 1) * 64], gKT[:, n * 64:(n + 1) * 64], gQT[:, n * 64:(n + 1) * 64],
                                         True, True, [RgK, RgQ], [Rp2], signal=(q == 7))
                                c.tt("dve", QKg[:, gs, :], pb2[0:64, :].rearrange("p (a b) -> p a b", a=8), gamU[:, gs, :], ALU.mult,
                                     [Rp2, RgU], [Rqk])
                                pb3, Rp3 = c.bank()
                                for q in range(8):
                                    n = grp * 8 + q
                                    c.tr(pb3[0:64, q * 64:(q + 1) * 64], Nf[:, n, :], id64f, [RNfg[grp], Rc], [Rp3], signal=(q == 7))
                                pv3 = pb3[0:64, :].rearrange("p (a b) -> p a b", a=8)
                                c.cp("act", Mk[0][:, gs, :], pv3, [Rp3], [RMk[0][grp]])
                                c.stt("dve", Pt[0][:, gs, :], pv3, -1.0, c64f[:, 3:4, :].to_broadcast([64, 8, 64]), ALU.mult, ALU.add,
                                      [Rp3, Rc, RMk[0][grp]], [RPt[0][grp]])
                            _chk(c, "G2")
                            for lv in range(5):
                                a, b = lv % 2, (lv + 1) % 2
                                last = lv == 4
                                for grp in range(4):
                                    gs = slice(grp * 8, grp * 8 + 8)
                                    pbN, RpN = c.bank()
                                    for q in range(8):
                                        n = grp * 8 + q
                                        c.mm(pbN[0:64, q * 64:(q + 1) * 64], Mk[a][:, n, :], Nk[a][:, n, :], True, True,
                                             [RMk[a][grp], RNk[a][grp]], [RpN], signal=(q == 7))
                                    pvN = pbN[0:64, :].rearrange("p (a b) -> p a b", a=8)
                                    c.stt("dve", INk[:, gs, :], pvN, 1.0, c64f[:, 3:4, :].to_broadcast([64, 8, 64]), ALU.mult, ALU.add,
                                          [RpN, Rc], [RINg[grp]])
                                    if not last:
                                        c.cp("act", Nk[b][:, gs, :], pvN, [RpN, RINg[grp]], [RNk[b][grp]])
                                        pbM, RpM = c.bank()
                                        for q in range(8):
                                            n = grp * 8 + q
                                            c.mm(pbM[0:64, q * 64:(q + 1) * 64], Nk[a][:, n, :], Mk[a][:, n, :], True, True,
                                                 [RMk[a][grp], RNk[a][grp]], [RpM], signal=(q == 7))
                                        c.cp("act", Mk[b][:, gs, :], pbM[0:64, :].rearrange("p (a b) -> p a b", a=8), [RpM], [RMk[b][grp]])
                                    pbP, RpP = c.bank()
                                    for q in range(8):
                                        n = grp * 8 + q
                                        c.mm(pbP[0:64, q * 64:(q + 1) * 64], INk[:, n, :], Pt[a][:, n, :], True, True,
                                             [RINg[grp], RPt[a][grp]], [RpP], signal=(q == 7))
                                    pvP = pbP[0:64, :].rearrange("p (a b) -> p a b", a=8)
                                    if last:
                                        c.cp("dve", P[:, gs, :], pvP, [RpP], [RP])
                                    else:
                                        c.cp("dve", Pt[b][:, gs, :], pvP, [RpP], [RPt[b][grp]])
                            for grp in range(4):
                                pb, Rp = c.bank()
                                for q in range(8):
                                    n = grp * 8 + q
                                    c.mm(pb[:, q * 64:(q + 1) * 64], kbg[:, n, :], P[:, n, :], True, True, [Rkbg, RP], [Rp],
                                         signal=(q == 7))
                                c.ts("dve", nwT[:, grp * 512:(grp + 1) * 512], pb[:], -1.0, None, ALU.mult, None, [Rp], [Rnw])
                            S.barrier()

                        _chk(c, "GPREP")
                        with ExitStack() as res:
                            Sf = c.sb([128, 128], F32, res)
                            Sb = [c.sb([128, 128], BF16, res) for _ in range(2)]
                            vn = [c.sb([64, 128], BF16, res) for _ in range(2)]
                            oT = c.sb([128, SEQ], F32, res, name=f"goT{h}")
                            sqb = c.sb([128, SEQ], BF16, res)
                            rn = c.sb([128, SEQ], F32, res)
                            yst = c.sb([128, SEQ], BF16, res)
                            RSf, RoT, Rsq, Rrn, Ryst = [Reg() for _ in range(5)]
                            RSb = [Reg(), Reg()]
                            Rvn = [Reg(), Reg()]
                            c.S.op("dve", lambda e: e.memset(Sf[:], 0.0), writes=[RSf])
                            c.S.op("pool", lambda e: e.memset(Sb[0][:], 0.0), writes=[RSb[0]])
                            for n in range(32):
                                a, b = n % 2, (n + 1) % 2
                                vps, Rvps = c.pb[n % 2], c.Rpb[n % 2]
                                ops, Rops = c.pb[2 + (n // 8) % 2], c.Rpb[2 + (n // 8) % 2]
                                dps, Rdps = c.pb[4 + n % 2], c.Rpb[4 + n % 2]
                                q = n % 8
                                c.mm(vps[0:64, 0:128], P[:, n, :], vb[:, n, :], True, False, [RP, Rvb], [Rvps], signal=False)
                                c.mm(vps[0:64, 0:128], nwT[:, n * 64:(n + 1) * 64], Sb[a][:], False, True, [Rnw, RSb[a]], [Rvps])
                                c.cp("act", vn[a][:], vps[0:64, 0:128], [Rvps], [Rvn[a]])
                                c.mm(ops[:, q * 64:(q + 1) * 64], Sb[a][:], qdT[:, n * 64:(n + 1) * 64], True, False,
                                     [RSb[a], Rqd], [Rops], signal=False)
                                c.mm(ops[:, q * 64:(q + 1) * 64], vn[a][:], QKg[:, n, :], False, True, [Rvn[a], Rqk], [Rops])
                                c.mm(dps[:, 0:128], kdec[:, n, :], vn[a][:], True, True, [Rkd, Rvn[a]], [Rdps])
                                c.stt("dve", Sf[:], Sf[:], decB[:, n, h:h + 1], dps[:, 0:128], ALU.mult, ALU.add,
                                      [RSf, Rs, Rdps], [RSf])
                                c.cp("act", Sb[b][:], Sf[:], [RSf], [RSb[b]])
                                if q == 7:
                                    c.cp("act", oT[:, (n - 7) * 64:(n + 1) * 64], ops[:], [Rops], [RoT])
                            c.act(sqb[:], oT[:], AF.Square, [RoT], [Rsq])
                            for tb in range(4):
                                pb, Rp = c.bank()
                                c.mm(pb[:], onesb, sqb[:, tb * 512:(tb + 1) * 512], True, True, [Rc, Rsq], [Rp])
                                c.act(rn[:, tb * 512:(tb + 1) * 512], pb[:], AF.Ln, [Rp], [Rrn], scale=1.0 / 128, bias=EPS)
                            c.act(rn[:], rn[:], AF.Exp, [Rrn], [Rrn], scale=-0.5)
                            c.stt("dve", rn[:], oT[:], gn[:, 0:1], rn[:], ALU.mult, ALU.mult, [RoT, Rrn, Rc], [Rrn])
                            c.tt("dve", yst[:], rn[:], gZ[:], ALU.mult, [Rrn, RgZ], [Ryst])
                            c.store("sp", y_d[896 + h * 128:896 + (h + 1) * 128, :], yst[:], [Ryst], [Ry], "go")
                            S.barrier()
        S.barrier()


OFF_P, OFF_SQ, OFF_SK, OFF_SV = 0, 512, 1280, 2048
OFF_GQ, OFF_GK, OFF_GV, OFF_GZ, OFF_A, OFF_B, OFF_G = 2816, 3584, 4352, 5120, 5888, 5894, 5900


def _blk(w):
    n = w.shape[1]
    return np.ascontiguousarray(w.reshape(16, 128, n).transpose(1, 0, 2))


def p1_consts():
    j = np.arange(128)
    cst = np.zeros((128, 5, 128), np.float32)
    cst[:, 0, :] = np.eye(128)
    cst[:, 1, :] = (j[:, None] >= j[None, :])
    cst[:, 2, :] = (j[:, None] < j[None, :])
    cst[:, 3, :] = 1.0
    mb = np.zeros((128, 4, 512), np.float32)
    q = np.arange(512)
    for jj in range(4):
        mb[:, jj, :] = np.where((128 * jj + j[:, None]) < q[None, :], 0.0, NEG)
    i = np.arange(64)
    c64 = np.zeros((64, 5, 64), np.float32)
    c64[:, 0, :] = (i[:, None] <= i[None, :])
    c64[:, 1, :] = (i[:, None] > i[None, :])
    c64[:, 2, :] = (i[None, :] >= i[:, None])
    c64[:, 3, :] = np.eye(64)
    c64[:, 4, :] = 1.0
    return cst, mb, c64


def p1_inputs(inp, l, hh, consts):
    cst, mb, c64 = consts
    W = inp["w_in"][l]
    hg = [hh * 3 + h for h in range(3)]
    cols = [OFF_P + g * 128 for g in range(4)]
    for off in (OFF_SQ, OFF_SK, OFF_GQ, OFF_GK, OFF_GV, OFF_GZ):
        cols += [off + h * 128 for h in hg]
    wF = np.stack([_blk(W[:, c0:c0 + 128]) for c0 in cols])
    wV = _blk(np.concatenate([W[:, OFF_SV + h * 128:OFF_SV + (h + 1) * 128] for h in hg], axis=1))
    wAB = _blk(np.concatenate([W[:, [OFF_A + h for h in hg]], W[:, [OFF_B + h for h in hg]]], axis=1))
    poolw = np.ascontiguousarray(inp["pool_w"][l].transpose(1, 0, 2))
    poolc = np.zeros((128, 4, 18), np.float32)
    t = np.arange(16)
    for g in range(4):
        w = 2 ** (g + 1)
        poolc[:, g, 0] = inp["pool_scale"][l][g * 128:(g + 1) * 128]
        poolc[:, g, 1] = 1.0 / w
        poolc[:, g, 2:18] = (1.0 / np.minimum(t + 1, w))[None, :]
    convw = np.zeros((128, 9, 4), np.float32)
    for kind in range(3):
        for h in range(3):
            ch0 = kind * 768 + hg[h] * 128
            convw[:, kind * 3 + h, :] = inp["gdn_conv"][l][:, ch0:ch0 + 128].T
    gsc = np.zeros((64, 3, 32, 3), np.float32)
    for h in range(3):
        gsc[:, 0, :, h] = inp["gdn_dt_bias"][l][hg[h]]
        gsc[:, 1, :, h] = inp["gdn_a_log"][l][hg[h]]
    gn = np.ascontiguousarray(inp["gdn_norm"][l][:, None])
    gain = np.ascontiguousarray(np.broadcast_to(inp["attn_norm"][l][None, :], (128, D)))
    return dict(gain=gain, wF=wF, wV=wV, wAB=wAB, cst=cst, mb=mb, c64=c64, poolw=poolw, poolc=poolc,
                convw=convw, gsc=gsc, gn=gn)


TT = 1024
NT = TT // 128


def build_p2(final):
    nc = bass.Bass("TRN2", target_bir_lowering=False)
    dt = lambda name, shape, kind="ExternalInput", dty=F32: nc.dram_tensor(name, shape, dty, kind=kind).ap()
    x_d = dt("x", [TT, D])
    yT_d = dt("yT", [D, TT], dty=BF16)
    o_d = dt("o", [TT, D], kind="ExternalOutput")
    wd = dict(gain1=dt("gain1", [128, D]), gain2=dt("gain2", [128, D]), gainF=dt("gainF", [128, D]),
              ident=dt("ident", [128, 128]),
              wG=dt("wG", [48, 128, 16, 128]), wUp=dt("wUp", [16, 128, 16, 128]), wO=dt("wO", [4, 128, 16, 512]),
              wF1=dt("wF1", [64, 128, 16, 128]), wF2=dt("wF2", [8, 4, 128, 8, 512]))
    with ExitStack() as es:
        S = Sched(nc, es)
        c = Ctx(nc, S, es)
        Ro = Reg()
        emit_p2(c, S, x_d, lambda kc: yT_d[kc * 128:(kc + 1) * 128, :], o_d, wd, Ro, final)
        S.barrier()
        S.wait_all("sp", [Ro])
    return nc


def emit_p2(c, S, x_d, yrows, o_d, wd, Ro, final):
    with ExitStack() as es:
        identf = c.sb([128, 128], F32, es)
        Rc = Reg()
        c.load("sp", identf[:], wd["ident"], [Rc], "c0")
        xs = c.sb([128, NT, D], F32, es, name="xres")
        Rx = [Reg() for _ in range(NT)]
        for tt in range(NT):
            c.load("sp", xs[:, tt, :], x_d[tt * 128:(tt + 1) * 128, :], [Rx[tt]], f"xl{tt % 2}")
        S.barrier()
        NWB = 3
        wcount = [0]

        def stream_blocks(src_list, wblk, Rw, body):
            nb = len(wblk)
            ahead = nb - 1

            def load_w(k):
                s = wcount[0] % nb
                wcount[0] += 1
                c.load("pool", wblk[s][:], src_list[k], [Rw[s]], f"w{s}")
                return s
            slots = {}
            for k in range(min(ahead, len(src_list))):
                slots[k] = load_w(k)
            for k in range(len(src_list)):
                if k + ahead < len(src_list):
                    slots[k + ahead] = load_w(k + ahead)
                s = slots[k]
                body(k, wblk[s], Rw[s])

        def do_norm(gain_key, uT, RuT, pes):
            gain_sb = c.sb([128, D], F32, pes)
            Rg = Reg()
            c.load("sp", gain_sb[:], wd[gain_key], [Rg], "gl")
            scrs = []
            for _ in range(2):
                sq = c.sb([128, D], BF16, pes)
                ss = c.sb([128, 4], F32, pes)
                u32 = c.sb([128, D], F32, pes)
                scrs.append((sq, Reg(), ss, Reg(), u32, Reg()))
            RuTs = [Reg() for _ in range(NT)]
            for tt in range(NT):
                norm_transpose(c, xs[:, tt, :], Rx[tt], gain_sb, Rg, identf, Rc, uT, RuTs[tt], tt * 128, scrs[tt % 2])
            S.barrier()

        with ExitStack() as aes:
            uT = c.sb([128, 16, TT], BF16, aes, name="uT2")
            yT = c.sb([128, 16, TT], BF16, aes, name="yT2")
            mT = c.sb([128, 16, TT], BF16, aes, name="mT2")
            RuT, RyT, RmT = Reg(), Reg(), Reg()
            for kc in range(16):
                c.load("sp", yT[:, kc, :], yrows(kc), [RyT], f"yl{kc % 2}")
            with ExitStack() as pes:
                do_norm("gain1", uT, RuT, pes)
            S.barrier()
            with ExitStack() as pes:
                wblk = [c.sb([128, 16, 128], BF16, pes) for _ in range(NWB)]
                Rw = [Reg() for _ in range(NWB)]
                sig = [c.sb([128, 512], F32, pes) for _ in range(2)]
                Rsig = [Reg(), Reg()]
                macc = [c.sb([128, 512], F32, pes) for _ in range(2)]
                Rmacc = [Reg(), Reg()]
                tmpm = c.sb([128, 512], F32, pes)
                Rtmp = Reg()
                srcs = []
                for dc in range(16):
                    srcs += [wd["wG"][dc * 3 + br] for br in range(3)] + [wd["wUp"][dc]]
                kr = [(0, 4), (4, 10), (10, 16)]
                cnt = [0]

                def body(k, wb, Rwb):
                    dc, j = k // 4, k % 4
                    if j < 3:
                        body.gw[j] = (wb, Rwb)
                        return
                    for tb in range(2):
                        tsl = slice(tb * 512, (tb + 1) * 512)
                        mi = cnt[0] % 2
                        cnt[0] += 1
                        for br in range(3):
                            gwb, Rgw = body.gw[br]
                            pg, Rpg = c.bank()
                            for kc in range(16):
                                c.mm(pg[:], gwb[:, kc, :], uT[:, kc, tsl], kc == 0, kc == 15, [Rgw, RuT], [Rpg],
                                     signal=(kc == 15))
                            si = (cnt[0] + br) % 2
                            c.act(sig[si][:], pg[:], AF.Sigmoid, [Rpg], [Rsig[si]])
                            pu, Rpu = c.bank()
                            k0, k1 = kr[br]
                            for kc in range(k0, k1):
                                c.mm(pu[:], wb[:, kc, :], yT[:, kc, tsl], kc == k0, kc == k1 - 1, [Rwb, RyT], [Rpu],
                                     signal=(kc == k1 - 1))
                            if br == 0:
                                c.tt("dve", macc[mi][:], pu[:], sig[si][:], ALU.mult, [Rpu, Rsig[si]], [Rmacc[mi]])
                            else:
                                c.tt("dve", tmpm[:], pu[:], sig[si][:], ALU.mult, [Rpu, Rsig[si]], [Rtmp])
                                if br == 1:
                                    c.tt("pool", macc[mi][:], macc[mi][:], tmpm[:], ALU.add, [Rmacc[mi], Rtmp], [Rmacc[mi]])
                                else:
                                    c.tt("pool", mT[:, dc, tsl], macc[mi][:], tmpm[:], ALU.add, [Rmacc[mi], Rtmp], [RmT])
                body.gw = {}
                wblk5 = wblk + [c.sb([128, 16, 128], BF16, pes) for _ in range(5)]
                Rw5 = Rw + [Reg() for _ in range(5)]
                NW5 = 8
                w5 = [0]

                def load5(k):
                    s = w5[0] % NW5
                    w5[0] += 1
                    c.load("pool", wblk5[s][:], srcs[k], [Rw5[s]], f"v{s}")
                    return s
                slots = {}
                for k in range(4):
                    slots[k] = load5(k)
                for k in range(len(srcs)):
                    if k + 4 < len(srcs):
                        slots[k + 4] = load5(k + 4)
                    body(k, wblk5[slots[k]], Rw5[slots[k]])
                S.barrier()
            with ExitStack() as pes:
                wo = [c.sb([128, 16, 512], BF16, pes) for _ in range(2)]
                Rwo = [Reg(), Reg()]
                c.load("pool", wo[0][:], wd["wO"][0], [Rwo[0]], "wo0")
                for ob in range(4):
                    if ob + 1 < 4:
                        c.load("pool", wo[(ob + 1) % 2][:], wd["wO"][ob + 1], [Rwo[(ob + 1) % 2]], f"wo{(ob + 1) % 2}")
                    for tt in range(NT):
                        pb, Rp = c.bank()
                        for kc in range(16):
                            c.mm(pb[:], mT[:, kc, tt * 128:(tt + 1) * 128], wo[ob % 2][:, kc, :], kc == 0, kc == 15,
                                 [RmT, Rwo[ob % 2]], [Rp], signal=(kc == 15))
                        c.tt("dve", xs[:, tt, ob * 512:(ob + 1) * 512], xs[:, tt, ob * 512:(ob + 1) * 512], pb[:], ALU.add,
                             [Rp, Rx[tt]], [Rx[tt]])
                S.barrier()

        with ExitStack() as mes:
            uT = c.sb([128, 16, TT], BF16, mes, name="u2T")
            RuT = Reg()
            with ExitStack() as pes:
                do_norm("gain2", uT, RuT, pes)
            S.barrier()
            hT = c.sb([128, 8, TT], BF16, mes, name="hT")
            RhT = Reg()
            wblk = [c.sb([128, 16, 128], BF16, mes) for _ in range(6)]
            Rw = [Reg() for _ in range(6)]
            w2 = [c.sb([128, 8, 512], BF16, mes) for _ in range(3)]
            Rw2 = [Reg(), Reg(), Reg()]
            rl = [c.sb([128, 512], F32, mes) for _ in range(2)]
            Rrl = [Reg(), Reg()]
            w2c = [0]

            def load2(g, ob):
                s = w2c[0] % 3
                w2c[0] += 1
                c.load("pool", w2[s][:], wd["wF2"][g, ob], [Rw2[s]], f"w2{s}")
                return s
            rc = [0]
            for g in range(8):
                srcs = [wd["wF1"][g * 8 + cb] for cb in range(8)]

                def body(k, wb, Rwb):
                    for tb in range(2):
                        tsl = slice(tb * 512, (tb + 1) * 512)
                        pb, Rp = c.bank()
                        for kc in range(16):
                            c.mm(pb[:], wb[:, kc, :], uT[:, kc, tsl], kc == 0, kc == 15, [Rwb, RuT], [Rp], signal=(kc == 15))
                        ri = rc[0] % 2
                        rc[0] += 1
                        c.act(rl[ri][:], pb[:], AF.Relu, [Rp], [Rrl[ri]])
                        c.tt("dve" if ri else "pool", hT[:, k, tsl], rl[ri][:], rl[ri][:], ALU.mult, [Rrl[ri]], [RhT])
                stream_blocks(srcs, wblk, Rw, body)
                q2 = [load2(g, 0), load2(g, 1)]
                for ob in range(4):
                    cur = q2.pop(0)
                    if ob + 2 < 4:
                        q2.append(load2(g, ob + 2))
                    for tt in range(NT):
                        pb, Rp = c.bank()
                        for kc in range(8):
                            c.mm(pb[:], hT[:, kc, tt * 128:(tt + 1) * 128], w2[cur][:, kc, :], kc == 0, kc == 7,
                                 [RhT, Rw2[cur]], [Rp], signal=(kc == 7))
                        c.tt("dve", xs[:, tt, ob * 512:(ob + 1) * 512], xs[:, tt, ob * 512:(ob + 1) * 512], pb[:], ALU.add,
                             [Rp, Rx[tt]], [Rx[tt]])
            S.barrier()

        if final:
            with ExitStack() as pes:
                gain_sb = c.sb([128, D], F32, pes)
                Rg = Reg()
                c.load("sp", gain_sb[:], wd["gainF"], [Rg], "gl")
                sq = c.sb([128, D], F32, pes)
                ss = c.sb([128, 4], F32, pes)
                ob_ = [c.sb([128, D], F32, pes) for _ in range(2)]
                Rob = [Reg(), Reg()]
                Rsq, Rss = Reg(), Reg()
                for tt in range(NT):
                    b = tt % 2
                    c.S.op("act", lambda e, tt=tt: e.activation(out=sq[:], in_=xs[:, tt, :], func=AF.Square, accum_out=ss[:, 0:1]),
                           reads=[Rx[tt]], writes=[Rsq, Rss])
                    c.act(ss[:, 1:2], ss[:, 0:1], AF.Ln, [Rss], [Rss], scale=1.0 / D, bias=EPS)
                    c.act(ss[:, 2:3], ss[:, 1:2], AF.Exp, [Rss], [Rss], scale=-0.5)
                    c.stt("dve", ob_[b][:], xs[:, tt, :], ss[:, 2:3], gain_sb[:], ALU.mult, ALU.mult, [Rx[tt], Rss, Rg], [Rob[b]])
                    c.store("sp", o_d[tt * 128:(tt + 1) * 128, :], ob_[b][:], [Rob[b]], [Ro], f"os{b}")
                S.barrier()
        else:
            for tt in range(NT):
                c.store("sp", o_d[tt * 128:(tt + 1) * 128, :], xs[:, tt, :], [Rx[tt]], [Ro], f"os{tt % 2}")
            S.barrier()


def p2_inputs(inp, l):
    W = inp["w_in"][l]
    wG = np.stack([_blk(W[:, OFF_G + br * D + dc * 128:OFF_G + br * D + (dc + 1) * 128]) for dc in range(16) for br in range(3)])
    Wup = np.concatenate([inp["w_pool_up"][l], inp["w_sb_up"][l], inp["w_gdn_up"][l]], axis=0)
    wUp = np.stack([_blk(Wup[:, dc * 128:(dc + 1) * 128]) for dc in range(16)])
    wO = np.stack([_blk(inp["w_out"][l][:, ob * 512:(ob + 1) * 512]) for ob in range(4)])
    wF1 = np.stack([_blk(inp["w_ff1"][l][:, cb * 128:(cb + 1) * 128]) for cb in range(64)])
    W2 = inp["w_ff2"][l]
    wF2 = np.stack([np.stack([np.ascontiguousarray(
        W2[g * 1024:(g + 1) * 1024, ob * 512:(ob + 1) * 512].reshape(8, 128, 512).transpose(1, 0, 2)) for ob in range(4)])
        for g in range(8)])
    bc = lambda v: np.ascontiguousarray(np.broadcast_to(v[None, :], (128, D)))
    return dict(gain1=bc(inp["attn_norm"][l]), gain2=bc(inp["mlp_norm"][l]), gainF=bc(inp["final_norm"]),
                ident=np.eye(128, dtype=np.float32), wG=wG, wUp=wUp, wO=wO, wF1=wF1, wF2=wF2)


def kernel_unfused(**inputs):
    inp = {k: np.asarray(v) for k, v in inputs.items()}
    x = np.ascontiguousarray(inp["x"], dtype=np.float32)
    consts = p1_consts()
    cores = list(range(8))
    depth = inp["w_in"].shape[0]
    for l in range(depth):
        p1h = [p1_inputs(inp, l, hh, consts) for hh in range(2)]
        in_maps = []
        for core in cores:
            m = dict(p1h[core % 2])
            m["x"] = np.ascontiguousarray(x[core // 2])
            in_maps.append(m)
        res = run_bass_kernel_spmd(build_p1(), in_maps, core_ids=cores)
        ys = [np.asarray(res.results[core]["y"]) for core in cores]
        del in_maps, p1h
        base = p2_inputs(inp, l)
        in_maps = []
        for core in cores:
            b, half = core // 2, core % 2
            y0, y1 = ys[2 * b], ys[2 * b + 1]
            tsl = slice(half * TT, (half + 1) * TT)
            yT = np.concatenate([y0[0:512, tsl], y0[512:896, tsl], y1[512:896, tsl], y0[896:1280, tsl], y1[896:1280, tsl]], axis=0)
            m = dict(base)
            m["x"] = np.ascontiguousarray(x[b, tsl, :])
            m["yT"] = np.ascontiguousarray(yT)
            in_maps.append(m)
        res = run_bass_kernel_spmd(build_p2(l == depth - 1), in_maps, core_ids=cores)
        x = np.stack([np.asarray(res.results[core]["o"]) for core in cores]).reshape(NB, SEQ, D)
        del in_maps, base
    return np.ascontiguousarray(x, dtype=np.float32)


P1_KEYS = ("gain", "wF", "wV", "wAB", "poolw", "poolc", "convw", "gsc", "gn")
P1_SHAPES = dict(gain=[128, D], wF=[NFB, 128, 16, 128], wV=[128, 16, 384], wAB=[128, 16, 6], poolw=[128, 4, 128],
                 poolc=[128, 4, 18], convw=[128, 9, 4], gsc=[64, 3, 32, 3], gn=[128, 1])
P2_SHAPES = dict(gain1=[128, D], gain2=[128, D], wG=[48, 128, 16, 128], wUp=[16, 128, 16, 128], wO=[4, 128, 16, 512],
                 wF1=[64, 128, 16, 128], wF2=[8, 4, 128, 8, 512])


def build_fused(depth):
    nc = bass.Bass("TRN2", target_bir_lowering=False)
    dt = lambda name, shape, kind="ExternalInput", dty=F32: nc.dram_tensor(name, shape, dty, kind=kind).ap()
    x_d = dt("x", [SEQ, D])
    o_d = dt("o", [SEQ, D], kind="ExternalOutput")
    shared = dict(cst=dt("cst", [128, 5, 128]), mb=dt("mb", [128, 4, 512]), c64=dt("c64", [64, 5, 64]),
                  ident=dt("ident", [128, 128]), gainF=dt("gainF", [128, D]))
    xs = dt("xs_scratch", [SEQ, D], kind="Internal")
    ys = [dt(f"ys_scratch{hh}", [1280, SEQ], kind="Internal", dty=BF16) for hh in range(2)]
    w1 = {}
    w2 = {}
    for l in range(depth):
        for hh in range(2):
            d1 = {k: dt(f"{k}_{l}_{hh}", P1_SHAPES[k]) for k in P1_KEYS}
            d1.update(cst=shared["cst"], mb=shared["mb"], c64=shared["c64"])
            w1[(l, hh)] = d1
        d2 = {k: dt(f"{k}_{l}", P2_SHAPES[k]) for k in P2_SHAPES}
        d2.update(ident=shared["ident"], gainF=shared["gainF"])
        w2[l] = d2

    def yrows_for(t):
        tsl = slice(t * TT, (t + 1) * TT)

        def yrows(kc):
            if kc < 4:
                return ys[0][kc * 128:(kc + 1) * 128, tsl]
            if kc < 7:
                return ys[0][512 + (kc - 4) * 128:512 + (kc - 3) * 128, tsl]
            if kc < 10:
                return ys[1][512 + (kc - 7) * 128:512 + (kc - 6) * 128, tsl]
            if kc < 13:
                return ys[0][896 + (kc - 10) * 128:896 + (kc - 9) * 128, tsl]
            return ys[1][896 + (kc - 13) * 128:896 + (kc - 12) * 128, tsl]
        return yrows

    with ExitStack() as es:
        S = Sched(nc, es)
        c = Ctx(nc, S, es)
        Ro = Reg()
        for l in range(depth):
            src = x_d if l == 0 else xs
            last = l == depth - 1
            dst = o_d if last else xs
            Ry = Reg()
            emit_p1(c, S, es, src, ys, [w1[(l, 0)], w1[(l, 1)]], Ry)
            S.barrier()
            for t in range(2):
                emit_p2(c, S, src[t * TT:(t + 1) * TT, :], yrows_for(t), dst[t * TT:(t + 1) * TT, :], w2[l], Ro, last)
                S.barrier()
        S.barrier()
        S.wait_all("sp", [Ro])
        print("sem counts", S.count, max(S.dcount.values()))
    return nc


def kernel(**inputs):
    inp = {k: np.asarray(v) for k, v in inputs.items()}
    x = np.ascontiguousarray(inp["x"], dtype=np.float32)
    depth = inp["w_in"].shape[0]
    cst, mb, c64 = p1_consts()
    base = dict(cst=cst, mb=mb, c64=c64, ident=np.eye(128, dtype=np.float32))
    for l in range(depth):
        for hh in range(2):
            m = p1_inputs(inp, l, hh, (cst, mb, c64))
            for k in P1_KEYS:
                base[f"{k}_{l}_{hh}"] = m[k]
        m = p2_inputs(inp, l)
        for k in P2_SHAPES:
            base[f"{k}_{l}"] = m[k]
        base["gainF"] = m["gainF"]
    cores = list(range(NB))
    in_maps = []
    for b in cores:
        m = dict(base)
        m["x"] = np.ascontiguousarray(x[b])
        in_maps.append(m)
    res = run_bass_kernel_spmd(build_fused(depth), in_maps, core_ids=cores)
    out = np.stack([np.asarray(res.results[b]["o"]) for b in cores])
    return np.ascontiguousarray(out, dtype=np.float32)
```

```python
import numpy as np
from contextlib import ExitStack
import concourse.bass as bass
import concourse.mybir as mybir
from concourse.bass_utils import run_bass_kernel_spmd

F32 = mybir.dt.float32
BF16 = mybir.dt.bfloat16
AF = mybir.ActivationFunctionType
ALU = mybir.AluOpType
AX = mybir.AxisListType

D = 2048
SEQ = 2048
NB = 4
DFF = 8192
EPS = 1e-6
NEG = -30000.0
SAME_ENGINE_SYNC = True
ENGMAP = {"pe": "tensor", "act": "scalar", "dve": "vector", "pool": "gpsimd", "sp": "sync"}


class Reg:
    __slots__ = ("w", "r")

    def __init__(self):
        self.w = {}
        self.r = {}


class Sched:
    ENGS = ("pe", "act", "dve", "pool", "sp")

    def __init__(self, nc, es):
        self.nc = nc
        self.es = es
        self.count = {e: 0 for e in self.ENGS}
        self.waited = {e: {} for e in self.ENGS}
        self.sems = {}
        self.dcount = {}
        for e in self.ENGS:
            self.sems[e] = es.enter_context(nc.semaphore("s_" + e))

    def dsem(self, name):
        if name not in self.sems:
            self.sems[name] = self.es.enter_context(self.nc.semaphore("d_" + name))
            self.dcount[name] = 0
        return self.sems[name]

    def _waits(self, eng, reads, writes):
        need = {}
        for r in reads:
            for s, v in r.w.items():
                if v > need.get(s, 0):
                    need[s] = v
        for w in writes:
            for s, v in w.w.items():
                if v > need.get(s, 0):
                    need[s] = v
            for s, v in w.r.items():
                if v > need.get(s, 0):
                    need[s] = v
        out = []
        wd = self.waited[eng]
        for s, v in need.items():
            if s == eng and (eng == "pe" or not SAME_ENGINE_SYNC):
                continue
            if s in self.count:
                assert v <= self.count[s], f"wait on unsignaled op of {s}: {v} > {self.count[s]}"
            if wd.get(s, 0) >= v:
                continue
            wd[s] = v
            out.append((s, v))
        return out

    def _emit1(self, e, waits, fn, sig):
        eng = getattr(self.nc, ENGMAP[e])
        for s, v in waits:
            eng.wait_ge(self.sems[s], v)
        if fn is not None:
            ins = fn(eng)
            if sig is not None:
                ins.then_inc(self.sems[sig[0]], sig[1])

    def op(self, eng, fn, reads=(), writes=(), signal=True):
        waits = self._waits(eng, reads, writes)
        val = self.count[eng] + 1
        if signal:
            self.count[eng] = val
        self._emit1(eng, waits, fn, (eng, 1) if signal else None)
        for w in writes:
            w.w = {eng: val}
            w.r = {}
        for r in reads:
            if r.r.get(eng, 0) < val:
                r.r[eng] = val

    def dma(self, eng, fn, reads=(), writes=(), sem="dma"):
        self.dsem(sem)
        waits = self._waits(eng, reads, writes)
        self.dcount[sem] += 16
        val = self.dcount[sem]
        self._emit1(eng, waits, fn, (sem, 16))
        for w in writes:
            w.w = {sem: val}
            w.r = {}
        for r in reads:
            if r.r.get(sem, 0) < val:
                r.r[sem] = val

    def wait_all(self, eng, regs):
        waits = self._waits(eng, regs, ())
        self._emit1(eng, waits, None, None)

    def barrier(self):
        allv = dict(self.count)
        allv.update(self.dcount)
        for e in self.ENGS:
            waits = []
            for s, v in allv.items():
                if s == e or v == 0:
                    continue
                if self.waited[e].get(s, 0) >= v:
                    continue
                self.waited[e][s] = v
                waits.append((s, v))
            self._emit1(e, waits, None, None)


class Ctx:
    def __init__(self, nc, S, es):
        self.nc, self.S, self.es = nc, S, es
        self.pb = []
        self.Rpb = []
        for i in range(8):
            self.pb.append(es.enter_context(nc.psum_tensor(f"pb{i}", [128, 512], F32)))
            self.Rpb.append(Reg())
        self.pbi = 0
        self.n = 0

    def sb(self, shape, dt, es=None, name=None):
        self.n += 1
        return (es or self.es).enter_context(self.nc.sbuf_tensor((name or "t") + f"_{self.n}", shape, dt))

    def bank(self, lo=0, hi=8):
        i = lo + (self.pbi % (hi - lo))
        self.pbi += 1
        return self.pb[i], self.Rpb[i]

    def mm(self, out, lhsT, rhs, start, stop, reads, writes, signal=True):
        self.S.op("pe", lambda e: e.matmul(out, lhsT=lhsT, rhs=rhs, start=start, stop=stop),
                  reads=reads, writes=writes, signal=signal)

    def tr(self, out, in_, ident, reads, writes, signal=True):
        self.S.op("pe", lambda e: e.transpose(out, in_, ident), reads=reads, writes=writes, signal=signal)

    def act(self, out, in_, func, reads, writes, eng="act", **kw):
        self.S.op("act", lambda e: e.activation(out=out, in_=in_, func=func, **kw), reads=reads, writes=writes)

    def tt(self, eng, out, in0, in1, op, reads, writes):
        self.S.op(eng, lambda e: e.tensor_tensor(out=out, in0=in0, in1=in1, op=op), reads=reads, writes=writes)

    def ts(self, eng, out, in0, s1, s2, op0, op1, reads, writes):
        if op1 is None:
            self.S.op(eng, lambda e: e.tensor_scalar(out=out, in0=in0, scalar1=s1, scalar2=None, op0=op0),
                      reads=reads, writes=writes)
        else:
            self.S.op(eng, lambda e: e.tensor_scalar(out=out, in0=in0, scalar1=s1, scalar2=s2, op0=op0, op1=op1),
                      reads=reads, writes=writes)

    def stt(self, eng, out, in0, scalar, in1, op0, op1, reads, writes):
        self.S.op(eng, lambda e: e.scalar_tensor_tensor(out=out, in0=in0, scalar=scalar, in1=in1, op0=op0, op1=op1),
                  reads=reads, writes=writes)

    def cp(self, eng, out, in_, reads, writes):
        if eng == "act":
            self.S.op("act", lambda e: e.activation(out=out, in_=in_, func=AF.Copy), reads=reads, writes=writes)
        else:
            self.S.op(eng, lambda e: e.tensor_copy(out=out, in_=in_), reads=reads, writes=writes)

    def load(self, q, out, in_, writes, sem):
        self.S.dma(q, lambda e: e.dma_start(out=out, in_=in_), writes=writes, sem=sem)

    def store(self, q, out, in_, reads, writes, sem):
        self.S.dma(q, lambda e: e.dma_start(out=out, in_=in_), reads=reads, writes=writes, sem=sem)


def norm_transpose(c, xt_ap, Rxt, gain_sb, Rgain, identf, Rid, uT, RuT, tok0, scr, banks=(0, 8)):
    sq, Rsq, ss, Rss, u32, Ru32 = scr
    c.S.op("act", lambda e: e.activation(out=sq[:], in_=xt_ap, func=AF.Square, accum_out=ss[:, 0:1]),
           reads=[Rxt], writes=[Rsq, Rss])
    c.act(ss[:, 1:2], ss[:, 0:1], AF.Ln, [Rss], [Rss], scale=1.0 / D, bias=EPS)
    c.act(ss[:, 2:3], ss[:, 1:2], AF.Exp, [Rss], [Rss], scale=-0.5)
    c.stt("dve", u32[:], xt_ap, ss[:, 2:3], gain_sb[:], ALU.mult, ALU.mult, [Rxt, Rss, Rgain], [Ru32])
    for j in range(4):
        pb, Rp = c.bank(*banks)
        for i in range(4):
            kc = 4 * j + i
            c.tr(pb[:, i * 128:(i + 1) * 128], u32[:, kc * 128:(kc + 1) * 128], identf[:],
                 [Ru32, Rid], [Rp], signal=(i == 3))
        eng = "act" if j % 2 == 0 else "dve"
        c.cp(eng, uT[:, 4 * j:4 * j + 4, tok0:tok0 + 128],
             pb[:].rearrange("p (a b) -> p a b", a=4), [Rp], [RuT])


NFB = 22


def build_p1():
    nc = bass.Bass("TRN2", target_bir_lowering=False)
    dt = lambda name, shape, kind="ExternalInput", dty=F32: nc.dram_tensor(name, shape, dty, kind=kind).ap()
    x_d = dt("x", [SEQ, D])
    y_d = dt("y", [1280, SEQ], kind="ExternalOutput", dty=BF16)
    wd = dict(
        gain=dt("gain", [128, D]), wF=dt("wF", [NFB, 128, 16, 128]), wV=dt("wV", [128, 16, 384]),
        wAB=dt("wAB", [128, 16, 6]), cst=dt("cst", [128, 5, 128]), mb=dt("mb", [128, 4, 512]),
        c64=dt("c64", [64, 5, 64]), poolw=dt("poolw", [128, 4, 128]), poolc=dt("poolc", [128, 4, 18]),
        convw=dt("convw", [128, 9, 4]), gsc=dt("gsc", [64, 3, 32, 3]), gn=dt("gn", [128, 1]))
    with ExitStack() as es:
        S = Sched(nc, es)
        c = Ctx(nc, S, es)
        Ry = Reg()
        emit_p1(c, S, es, x_d, [y_d], [wd], Ry)
        S.barrier()
        S.wait_all("sp", [Ry])
    return nc


class _Stop(Exception):
    pass


def emit_p1(c, S, es0, x_d, y_ds, wds, Ry):
    try:
        _emit_p1(c, S, es0, x_d, y_ds, wds, Ry)
    except _Stop:
        S.barrier()


def _chk(c, name):
    import os
    if os.environ.get("P1_STOP") == name:
        c.S.barrier()
        raise _Stop()


def _emit_p1(c, S, es0, x_d, y_ds, wds, Ry):
    wd = wds[0]
    with ExitStack() as es:
        cstf = c.sb([128, 5, 128], F32, es)
        cstb = c.sb([128, 5, 128], BF16, es)
        c64f = c.sb([64, 5, 64], F32, es)
        poolw = c.sb([128, 4, 128], BF16, es)
        poolc = c.sb([128, 4, 18], F32, es)
        convw = c.sb([128, 9, 4], F32, es)
        gsc = c.sb([64, 3, 32, 3], F32, es)
        gn = c.sb([128, 1], F32, es)
        Rc = Reg()
        c.load("sp", cstf[:], wd["cst"], [Rc], "c0")
        c.load("pool", cstb[:], wd["cst"], [Rc], "c1")
        c.load("sp", c64f[:], wd["c64"], [Rc], "c3")
        c.load("pool", poolw[:], wd["poolw"], [Rc], "c5")
        c.load("sp", poolc[:], wd["poolc"], [Rc], "c6")
        S.barrier()
        identf = cstf[:, 0, :]
        identb = cstb[:, 0, :]
        TIb = cstb[:, 1, :]
        TSb = cstb[:, 2, :]
        onesb = cstb[:, 3, :]
        Rid = Rc

        uT = c.sb([128, 16, SEQ], BF16, es, name="uT")
        RuT = Reg()

        with ExitStack() as pes:
            gain_sb = c.sb([128, D], F32, pes)
            c.load("sp", gain_sb[:], wd["gain"], [Rc], "c10")
            S.barrier()
            xt = [c.sb([128, D], F32, pes) for _ in range(2)]
            Rxt = [Reg(), Reg()]
            scrs = []
            for _ in range(2):
                sq = c.sb([128, D], BF16, pes)
                ss = c.sb([128, 4], F32, pes)
                u32 = c.sb([128, D], F32, pes)
                scrs.append((sq, Reg(), ss, Reg(), u32, Reg()))
            RuTs = [Reg() for _ in range(16)]
            for tt in range(16):
                b = tt % 2
                c.load("sp", xt[b][:], x_d[tt * 128:(tt + 1) * 128, :], [Rxt[b]], f"x{b}")
                norm_transpose(c, xt[b][:], Rxt[b], gain_sb, Rc, identf, Rid, uT, RuTs[tt], tt * 128, scrs[b])
            S.barrier()

        _chk(c, "A")
        NWB = 3
        wcount = [0]

        def proj_blocks(blocks, wblk, Rw, post):
            def load_w(k):
                s = wcount[0] % NWB
                wcount[0] += 1
                c.load("pool", wblk[s][:], wF_d[blocks[k]], [Rw[s]], f"w{s}")
                return s
            slots = {}
            for k in range(min(2, len(blocks))):
                slots[k] = load_w(k)
            for k, i in enumerate(blocks):
                if k + 2 < len(blocks):
                    slots[k + 2] = load_w(k + 2)
                s = slots[k]
                for tb in range(4):
                    pb, Rp = c.bank()
                    for kc in range(16):
                        c.mm(pb[:], wblk[s][:, kc, :], uT[:, kc, tb * 512:(tb + 1) * 512], kc == 0, kc == 15,
                             [Rw[s], RuT], [Rp], signal=(kc == 15))
                    post(i, tb, pb, Rp)
                post(i, None, None, None)

        for hi, wd in enumerate(wds):
            y_d = y_ds[hi]
            wF_d, wV_d, wAB_d = wd["wF"], wd["wV"], wd["wAB"]
            Rc2 = Reg()
            c.load("sp", convw[:], wd["convw"], [Rc2], "c7")
            c.load("sp", gsc[:], wd["gsc"], [Rc2], "c8")
            c.load("sp", gn[:], wd["gn"], [Rc2], "c9")
            S.barrier()
            with ExitStack() as ses:
                sQT = c.sb([128, 3, SEQ], BF16, ses, name="sQT")
                sKT = c.sb([128, 3, SEQ], BF16, ses, name="sKT")
                sV = c.sb([128, 16, 384], BF16, ses, name="sV")
                mbb = c.sb([128, 4, 512], BF16, ses)
                Rmb = Reg()
                c.load("pool", mbb[:], wd["mb"], [Rmb], "c2")
                RsQ = [Reg() for _ in range(3)]
                RsK = [Reg() for _ in range(3)]
                RsV = Reg()
                with ExitStack() as pes:
                    wblk = [c.sb([128, 16, 128], BF16, pes) for _ in range(NWB)]
                    Rw = [Reg() for _ in range(NWB)]
                    wV = c.sb([128, 16, 384], BF16, pes)
                    RwV = Reg()
                    c.load("pool", wV[:], wV_d, [RwV], "wv")
                    big = c.sb([128, 16 + SEQ], F32, pes, name="big")
                    Rbig = Reg()
                    t1 = c.sb([128, SEQ], F32, pes, name="t1")
                    t2 = c.sb([128, SEQ], F32, pes, name="t2")
                    Rt1, Rt2 = Reg(), Reg()
                    tb16 = c.sb([128, SEQ], BF16, pes, name="tb16")
                    Rtb = Reg()
                    yst = c.sb([128, SEQ], BF16, pes, name="yst")
                    Ryst = Reg()
                    c.S.op("dve", lambda e: e.memset(big[:, 0:16], 0.0), writes=[Rbig])

                    def post_sb(i, tb, pb, Rp):
                        if tb is not None:
                            tsl = slice(tb * 512, (tb + 1) * 512)
                            if i < 4:
                                c.cp("act" if tb % 2 == 0 else "dve", big[:, 16 + tb * 512:16 + (tb + 1) * 512], pb[:], [Rp], [Rbig])
                            elif i < 7:
                                c.cp("act", sQT[:, i - 4, tsl], pb[:], [Rp], [RsQ[i - 4]])
                            else:
                                c.cp("act", sKT[:, i - 7, tsl], pb[:], [Rp], [RsK[i - 7]])
                            return
                        if i >= 4:
                            return
                        g = i
                        p = big[:, 16:16 + SEQ]
                        bufs = [t1, t2]
                        Rb = [Rt1, Rt2]
                        sh = 1
                        for lv in range(g + 1):
                            o = bufs[lv % 2]
                            if lv == 0:
                                c.tt("dve", o[:], p, big[:, 15:15 + SEQ], ALU.add, [Rbig], [Rb[0]])
                            else:
                                prev = bufs[(lv - 1) % 2]
                                c.tt("dve", o[:, sh:], prev[:, sh:], prev[:, 0:SEQ - sh], ALU.add, [Rb[(lv - 1) % 2]], [Rb[lv % 2]])
                                c.cp("dve", o[:, 0:sh], prev[:, 0:sh], [Rb[(lv - 1) % 2]], [Rb[lv % 2]])
                            sh *= 2
                        sw, Rsw = bufs[g % 2], Rb[g % 2]
                        dd, Rdd = bufs[(g + 1) % 2], Rb[(g + 1) % 2]
                        c.stt("dve", dd[:, 16:], sw[:, 16:], poolc[:, g, 1:2], big[:, 32:16 + SEQ], ALU.mult, ALU.subtract,
                              [Rsw, Rbig, Rc], [Rdd])
                        c.tt("dve", dd[:, 0:16], sw[:, 0:16], poolc[:, g, 2:18], ALU.mult, [Rsw, Rc], [Rdd])
                        c.tt("dve", dd[:, 0:16], dd[:, 0:16], big[:, 16:32], ALU.subtract, [Rdd, Rbig], [Rdd])
                        c.cp("act", tb16[:], dd[:], [Rdd], [Rtb])
                        for tb2 in range(4):
                            pb2, Rp2 = c.bank()
                            c.mm(pb2[:], poolw[:, g, :], tb16[:, tb2 * 512:(tb2 + 1) * 512], True, True, [Rtb, Rc], [Rp2])
                            c.ts("dve", yst[:, tb2 * 512:(tb2 + 1) * 512], pb2[:], poolc[:, g, 0:1], None, ALU.mult, None,
                                 [Rp2, Rc], [Ryst])
                        c.store("sp", y_d[g * 128:(g + 1) * 128, :], yst[:], [Ryst], [Ry], "yo")

                    proj_blocks(list(range(10)) if hi == 0 else list(range(4, 10)), wblk, Rw, post_sb)
                    _chk(c, "SBP")
                    for tt in range(16):
                        pb, Rp = c.bank()
                        for kc in range(16):
                            c.mm(pb[:, 0:384], uT[:, kc, tt * 128:(tt + 1) * 128], wV[:, kc, :], kc == 0, kc == 15,
                                 [RwV, RuT], [Rp], signal=(kc == 15))
                        c.cp("act" if tt % 2 else "dve", sV[:, tt, :], pb[:, 0:384], [Rp], [RsV])
                    S.barrier()

                _chk(c, "SBV")
                with ExitStack() as pes:
                    NE = 6
                    Eb = [c.sb([128, 512], F32, pes) for _ in range(NE)]
                    SPb = [c.sb([128, 512], BF16, pes) for _ in range(NE)]
                    Xb = [c.sb([128, 512], F32, pes) for _ in range(NE)]
                    Ab = [c.sb([128, 512], BF16, pes) for _ in range(NE)]
                    RE = [Reg() for _ in range(NE)]
                    RSP = [Reg() for _ in range(NE)]
                    RX = [Reg() for _ in range(NE)]
                    RA = [Reg() for _ in range(NE)]
                    ost = [c.sb([128, 512], BF16, pes) for _ in range(2)]
                    Rost = [Reg(), Reg()]
                    scale = float(128 ** -0.5)
                    items = []
                    for QB in range(4):
                        for kb in range(4 * QB + 3, -1, -1):
                            for h in range(3):
                                items.append((QB, kb, h))
                    n_it = len(items)
                    oc = [0]

                    def stage0(idx):
                        QB, kb, h = items[idx]
                        s = idx % NE
                        zb, Rz = c.pb[idx % 2], c.Rpb[idx % 2]
                        q0 = QB * 512
                        diag = kb >= 4 * QB
                        c.mm(zb[:], sKT[:, h, kb * 128:(kb + 1) * 128], sQT[:, h, q0:q0 + 512], True, not diag,
                             [RsK[h], RsQ[h]], [Rz], signal=not diag)
                        if diag:
                            c.mm(zb[:], identb, mbb[:, kb - 4 * QB, :], False, True, [Rc, Rmb], [Rz])
                        c.act(Eb[s][:], zb[:], AF.Exp, [Rz], [RE[s]], scale=scale)
                        c.act(SPb[s][:], Eb[s][:], AF.Ln, [RE[s]], [RSP[s]], bias=1.0)

                    def stage1(idx):
                        QB, kb, h = items[idx]
                        s = idx % NE
                        cb, Rcb = c.pb[2 + h], c.Rpb[2 + h]
                        first = kb == 4 * QB + 3
                        c.mm(cb[:], TIb, SPb[s][:], first, True, [Rc, RSP[s]], [Rcb])
                        c.act(Xb[s][:], cb[:], AF.Exp, [Rcb], [RX[s]], scale=-1.0)

                    def stage2(idx):
                        QB, kb, h = items[idx]
                        s = idx % NE
                        cb, Rcb = c.pb[2 + h], c.Rpb[2 + h]
                        c.mm(cb[:], TSb, SPb[s][:], False, True, [Rc, RSP[s]], [Rcb])
                        c.tt("dve" if idx % 2 else "pool", Ab[s][:], Eb[s][:], Xb[s][:], ALU.mult, [RE[s], RX[s]], [RA[s]])

                    def stage3(idx):
                        QB, kb, h = items[idx]
                        s = idx % NE
                        ob, Rob = c.pb[5 + h], c.Rpb[5 + h]
                        first = kb == 4 * QB + 3
                        c.mm(ob[:], sV[:, kb, h * 128:(h + 1) * 128], Ab[s][:], first, kb == 0, [RsV, RA[s]], [Rob])
                        if kb == 0:
                            o = oc[0] % 2
                            oc[0] += 1
                            c.cp("dve", ost[o][:], ob[:], [Rob], [Rost[o]])
                            c.store("sp", y_d[512 + h * 128:512 + (h + 1) * 128, QB * 512:(QB + 1) * 512], ost[o][:],
                                    [Rost[o]], [Ry], f"so{o}")

                    for idx in range(n_it + 3):
                        if idx < n_it:
                            stage0(idx)
                        if 1 <= idx < n_it + 1:
                            stage1(idx - 1)
                        if 2 <= idx < n_it + 2:
                            stage2(idx - 2)
                        if 3 <= idx:
                            stage3(idx - 3)
                    S.barrier()

            _chk(c, "SB")
            with ExitStack() as ges:
                tri64 = c64f[:, 0, :]
                id64f = c64f[:, 3, :]
                onesf128 = cstf[0:64, 3, :]
                abc = c.sb([64, 32, 6], F32, ges, name="abc")
                Rabc = Reg()
                la = c.sb([64, 32, 3], F32, ges)
                beta = c.sb([64, 32, 3], F32, ges)
                gcol = c.sb([64, 32, 3], F32, ges)
                glast = c.sb([64, 32, 3], F32, ges)
                egcol = c.sb([64, 32, 3], F32, ges)
                sc_kbg = c.sb([64, 32, 3], F32, ges)
                sc_kdec = c.sb([64, 32, 3], F32, ges)
                decB = c.sb([128, 32, 3], F32, ges)
                tmp = c.sb([64, 32, 3], F32, ges)
                nA = c.sb([64, 32, 3], F32, ges)
                Rs = Reg()
                with ExitStack() as pes:
                    wAB = c.sb([128, 16, 6], BF16, pes)
                    RwAB = Reg()
                    c.load("pool", wAB[:], wAB_d, [RwAB], "wv")
                    pb, Rp = c.bank()
                    for n in range(32):
                        for kc in range(16):
                            c.mm(pb[0:64, n * 6:(n + 1) * 6], uT[:, kc, n * 64:(n + 1) * 64], wAB[:, kc, :], kc == 0, kc == 15,
                                 [RwAB, RuT], [Rp], signal=(kc == 15 and n == 31))
                    c.cp("dve", abc[:].rearrange("p a b -> p (a b)"), pb[0:64, 0:192], [Rp], [Rabc])
                    a_ap = abc[:, :, 0:3]
                    b_ap = abc[:, :, 3:6]
                    c.tt("dve", tmp[:], a_ap, gsc[:, 0, :, :], ALU.add, [Rabc, Rc], [Rs])
                    c.act(tmp[:], tmp[:], AF.Exp, [Rs], [Rs])
                    c.act(tmp[:], tmp[:], AF.Ln, [Rs], [Rs], bias=1.0)
                    c.act(nA[:], gsc[:, 1, :, :], AF.Exp, [Rc, Rs], [Rs])
                    c.stt("dve", la[:], tmp[:], -1.0, nA[:], ALU.mult, ALU.mult, [Rs], [Rs])
                    c.act(tmp[:], b_ap, AF.Exp, [Rabc, Rs], [Rs], scale=-1.0)
                    c.ts("dve", tmp[:], tmp[:], 1.0, None, ALU.add, None, [Rs], [Rs])
                    c.S.op("dve", lambda e: e.reciprocal(out=beta[:], in_=tmp[:]), reads=[Rs], writes=[Rs])
                    la2 = la[:].rearrange("p a b -> p (a b)")
                    pb, Rp = c.bank()
                    c.mm(pb[0:64, 0:96], tri64, la2, True, True, [Rc, Rs], [Rp])
                    c.cp("dve", gcol[:].rearrange("p a b -> p (a b)"), pb[0:64, 0:96], [Rp], [Rs])
                    pb, Rp = c.bank()
                    c.mm(pb[:, 0:96], onesf128, la2, True, True, [Rc, Rs], [Rp])
                    c.cp("dve", glast[:].rearrange("p a b -> p (a b)"), pb[0:64, 0:96], [Rp], [Rs])
                    c.act(decB[:].rearrange("p a b -> p (a b)"), pb[:, 0:96], AF.Exp, [Rp, Rs], [Rs])
                    c.act(egcol[:], gcol[:], AF.Exp, [Rs], [Rs])
                    c.tt("dve", sc_kbg[:], egcol[:], beta[:], ALU.mult, [Rs], [Rs])
                    c.tt("dve", tmp[:], glast[:], gcol[:], ALU.subtract, [Rs], [Rs])
                    c.act(sc_kdec[:], tmp[:], AF.Exp, [Rs], [Rs])
                    S.barrier()

                _chk(c, "GAB")
                for h in range(3):
                    with ExitStack() as hes:
                        gQT = c.sb([128, SEQ], BF16, hes, name=f"gQT{h}")
                        gKT = c.sb([128, SEQ], BF16, hes, name=f"gKT{h}")
                        gKtok = c.sb([64, 32, 128], BF16, hes, name=f"gKtok{h}")
                        gVtok = c.sb([64, 32, 128], BF16, hes, name=f"gVtok{h}")
                        gZ = c.sb([128, SEQ], BF16, hes, name=f"gZ{h}")
                        P = c.sb([64, 32, 64], BF16, hes, name=f"gP{h}")
                        vb = c.sb([64, 32, 128], BF16, hes, name=f"gvb{h}")
                        kdec = c.sb([64, 32, 128], BF16, hes, name=f"gkdec{h}")
                        nwT = c.sb([128, SEQ], BF16, hes, name=f"gnwT{h}")
                        qdT = c.sb([128, SEQ], BF16, hes, name=f"gqdT{h}")
                        QKg = c.sb([64, 32, 64], BF16, hes, name=f"gQKg{h}")
                        RgQ, RgK, RgKt, RgVt, RgZ, RP, Rvb, Rkd, Rnw, Rqd, Rqk = [Reg() for _ in range(11)]
                        with ExitStack() as pes:
                            wblk = [c.sb([128, 16, 128], BF16, pes) for _ in range(NWB)]
                            Rw = [Reg() for _ in range(NWB)]
                            big = c.sb([128, 16 + SEQ], F32, pes, name=f"gbig{h}")
                            Rbig = Reg()
                            t1 = c.sb([128, SEQ], F32, pes, name=f"gt1{h}")
                            t2 = c.sb([128, SEQ], F32, pes, name=f"gt2{h}")
                            Rt1, Rt2 = Reg(), Reg()
                            tb16 = c.sb([128, SEQ], BF16, pes, name=f"gtb16{h}")
                            Rtb = Reg()
                            c.S.op("dve", lambda e: e.memset(big[:, 0:16], 0.0), writes=[Rbig])

                            def post_g(i, tb, pb, Rp):
                                kind = (i - 10) // 3
                                if tb is not None:
                                    tsl = slice(tb * 512, (tb + 1) * 512)
                                    if kind < 3:
                                        c.cp("act" if tb % 2 == 0 else "dve", big[:, 16 + tb * 512:16 + (tb + 1) * 512], pb[:],
                                             [Rp], [Rbig])
                                    else:
                                        c.act(gZ[:, tsl], pb[:], AF.Silu, [Rp], [RgZ])
                                    return
                                if kind == 3:
                                    return
                                j = kind * 3 + h
                                c.ts("dve", t1[:], big[:, 13:13 + SEQ], convw[:, j, 0:1], None, ALU.mult, None, [Rbig, Rc], [Rt1])
                                for ci in range(1, 4):
                                    c.stt("dve", t1[:], big[:, 13 + ci:13 + ci + SEQ], convw[:, j, ci:ci + 1], t1[:],
                                          ALU.mult, ALU.add, [Rbig, Rc, Rt1], [Rt1])
                                c.act(t2[:], t1[:], AF.Silu, [Rt1], [Rt2])
                                if kind < 2:
                                    c.act(tb16[:], t2[:], AF.Square, [Rt2], [Rtb])
                                    for tb2 in range(4):
                                        pb2, Rp2 = c.bank()
                                        c.mm(pb2[:], onesb, tb16[:, tb2 * 512:(tb2 + 1) * 512], True, True, [Rtb, Rc], [Rp2])
                                        c.act(t1[:, tb2 * 512:(tb2 + 1) * 512], pb2[:], AF.Ln, [Rp2], [Rt1], bias=EPS)
                                    c.act(t1[:], t1[:], AF.Exp, [Rt1], [Rt1], scale=-0.5)
                                    if kind == 0:
                                        c.stt("dve", gQT[:], t2[:], float(128 ** -0.5), t1[:], ALU.mult, ALU.mult, [Rt1, Rt2], [RgQ])
                                    else:
                                        c.tt("dve", t2[:], t2[:], t1[:], ALU.mult, [Rt1, Rt2], [Rt2])
                                        c.cp("act", gKT[:], t2[:], [Rt2], [RgK])
                                if kind >= 1:
                                    dst, Rdst = (gKtok, RgKt) if kind == 1 else (gVtok, RgVt)
                                    for grp in range(8):
                                        pb2, Rp2 = c.bank()
                                        for q in range(4):
                                            n = grp * 4 + q
                                            c.tr(pb2[0:64, q * 128:(q + 1) * 128], t2[:, n * 64:(n + 1) * 64], identf, [Rt2, Rid],
                                                 [Rp2], signal=(q == 3))
                                        c.cp("act" if grp % 2 else "dve", dst[:, grp * 4:grp * 4 + 4, :],
                                             pb2[0:64, :].rearrange("p (a b) -> p a b", a=4), [Rp2], [Rdst])

                            proj_blocks([10 + h, 13 + h, 16 + h, 19 + h], wblk, Rw, post_g)
                            S.barrier()

                        _chk(c, "GP")
                        with ExitStack() as pes:
                            labt = c.sb([64, 32, 64], F32, pes)
                            Dm = c.sb([64, 32, 64], F32, pes)
                            gamL = c.sb([64, 32, 64], F32, pes)
                            egB = c.sb([128, SEQ], F32, pes)
                            kbg = c.sb([64, 32, 128], BF16, pes)
                            Mk = [c.sb([64, 32, 64], BF16, pes) for _ in range(2)]
                            Nk = [c.sb([64, 32, 64], BF16, pes) for _ in range(2)]
                            INk = c.sb([64, 32, 64], BF16, pes)
                            Pt = [c.sb([64, 32, 64], BF16, pes) for _ in range(2)]
                            Rl, RD, RgL, ReB, Rkbg, RIN = [Reg() for _ in range(6)]
                            RMk = [[Reg() for _ in range(4)] for _ in range(2)]
                            RNk = [[Reg() for _ in range(4)] for _ in range(2)]
                            RPt = [[Reg() for _ in range(4)] for _ in range(2)]
                            RINg = [Reg() for _ in range(4)]
                            RNfg = [Reg() for _ in range(4)]
                            c.tt("dve", labt[:], la[:, :, h:h + 1].to_broadcast([64, 32, 64]),
                                 c64f[:, 0:1, :].to_broadcast([64, 32, 64]), ALU.mult, [Rs, Rc], [Rl])
                            _chk(c, "G0a")
                            l2 = labt[:].rearrange("p a b -> p (a b)")
                            for tb in range(4):
                                pb, Rp = c.bank()
                                c.mm(pb[:], onesf128, l2[:, tb * 512:(tb + 1) * 512], True, True, [Rc, Rl], [Rp])
                                c.act(egB[:, tb * 512:(tb + 1) * 512], pb[:], AF.Exp, [Rp], [ReB])
                                c.tt("dve", Dm[:, tb * 8:(tb + 1) * 8, :],
                                     gcol[:, tb * 8:(tb + 1) * 8, h:h + 1].to_broadcast([64, 8, 64]),
                                     pb[0:64, :].rearrange("p (a b) -> p a b", a=8), ALU.subtract, [Rp, Rs, ReB], [RD])
                            _chk(c, "G0b")
                            c.ts("dve", gamL[:], Dm[:], 0.0, None, ALU.min, None, [RD], [RgL])
                            c.ts("dve", Dm[:], Dm[:], -1.0, 0.0, ALU.mult, ALU.min, [RD, RgL], [RD])
                            gamU, RgU = Dm, RD
                            c.act(gamL[:], gamL[:], AF.Exp, [RgL], [RgL])
                            c.act(gamU[:], gamU[:], AF.Exp, [RgU], [RgU])
                            c.tt("dve", gamL[:], gamL[:], c64f[:, 1:2, :].to_broadcast([64, 32, 64]), ALU.mult, [RgL, Rc], [RgL])
                            c.tt("dve", gamU[:], gamU[:], c64f[:, 2:3, :].to_broadcast([64, 32, 64]), ALU.mult, [RgU, Rc], [RgU])
                            _chk(c, "G0c")
                            c.tt("dve", qdT[:], gQT[:], egB[:], ALU.mult, [RgQ, ReB], [Rqd])
                            c.tt("dve", kbg[:], gKtok[:], sc_kbg[:, :, h:h + 1].to_broadcast([64, 32, 128]), ALU.mult,
                                 [RgKt, Rs], [Rkbg])
                            c.tt("dve", kdec[:], gKtok[:], sc_kdec[:, :, h:h + 1].to_broadcast([64, 32, 128]), ALU.mult,
                                 [RgKt, Rs], [Rkd])
                            c.tt("dve", vb[:], gVtok[:], beta[:, :, h:h + 1].to_broadcast([64, 32, 128]), ALU.mult,
                                 [RgVt, Rs], [Rvb])
                            _chk(c, "G1")
                            Nf = labt
                            S.wait_all("dve", [Rl])
                            for grp in range(4):
                                pb, Rp = c.bank()
                                for q in range(8):
                                    n = grp * 8 + q
                                    ks = gKT[:, n * 64:(n + 1) * 64]
                                    c.mm(pb[0:64, q * 64:(q + 1) * 64], ks, ks, True, True, [RgK], [Rp], signal=(q == 7))
                                gs = slice(grp * 8, grp * 8 + 8)
                                pv = pb[0:64, :].rearrange("p (a b) -> p a b", a=8)
                                c.tt("dve", Nf[:, gs, :], pv, gamL[:, gs, :], ALU.mult, [Rp, RgL, Rl], [RNfg[grp]])
                                c.tt("dve", Nf[:, gs, :], Nf[:, gs, :], beta[:, gs, h:h + 1].to_broadcast([64, 8, 64]), ALU.mult,
                                     [RNfg[grp], Rs], [RNfg[grp]])
                                c.cp("act", Nk[0][:, gs, :], Nf[:, gs, :], [RNfg[grp]], [RNk[0][grp]])
                                pb2, Rp2 = c.bank()
                                for q in range(8):
                                    n = grp * 8 + q
                                    c.mm(pb2[0:64, q * 64:(q + 1) * 64], gKT[:, n * 64:(n + 1) * 64], gQT[:, n * 64:(n + 1) * 64],
                                         True, True, [RgK, RgQ], [Rp2], signal=(q == 7))
                                c.tt("dve", QKg[:, gs, :], pb2[0:64, :].rearrange("p (a b) -> p a b", a=8), gamU[:, gs, :], ALU.mult,
                                     [Rp2, RgU], [Rqk])
                                pb3, Rp3 = c.bank()
                                for q in range(8):
                                    n = grp * 8 + q
                                    c.tr(pb3[0:64, q * 64:(q + 1) * 64], Nf[:, n, :], id64f, [RNfg[grp], Rc], [Rp3], signal=(q == 7))
                                pv3 = pb3[0:64, :].rearrange("p (a b) -> p a b", a=8)
                                c.cp("act", Mk[0][:, gs, :], pv3, [Rp3], [RMk[0][grp]])
                                c.stt("dve", Pt[0][:, gs, :], pv3, -1.0, c64f[:, 3:4, :].to_broadcast([64, 8, 64]), ALU.mult, ALU.add,
                                      [Rp3, Rc, RMk[0][grp]], [RPt[0][grp]])
                            _chk(c, "G2")
                            for lv in range(5):
                                a, b = lv % 2, (lv + 1) % 2
                                last = lv == 4
                                for grp in range(4):
                                    gs = slice(grp * 8, grp * 8 + 8)
                                    pbN, RpN = c.bank()
                                    for q in range(8):
                                        n = grp * 8 + q
                                        c.mm(pbN[0:64, q * 64:(q + 1) * 64], Mk[a][:, n, :], Nk[a][:, n, :], True, True,
                                             [RMk[a][grp], RNk[a][grp]], [RpN], signal=(q == 7))
                                    pvN = pbN[0:64, :].rearrange("p (a b) -> p a b", a=8)
                                    c.stt("dve", INk[:, gs, :], pvN, 1.0, c64f[:, 3:4, :].to_broadcast([64, 8, 64]), ALU.mult, ALU.add,
                                          [RpN, Rc], [RINg[grp]])
                                    if not last:
                                        c.cp("act", Nk[b][:, gs, :], pvN, [RpN, RINg[grp]], [RNk[b][grp]])
                                        pbM, RpM = c.bank()
                                        for q in range(8):
                                            n = grp * 8 + q
                                            c.mm(pbM[0:64, q * 64:(q + 1) * 64], Nk[a][:, n, :], Mk[a][:, n, :], True, True,
                                                 [RMk[a][grp], RNk[a][grp]], [RpM], signal=(q == 7))
                                        c.cp("act", Mk[b][:, gs, :], pbM[0:64, :].rearrange("p (a b) -> p a b", a=8), [RpM], [RMk[b][grp]])
                                    pbP, RpP = c.bank()
                                    for q in range(8):
                                        n = grp * 8 + q
                                        c.mm(pbP[0:64, q * 64:(q + 1) * 64], INk[:, n, :], Pt[a][:, n, :], True, True,
                                             [RINg[grp], RPt[a][grp]], [RpP], signal=(q == 7))
                                    pvP = pbP[0:64, :].rearrange("p (a b) -> p a b", a=8)
                                    if last:
                                        c.cp("dve", P[:, gs, :], pvP, [RpP], [RP])
                                    else:
                                        c.cp("dve", Pt[b][:, gs, :], pvP, [RpP], [RPt[b][grp]])
                            for grp in range(4):
                                pb, Rp = c.bank()
                                for q in range(8):
                                    n = grp * 8 + q
                                    c.mm(pb[:, q * 64:(q + 1) * 64], kbg[:, n, :], P[:, n, :], True, True, [Rkbg, RP], [Rp],
                                         signal=(q == 7))
                                c.ts("dve", nwT[:, grp * 512:(grp + 1) * 512], pb[:], -1.0, None, ALU.mult, None, [Rp], [Rnw])
                            S.barrier()

                        _chk(c, "GPREP")
                        with ExitStack() as res:
                            Sf = c.sb([128, 128], F32, res)
                            Sb = [c.sb([128, 128], BF16, res) for _ in range(2)]
                            vn = [c.sb([64, 128], BF16, res) for _ in range(2)]
                            oT = c.sb([128, SEQ], F32, res, name=f"goT{h}")
                            sqb = c.sb([128, SEQ], BF16, res)
                            rn = c.sb([128, SEQ], F32, res)
                            yst = c.sb([128, SEQ], BF16, res)
                            RSf, RoT, Rsq, Rrn, Ryst = [Reg() for _ in range(5)]
                            RSb = [Reg(), Reg()]
                            Rvn = [Reg(), Reg()]
                            c.S.op("dve", lambda e: e.memset(Sf[:], 0.0), writes=[RSf])
                            c.S.op("pool", lambda e: e.memset(Sb[0][:], 0.0), writes=[RSb[0]])
                            for n in range(32):
                                a, b = n % 2, (n + 1) % 2
                                vps, Rvps = c.pb[n % 2], c.Rpb[n % 2]
                                ops, Rops = c.pb[2 + (n // 8) % 2], c.Rpb[2 + (n // 8) % 2]
                                dps, Rdps = c.pb[4 + n % 2], c.Rpb[4 + n % 2]
                                q = n % 8
                                c.mm(vps[0:64, 0:128], P[:, n, :], vb[:, n, :], True, False, [RP, Rvb], [Rvps], signal=False)
                                c.mm(vps[0:64, 0:128], nwT[:, n * 64:(n + 1) * 64], Sb[a][:], False, True, [Rnw, RSb[a]], [Rvps])
                                c.cp("act", vn[a][:], vps[0:64, 0:128], [Rvps], [Rvn[a]])
                                c.mm(ops[:, q * 64:(q + 1) * 64], Sb[a][:], qdT[:, n * 64:(n + 1) * 64], True, False,
                                     [RSb[a], Rqd], [Rops], signal=False)
                                c.mm(ops[:, q * 64:(q + 1) * 64], vn[a][:], QKg[:, n, :], False, True, [Rvn[a], Rqk], [Rops])
                                c.mm(dps[:, 0:128], kdec[:, n, :], vn[a][:], True, True, [Rkd, Rvn[a]], [Rdps])
                                c.stt("dve", Sb[b][:], Sf[:], decB[:, n, h:h + 1], dps[:, 0:128], ALU.mult, ALU.add,
                                      [RSf, Rs, Rdps], [RSb[b]])
                                c.stt("dve", Sf[:], Sf[:], decB[:, n, h:h + 1], dps[:, 0:128], ALU.mult, ALU.add,
                                      [RSf, Rs, Rdps], [RSf])
                                if q == 7:
                                    c.cp("act", oT[:, (n - 7) * 64:(n + 1) * 64], ops[:], [Rops], [RoT])
                            c.act(sqb[:], oT[:], AF.Square, [RoT], [Rsq])
                            for tb in range(4):
                                pb, Rp = c.bank()
                                c.mm(pb[:], onesb, sqb[:, tb * 512:(tb + 1) * 512], True, True, [Rc, Rsq], [Rp])
                                c.act(rn[:, tb * 512:(tb + 1) * 512], pb[:], AF.Ln, [Rp], [Rrn], scale=1.0 / 128, bias=EPS)
                            c.act(rn[:], rn[:], AF.Exp, [Rrn], [Rrn], scale=-0.5)
                            c.stt("dve", rn[:], oT[:], gn[:, 0:1], rn[:], ALU.mult, ALU.mult, [RoT, Rrn, Rc], [Rrn])
                            c.tt("dve", yst[:], rn[:], gZ[:], ALU.mult, [Rrn, RgZ], [Ryst])
                            c.store("sp", y_d[896 + h * 128:896 + (h + 1) * 128, :], yst[:], [Ryst], [Ry], "go")
                            S.barrier()
        S.barrier()


OFF_P, OFF_SQ, OFF_SK, OFF_SV = 0, 512, 1280, 2048
OFF_GQ, OFF_GK, OFF_GV, OFF_GZ, OFF_A, OFF_B, OFF_G = 2816, 3584, 4352, 5120, 5888, 5894, 5900


def _blk(w):
    n = w.shape[1]
    return np.ascontiguousarray(w.reshape(16, 128, n).transpose(1, 0, 2))


def p1_consts():
    j = np.arange(128)
    cst = np.zeros((128, 5, 128), np.float32)
    cst[:, 0, :] = np.eye(128)
    cst[:, 1, :] = (j[:, None] >= j[None, :])
    cst[:, 2, :] = (j[:, None] < j[None, :])
    cst[:, 3, :] = 1.0
    mb = np.zeros((128, 4, 512), np.float32)
    q = np.arange(512)
    for jj in range(4):
        mb[:, jj, :] = np.where((128 * jj + j[:, None]) < q[None, :], 0.0, NEG)
    i = np.arange(64)
    c64 = np.zeros((64, 5, 64), np.float32)
    c64[:, 0, :] = (i[:, None] <= i[None, :])
    c64[:, 1, :] = (i[:, None] > i[None, :])
    c64[:, 2, :] = (i[None, :] >= i[:, None])
    c64[:, 3, :] = np.eye(64)
    c64[:, 4, :] = 1.0
    return cst, mb, c64


def p1_inputs(inp, l, hh, consts):
    cst, mb, c64 = consts
    W = inp["w_in"][l]
    hg = [hh * 3 + h for h in range(3)]
    cols = [OFF_P + g * 128 for g in range(4)]
    for off in (OFF_SQ, OFF_SK, OFF_GQ, OFF_GK, OFF_GV, OFF_GZ):
        cols += [off + h * 128 for h in hg]
    wF = np.stack([_blk(W[:, c0:c0 + 128]) for c0 in cols])
    wV = _blk(np.concatenate([W[:, OFF_SV + h * 128:OFF_SV + (h + 1) * 128] for h in hg], axis=1))
    wAB = _blk(np.concatenate([W[:, [OFF_A + h for h in hg]], W[:, [OFF_B + h for h in hg]]], axis=1))
    poolw = np.ascontiguousarray(inp["pool_w"][l].transpose(1, 0, 2))
    poolc = np.zeros((128, 4, 18), np.float32)
    t = np.arange(16)
    for g in range(4):
        w = 2 ** (g + 1)
        poolc[:, g, 0] = inp["pool_scale"][l][g * 128:(g + 1) * 128]
        poolc[:, g, 1] = 1.0 / w
        poolc[:, g, 2:18] = (1.0 / np.minimum(t + 1, w))[None, :]
    convw = np.zeros((128, 9, 4), np.float32)
    for kind in range(3):
        for h in range(3):
            ch0 = kind * 768 + hg[h] * 128
            convw[:, kind * 3 + h, :] = inp["gdn_conv"][l][:, ch0:ch0 + 128].T
    gsc = np.zeros((64, 3, 32, 3), np.float32)
    for h in range(3):
        gsc[:, 0, :, h] = inp["gdn_dt_bias"][l][hg[h]]
        gsc[:, 1, :, h] = inp["gdn_a_log"][l][hg[h]]
    gn = np.ascontiguousarray(inp["gdn_norm"][l][:, None])
    gain = np.ascontiguousarray(np.broadcast_to(inp["attn_norm"][l][None, :], (128, D)))
    return dict(gain=gain, wF=wF, wV=wV, wAB=wAB, cst=cst, mb=mb, c64=c64, poolw=poolw, poolc=poolc,
                convw=convw, gsc=gsc, gn=gn)


TT = 1024
NT = TT // 128


def build_p2(final):
    nc = bass.Bass("TRN2", target_bir_lowering=False)
    dt = lambda name, shape, kind="ExternalInput", dty=F32: nc.dram_tensor(name, shape, dty, kind=kind).ap()
    x_d = dt("x", [TT, D])
    yT_d = dt("yT", [D, TT], dty=BF16)
    o_d = dt("o", [TT, D], kind="ExternalOutput")
    wd = dict(gain1=dt("gain1", [128, D]), gain2=dt("gain2", [128, D]), gainF=dt("gainF", [128, D]),
              ident=dt("ident", [128, 128]),
              wG=dt("wG", [48, 128, 16, 128]), wUp=dt("wUp", [16, 128, 16, 128]), wO=dt("wO", [4, 128, 16, 512]),
              wF1=dt("wF1", [64, 128, 16, 128]), wF2=dt("wF2", [8, 4, 128, 8, 512]))
    with ExitStack() as es:
        S = Sched(nc, es)
        c = Ctx(nc, S, es)
        Ro = Reg()
        emit_p2(c, S, x_d, lambda kc: yT_d[kc * 128:(kc + 1) * 128, :], o_d, wd, Ro, final)
        S.barrier()
        S.wait_all("sp", [Ro])
    return nc


def emit_p2(c, S, x_d, yrows, o_d, wd, Ro, final):
    with ExitStack() as es:
        identf = c.sb([128, 128], F32, es)
        Rc = Reg()
        c.load("sp", identf[:], wd["ident"], [Rc], "c0")
        xs = c.sb([128, NT, D], F32, es, name="xres")
        Rx = [Reg() for _ in range(NT)]
        for tt in range(NT):
            c.load("sp", xs[:, tt, :], x_d[tt * 128:(tt + 1) * 128, :], [Rx[tt]], f"xl{tt % 2}")
        S.barrier()
        NWB = 3
        wcount = [0]

        def stream_blocks(src_list, wblk, Rw, body):
            nb = len(wblk)
            ahead = nb - 1

            def load_w(k):
                s = wcount[0] % nb
                wcount[0] += 1
                c.load("pool", wblk[s][:], src_list[k], [Rw[s]], f"w{s}")
                return s
            slots = {}
            for k in range(min(ahead, len(src_list))):
                slots[k] = load_w(k)
            for k in range(len(src_list)):
                if k + ahead < len(src_list):
                    slots[k + ahead] = load_w(k + ahead)
                s = slots[k]
                body(k, wblk[s], Rw[s])

        def do_norm(gain_key, uT, RuT, pes):
            gain_sb = c.sb([128, D], F32, pes)
            Rg = Reg()
            c.load("sp", gain_sb[:], wd[gain_key], [Rg], "gl")
            scrs = []
            for _ in range(2):
                sq = c.sb([128, D], BF16, pes)
                ss = c.sb([128, 4], F32, pes)
                u32 = c.sb([128, D], F32, pes)
                scrs.append((sq, Reg(), ss, Reg(), u32, Reg()))
            RuTs = [Reg() for _ in range(NT)]
            for tt in range(NT):
                norm_transpose(c, xs[:, tt, :], Rx[tt], gain_sb, Rg, identf, Rc, uT, RuTs[tt], tt * 128, scrs[tt % 2])
            S.barrier()

        with ExitStack() as aes:
            uT = c.sb([128, 16, TT], BF16, aes, name="uT2")
            yT = c.sb([128, 16, TT], BF16, aes, name="yT2")
            mT = c.sb([128, 16, TT], BF16, aes, name="mT2")
            RuT, RyT, RmT = Reg(), Reg(), Reg()
            for kc in range(16):
                c.load("sp", yT[:, kc, :], yrows(kc), [RyT], f"yl{kc % 2}")
            with ExitStack() as pes:
                do_norm("gain1", uT, RuT, pes)
            S.barrier()
            with ExitStack() as pes:
                wblk = [c.sb([128, 16, 128], BF16, pes) for _ in range(NWB)]
                Rw = [Reg() for _ in range(NWB)]
                sig = [c.sb([128, 512], F32, pes) for _ in range(2)]
                Rsig = [Reg(), Reg()]
                macc = [c.sb([128, 512], F32, pes) for _ in range(2)]
                Rmacc = [Reg(), Reg()]
                tmpm = c.sb([128, 512], F32, pes)
                Rtmp = Reg()
                srcs = []
                for dc in range(16):
                    srcs += [wd["wG"][dc * 3 + br] for br in range(3)] + [wd["wUp"][dc]]
                kr = [(0, 4), (4, 10), (10, 16)]
                cnt = [0]

                def body(k, wb, Rwb):
                    dc, j = k // 4, k % 4
                    if j < 3:
                        body.gw[j] = (wb, Rwb)
                        return
                    for tb in range(2):
                        tsl = slice(tb * 512, (tb + 1) * 512)
                        mi = cnt[0] % 2
                        cnt[0] += 1
                        for br in range(3):
                            gwb, Rgw = body.gw[br]
                            pg, Rpg = c.bank()
                            for kc in range(16):
                                c.mm(pg[:], gwb[:, kc, :], uT[:, kc, tsl], kc == 0, kc == 15, [Rgw, RuT], [Rpg],
                                     signal=(kc == 15))
                            si = (cnt[0] + br) % 2
                            c.act(sig[si][:], pg[:], AF.Sigmoid, [Rpg], [Rsig[si]])
                            pu, Rpu = c.bank()
                            k0, k1 = kr[br]
                            for kc in range(k0, k1):
                                c.mm(pu[:], wb[:, kc, :], yT[:, kc, tsl], kc == k0, kc == k1 - 1, [Rwb, RyT], [Rpu],
                                     signal=(kc == k1 - 1))
                            if br == 0:
                                c.tt("dve", macc[mi][:], pu[:], sig[si][:], ALU.mult, [Rpu, Rsig[si]], [Rmacc[mi]])
                            else:
                                c.tt("dve", tmpm[:], pu[:], sig[si][:], ALU.mult, [Rpu, Rsig[si]], [Rtmp])
                                if br == 1:
                                    c.tt("pool", macc[mi][:], macc[mi][:], tmpm[:], ALU.add, [Rmacc[mi], Rtmp], [Rmacc[mi]])
                                else:
                                    c.tt("pool", mT[:, dc, tsl], macc[mi][:], tmpm[:], ALU.add, [Rmacc[mi], Rtmp], [RmT])
                body.gw = {}
                wblk5 = wblk + [c.sb([128, 16, 128], BF16, pes) for _ in range(5)]
                Rw5 = Rw + [Reg() for _ in range(5)]
                NW5 = 8
                w5 = [0]

                def load5(k):
                    s = w5[0] % NW5
                    w5[0] += 1
                    c.load("pool", wblk5[s][:], srcs[k], [Rw5[s]], f"v{s}")
                    return s
                slots = {}
                for k in range(4):
                    slots[k] = load5(k)
                for k in range(len(srcs)):
                    if k + 4 < len(srcs):
                        slots[k + 4] = load5(k + 4)
                    body(k, wblk5[slots[k]], Rw5[slots[k]])
                S.barrier()
            with ExitStack() as pes:
                wo = [c.sb([128, 16, 512], BF16, pes) for _ in range(2)]
                Rwo = [Reg(), Reg()]
                c.load("pool", wo[0][:], wd["wO"][0], [Rwo[0]], "wo0")
                for ob in range(4):
                    if ob + 1 < 4:
                        c.load("pool", wo[(ob + 1) % 2][:], wd["wO"][ob + 1], [Rwo[(ob + 1) % 2]], f"wo{(ob + 1) % 2}")
                    for tt in range(NT):
                        pb, Rp = c.bank()
                        for kc in range(16):
                            c.mm(pb[:], mT[:, kc, tt * 128:(tt + 1) * 128], wo[ob % 2][:, kc, :], kc == 0, kc == 15,
                                 [RmT, Rwo[ob % 2]], [Rp], signal=(kc == 15))
                        c.tt("dve", xs[:, tt, ob * 512:(ob + 1) * 512], xs[:, tt, ob * 512:(ob + 1) * 512], pb[:], ALU.add,
                             [Rp, Rx[tt]], [Rx[tt]])
                S.barrier()

        with ExitStack() as mes:
            uT = c.sb([128, 16, TT], BF16, mes, name="u2T")
            RuT = Reg()
            with ExitStack() as pes:
                do_norm("gain2", uT, RuT, pes)
            S.barrier()
            hT = c.sb([128, 8, TT], BF16, mes, name="hT")
            RhT = Reg()
            wblk = [c.sb([128, 16, 128], BF16, mes) for _ in range(6)]
            Rw = [Reg() for _ in range(6)]
            w2 = [c.sb([128, 8, 512], BF16, mes) for _ in range(3)]
            Rw2 = [Reg(), Reg(), Reg()]
            rl = [c.sb([128, 512], F32, mes) for _ in range(2)]
            Rrl = [Reg(), Reg()]
            w2c = [0]

            def load2(g, ob):
                s = w2c[0] % 3
                w2c[0] += 1
                c.load("pool", w2[s][:], wd["wF2"][g, ob], [Rw2[s]], f"w2{s}")
                return s
            rc = [0]
            for g in range(8):
                srcs = [wd["wF1"][g * 8 + cb] for cb in range(8)]

                def body(k, wb, Rwb):
                    for tb in range(2):
                        tsl = slice(tb * 512, (tb + 1) * 512)
                        pb, Rp = c.bank()
                        for kc in range(16):
                            c.mm(pb[:], wb[:, kc, :], uT[:, kc, tsl], kc == 0, kc == 15, [Rwb, RuT], [Rp], signal=(kc == 15))
                        ri = rc[0] % 2
                        rc[0] += 1
                        c.act(rl[ri][:], pb[:], AF.Relu, [Rp], [Rrl[ri]])
                        c.tt("dve" if ri else "pool", hT[:, k, tsl], rl[ri][:], rl[ri][:], ALU.mult, [Rrl[ri]], [RhT])
                stream_blocks(srcs, wblk, Rw, body)
                q2 = [load2(g, 0), load2(g, 1)]
                for ob in range(4):
                    cur = q2.pop(0)
                    if ob + 2 < 4:
                        q2.append(load2(g, ob + 2))
                    for tt in range(NT):
                        pb, Rp = c.bank()
                        for kc in range(8):
                            c.mm(pb[:], hT[:, kc, tt * 128:(tt + 1) * 128], w2[cur][:, kc, :], kc == 0, kc == 7,
                                 [RhT, Rw2[cur]], [Rp], signal=(kc == 7))
                        c.tt("dve", xs[:, tt, ob * 512:(ob + 1) * 512], xs[:, tt, ob * 512:(ob + 1) * 512], pb[:], ALU.add,
                             [Rp, Rx[tt]], [Rx[tt]])
            S.barrier()

        if final:
            with ExitStack() as pes:
                gain_sb = c.sb([128, D], F32, pes)
                Rg = Reg()
                c.load("sp", gain_sb[:], wd["gainF"], [Rg], "gl")
                sq = c.sb([128, D], F32, pes)
                ss = c.sb([128, 4], F32, pes)
                ob_ = [c.sb([128, D], F32, pes) for _ in range(2)]
                Rob = [Reg(), Reg()]
                Rsq, Rss = Reg(), Reg()
                for tt in range(NT):
                    b = tt % 2
                    c.S.op("act", lambda e, tt=tt: e.activation(out=sq[:], in_=xs[:, tt, :], func=AF.Square, accum_out=ss[:, 0:1]),
                           reads=[Rx[tt]], writes=[Rsq, Rss])
                    c.act(ss[:, 1:2], ss[:, 0:1], AF.Ln, [Rss], [Rss], scale=1.0 / D, bias=EPS)
                    c.act(ss[:, 2:3], ss[:, 1:2], AF.Exp, [Rss], [Rss], scale=-0.5)
                    c.stt("dve", ob_[b][:], xs[:, tt, :], ss[:, 2:3], gain_sb[:], ALU.mult, ALU.mult, [Rx[tt], Rss, Rg], [Rob[b]])
                    c.store("sp", o_d[tt * 128:(tt + 1) * 128, :], ob_[b][:], [Rob[b]], [Ro], f"os{b}")
                S.barrier()
        else:
            for tt in range(NT):
                c.store("sp", o_d[tt * 128:(tt + 1) * 128, :], xs[:, tt, :], [Rx[tt]], [Ro], f"os{tt % 2}")
            S.barrier()


def p2_inputs(inp, l):
    W = inp["w_in"][l]
    wG = np.stack([_blk(W[:, OFF_G + br * D + dc * 128:OFF_G + br * D + (dc + 1) * 128]) for dc in range(16) for br in range(3)])
    Wup = np.concatenate([inp["w_pool_up"][l], inp["w_sb_up"][l], inp["w_gdn_up"][l]], axis=0)
    wUp = np.stack([_blk(Wup[:, dc * 128:(dc + 1) * 128]) for dc in range(16)])
    wO = np.stack([_blk(inp["w_out"][l][:, ob * 512:(ob + 1) * 512]) for ob in range(4)])
    wF1 = np.stack([_blk(inp["w_ff1"][l][:, cb * 128:(cb + 1) * 128]) for cb in range(64)])
    W2 = inp["w_ff2"][l]
    wF2 = np.stack([np.stack([np.ascontiguousarray(
        W2[g * 1024:(g + 1) * 1024, ob * 512:(ob + 1) * 512].reshape(8, 128, 512).transpose(1, 0, 2)) for ob in range(4)])
        for g in range(8)])
    bc = lambda v: np.ascontiguousarray(np.broadcast_to(v[None, :], (128, D)))
    return dict(gain1=bc(inp["attn_norm"][l]), gain2=bc(inp["mlp_norm"][l]), gainF=bc(inp["final_norm"]),
                ident=np.eye(128, dtype=np.float32), wG=wG, wUp=wUp, wO=wO, wF1=wF1, wF2=wF2)


def kernel_unfused(**inputs):
    inp = {k: np.asarray(v) for k, v in inputs.items()}
    x = np.ascontiguousarray(inp["x"], dtype=np.float32)
    consts = p1_consts()
    cores = list(range(8))
    depth = inp["w_in"].shape[0]
    for l in range(depth):
        p1h = [p1_inputs(inp, l, hh, consts) for hh in range(2)]
        in_maps = []
        for core in cores:
            m = dict(p1h[core % 2])
            m["x"] = np.ascontiguousarray(x[core // 2])
            in_maps.append(m)
        res = run_bass_kernel_spmd(build_p1(), in_maps, core_ids=cores)
        ys = [np.asarray(res.results[core]["y"]) for core in cores]
        del in_maps, p1h
        base = p2_inputs(inp, l)
        in_maps = []
        for core in cores:
            b, half = core // 2, core % 2
            y0, y1 = ys[2 * b], ys[2 * b + 1]
            tsl = slice(half * TT, (half + 1) * TT)
            yT = np.concatenate([y0[0:512, tsl], y0[512:896, tsl], y1[512:896, tsl], y0[896:1280, tsl], y1[896:1280, tsl]], axis=0)
            m = dict(base)
            m["x"] = np.ascontiguousarray(x[b, tsl, :])
            m["yT"] = np.ascontiguousarray(yT)
            in_maps.append(m)
        res = run_bass_kernel_spmd(build_p2(l == depth - 1), in_maps, core_ids=cores)
        x = np.stack([np.asarray(res.results[core]["o"]) for core in cores]).reshape(NB, SEQ, D)
        del in_maps, base
    return np.ascontiguousarray(x, dtype=np.float32)


P1_KEYS = ("gain", "wF", "wV", "wAB", "poolw", "poolc", "convw", "gsc", "gn")
P1_SHAPES = dict(gain=[128, D], wF=[NFB, 128, 16, 128], wV=[128, 16, 384], wAB=[128, 16, 6], poolw=[128, 4, 128],
                 poolc=[128, 4, 18], convw=[128, 9, 4], gsc=[64, 3, 32, 3], gn=[128, 1])
P2_SHAPES = dict(gain1=[128, D], gain2=[128, D], wG=[48, 128, 16, 128], wUp=[16, 128, 16, 128], wO=[4, 128, 16, 512],
                 wF1=[64, 128, 16, 128], wF2=[8, 4, 128, 8, 512])


def build_fused(depth):
    nc = bass.Bass("TRN2", target_bir_lowering=False)
    dt = lambda name, shape, kind="ExternalInput", dty=F32: nc.dram_tensor(name, shape, dty, kind=kind).ap()
    x_d = dt("x", [SEQ, D])
    o_d = dt("o", [SEQ, D], kind="ExternalOutput")
    shared = dict(cst=dt("cst", [128, 5, 128]), mb=dt("mb", [128, 4, 512]), c64=dt("c64", [64, 5, 64]),
                  ident=dt("ident", [128, 128]), gainF=dt("gainF", [128, D]))
    xs = dt("xs_scratch", [SEQ, D], kind="Internal")
    ys = [dt(f"ys_scratch{hh}", [1280, SEQ], kind="Internal", dty=BF16) for hh in range(2)]
    w1 = {}
    w2 = {}
    for l in range(depth):
        for hh in range(2):
            d1 = {k: dt(f"{k}_{l}_{hh}", P1_SHAPES[k]) for k in P1_KEYS}
            d1.update(cst=shared["cst"], mb=shared["mb"], c64=shared["c64"])
            w1[(l, hh)] = d1
        d2 = {k: dt(f"{k}_{l}", P2_SHAPES[k]) for k in P2_SHAPES}
        d2.update(ident=shared["ident"], gainF=shared["gainF"])
        w2[l] = d2

    def yrows_for(t):
        tsl = slice(t * TT, (t + 1) * TT)

        def yrows(kc):
            if kc < 4:
                return ys[0][kc * 128:(kc + 1) * 128, tsl]
            if kc < 7:
                return ys[0][512 + (kc - 4) * 128:512 + (kc - 3) * 128, tsl]
            if kc < 10:
                return ys[1][512 + (kc - 7) * 128:512 + (kc - 6) * 128, tsl]
            if kc < 13:
                return ys[0][896 + (kc - 10) * 128:896 + (kc - 9) * 128, tsl]
            return ys[1][896 + (kc - 13) * 128:896 + (kc - 12) * 128, tsl]
        return yrows

    with ExitStack() as es:
        S = Sched(nc, es)
        c = Ctx(nc, S, es)
        Ro = Reg()
        for l in range(depth):
            src = x_d if l == 0 else xs
            last = l == depth - 1
            dst = o_d if last else xs
            Ry = Reg()
            emit_p1(c, S, es, src, ys, [w1[(l, 0)], w1[(l, 1)]], Ry)
            S.barrier()
            for t in range(2):
                emit_p2(c, S, src[t * TT:(t + 1) * TT, :], yrows_for(t), dst[t * TT:(t + 1) * TT, :], w2[l], Ro, last)
                S.barrier()
        S.barrier()
        S.wait_all("sp", [Ro])
        print("sem counts", S.count, max(S.dcount.values()))
    return nc


def kernel(**inputs):
    inp = {k: np.asarray(v) for k, v in inputs.items()}
    x = np.ascontiguousarray(inp["x"], dtype=np.float32)
    depth = inp["w_in"].shape[0]
    cst, mb, c64 = p1_consts()
    base = dict(cst=cst, mb=mb, c64=c64, ident=np.eye(128, dtype=np.float32))
    for l in range(depth):
        for hh in range(2):
            m = p1_inputs(inp, l, hh, (cst, mb, c64))
            for k in P1_KEYS:
                base[f"{k}_{l}_{hh}"] = m[k]
        m = p2_inputs(inp, l)
        for k in P2_SHAPES:
            base[f"{k}_{l}"] = m[k]
        base["gainF"] = m["gainF"]
    cores = list(range(NB))
    in_maps = []
    for b in cores:
        m = dict(base)
        m["x"] = np.ascontiguousarray(x[b])
        in_maps.append(m)
    res = run_bass_kernel_spmd(build_fused(depth), in_maps, core_ids=cores)
    out = np.stack([np.asarray(res.results[b]["o"]) for b in cores])
    return np.ascontiguousarray(out, dtype=np.float32)
```

```python
import numpy as np
from contextlib import ExitStack
import concourse.bass as bass
import concourse.mybir as mybir
from concourse.bass_utils import run_bass_kernel_spmd

F32 = mybir.dt.float32
BF16 = mybir.dt.bfloat16
AF = mybir.ActivationFunctionType
ALU = mybir.AluOpType
AX = mybir.AxisListType

D = 2048
SEQ = 2048
NB = 4
DFF = 8192
EPS = 1e-6
NEG = -30000.0
SAME_ENGINE_SYNC = True
ENGMAP = {"pe": "tensor", "act": "scalar", "dve": "vector", "pool": "gpsimd", "sp": "sync"}


class Reg:
    __slots__ = ("w", "r")

    def __init__(self):
        self.w = {}
        self.r = {}


class Sched:
    ENGS = ("pe", "act", "dve", "pool", "sp")

    def __init__(self, nc, es):
        self.nc = nc
        self.es = es
        self.count = {e: 0 for e in self.ENGS}
        self.waited = {e: {} for e in self.ENGS}
        self.sems = {}
        self.dcount = {}
        for e in self.ENGS:
            self.sems[e] = es.enter_context(nc.semaphore("s_" + e))

    def dsem(self, name):
        if name not in self.sems:
            self.sems[name] = self.es.enter_context(self.nc.semaphore("d_" + name))
            self.dcount[name] = 0
        return self.sems[name]

    def _waits(self, eng, reads, writes):
        need = {}
        for r in reads:
            for s, v in r.w.items():
                if v > need.get(s, 0):
                    need[s] = v
        for w in writes:
            for s, v in w.w.items():
                if v > need.get(s, 0):
                    need[s] = v
            for s, v in w.r.items():
                if v > need.get(s, 0):
                    need[s] = v
        out = []
        wd = self.waited[eng]
        for s, v in need.items():
            if s == eng and (eng == "pe" or not SAME_ENGINE_SYNC):
                continue
            if s in self.count:
                assert v <= self.count[s], f"wait on unsignaled op of {s}: {v} > {self.count[s]}"
            if wd.get(s, 0) >= v:
                continue
            wd[s] = v
            out.append((s, v))
        return out

    def _emit1(self, e, waits, fn, sig):
        eng = getattr(self.nc, ENGMAP[e])
        for s, v in waits:
            eng.wait_ge(self.sems[s], v)
        if fn is not None:
            ins = fn(eng)
            if sig is not None:
                ins.then_inc(self.sems[sig[0]], sig[1])

    def op(self, eng, fn, reads=(), writes=(), signal=True):
        waits = self._waits(eng, reads, writes)
        val = self.count[eng] + 1
        if signal:
            self.count[eng] = val
        self._emit1(eng, waits, fn, (eng, 1) if signal else None)
        for w in writes:
            w.w = {eng: val}
            w.r = {}
        for r in reads:
            if r.r.get(eng, 0) < val:
                r.r[eng] = val

    def dma(self, eng, fn, reads=(), writes=(), sem="dma"):
        self.dsem(sem)
        waits = self._waits(eng, reads, writes)
        self.dcount[sem] += 16
        val = self.dcount[sem]
        self._emit1(eng, waits, fn, (sem, 16))
        for w in writes:
            w.w = {sem: val}
            w.r = {}
        for r in reads:
            if r.r.get(sem, 0) < val:
                r.r[sem] = val

    def wait_all(self, eng, regs):
        waits = self._waits(eng, regs, ())
        self._emit1(eng, waits, None, None)

    def barrier(self):
        allv = dict(self.count)
        allv.update(self.dcount)
        for e in self.ENGS:
            waits = []
            for s, v in allv.items():
                if s == e or v == 0:
                    continue
                if self.waited[e].get(s, 0) >= v:
                    continue
                self.waited[e][s] = v
                waits.append((s, v))
            self._emit1(e, waits, None, None)


class Ctx:
    def __init__(self, nc, S, es):
        self.nc, self.S, self.es = nc, S, es
        self.pb = []
        self.Rpb = []
        for i in range(8):
            self.pb.append(es.enter_context(nc.psum_tensor(f"pb{i}", [128, 512], F32)))
            self.Rpb.append(Reg())
        self.pbi = 0
        self.n = 0

    def sb(self, shape, dt, es=None, name=None):
        self.n += 1
        return (es or self.es).enter_context(self.nc.sbuf_tensor((name or "t") + f"_{self.n}", shape, dt))

    def bank(self, lo=0, hi=8):
        i = lo + (self.pbi % (hi - lo))
        self.pbi += 1
        return self.pb[i], self.Rpb[i]

    def mm(self, out, lhsT, rhs, start, stop, reads, writes, signal=True):
        self.S.op("pe", lambda e: e.matmul(out, lhsT=lhsT, rhs=rhs, start=start, stop=stop),
                  reads=reads, writes=writes, signal=signal)

    def tr(self, out, in_, ident, reads, writes, signal=True):
        self.S.op("pe", lambda e: e.transpose(out, in_, ident), reads=reads, writes=writes, signal=signal)

    def act(self, out, in_, func, reads, writes, eng="act", **kw):
        self.S.op("act", lambda e: e.activation(out=out, in_=in_, func=func, **kw), reads=reads, writes=writes)

    def tt(self, eng, out, in0, in1, op, reads, writes):
        self.S.op(eng, lambda e: e.tensor_tensor(out=out, in0=in0, in1=in1, op=op), reads=reads, writes=writes)

    def ts(self, eng, out, in0, s1, s2, op0, op1, reads, writes):
        if op1 is None:
            self.S.op(eng, lambda e: e.tensor_scalar(out=out, in0=in0, scalar1=s1, scalar2=None, op0=op0),
                      reads=reads, writes=writes)
        else:
            self.S.op(eng, lambda e: e.tensor_scalar(out=out, in0=in0, scalar1=s1, scalar2=s2, op0=op0, op1=op1),
                      reads=reads, writes=writes)

    def stt(self, eng, out, in0, scalar, in1, op0, op1, reads, writes):
        self.S.op(eng, lambda e: e.scalar_tensor_tensor(out=out, in0=in0, scalar=scalar, in1=in1, op0=op0, op1=op1),
                  reads=reads, writes=writes)

    def cp(self, eng, out, in_, reads, writes):
        if eng == "act":
            self.S.op("act", lambda e: e.activation(out=out, in_=in_, func=AF.Copy), reads=reads, writes=writes)
        else:
            self.S.op(eng, lambda e: e.tensor_copy(out=out, in_=in_), reads=reads, writes=writes)

    def load(self, q, out, in_, writes, sem):
        self.S.dma(q, lambda e: e.dma_start(out=out, in_=in_), writes=writes, sem=sem)

    def store(self, q, out, in_, reads, writes, sem):
        self.S.dma(q, lambda e: e.dma_start(out=out, in_=in_), reads=reads, writes=writes, sem=sem)


def norm_transpose(c, xt_ap, Rxt, gain_sb, Rgain, identf, Rid, uT, RuT, tok0, scr, banks=(0, 8)):
    sq, Rsq, ss, Rss, u32, Ru32 = scr
    c.S.op("act", lambda e: e.activation(out=sq[:], in_=xt_ap, func=AF.Square, accum_out=ss[:, 0:1]),
           reads=[Rxt], writes=[Rsq, Rss])
    c.act(ss[:, 1:2], ss[:, 0:1], AF.Ln, [Rss], [Rss], scale=1.0 / D, bias=EPS)
    c.act(ss[:, 2:3], ss[:, 1:2], AF.Exp, [Rss], [Rss], scale=-0.5)
    c.stt("dve", u32[:], xt_ap, ss[:, 2:3], gain_sb[:], ALU.mult, ALU.mult, [Rxt, Rss, Rgain], [Ru32])
    for j in range(4):
        pb, Rp = c.bank(*banks)
        for i in range(4):
            kc = 4 * j + i
            c.tr(pb[:, i * 128:(i + 1) * 128], u32[:, kc * 128:(kc + 1) * 128], identf[:],
                 [Ru32, Rid], [Rp], signal=(i == 3))
        eng = "act" if j % 2 == 0 else "dve"
        c.cp(eng, uT[:, 4 * j:4 * j + 4, tok0:tok0 + 128],
             pb[:].rearrange("p (a b) -> p a b", a=4), [Rp], [RuT])


NFB = 22


def build_p1():
    nc = bass.Bass("TRN2", target_bir_lowering=False)
    dt = lambda name, shape, kind="ExternalInput", dty=F32: nc.dram_tensor(name, shape, dty, kind=kind).ap()
    x_d = dt("x", [SEQ, D])
    y_d = dt("y", [1280, SEQ], kind="ExternalOutput", dty=BF16)
    wd = dict(
        gain=dt("gain", [128, D]), wF=dt("wF", [NFB, 128, 16, 128]), wV=dt("wV", [128, 16, 384]),
        wAB=dt("wAB", [128, 16, 6]), cst=dt("cst", [128, 5, 128]), mb=dt("mb", [128, 4, 512]),
        c64=dt("c64", [64, 5, 64]), poolw=dt("poolw", [128, 4, 128]), poolc=dt("poolc", [128, 4, 18]),
        convw=dt("convw", [128, 9, 4]), gsc=dt("gsc", [64, 3, 32, 3]), gn=dt("gn", [128, 1]))
    with ExitStack() as es:
        S = Sched(nc, es)
        c = Ctx(nc, S, es)
        Ry = Reg()
        emit_p1(c, S, es, x_d, [y_d], [wd], Ry)
        S.barrier()
        S.wait_all("sp", [Ry])
    return nc


class _Stop(Exception):
    pass


def emit_p1(c, S, es0, x_d, y_ds, wds, Ry):
    try:
        _emit_p1(c, S, es0, x_d, y_ds, wds, Ry)
    except _Stop:
        S.barrier()


def _chk(c, name):
    import os
    if os.environ.get("P1_STOP") == name:
        c.S.barrier()
        raise _Stop()


def _emit_p1(c, S, es0, x_d, y_ds, wds, Ry):
    wd = wds[0]
    with ExitStack() as es:
        cstf = c.sb([128, 5, 128], F32, es)
        cstb = c.sb([128, 5, 128], BF16, es)
        c64f = c.sb([64, 5, 64], F32, es)
        poolw = c.sb([128, 4, 128], BF16, es)
        poolc = c.sb([128, 4, 18], F32, es)
        convw = c.sb([128, 9, 4], F32, es)
        gsc = c.sb([64, 3, 32, 3], F32, es)
        gn = c.sb([128, 1], F32, es)
        Rc = Reg()
        c.load("sp", cstf[:], wd["cst"], [Rc], "c0")
        c.load("pool", cstb[:], wd["cst"], [Rc], "c1")
        c.load("sp", c64f[:], wd["c64"], [Rc], "c3")
        c.load("pool", poolw[:], wd["poolw"], [Rc], "c5")
        c.load("sp", poolc[:], wd["poolc"], [Rc], "c6")
        S.barrier()
        identf = cstf[:, 0, :]
        identb = cstb[:, 0, :]
        TIb = cstb[:, 1, :]
        TSb = cstb[:, 2, :]
        onesb = cstb[:, 3, :]
        Rid = Rc

        uT = c.sb([128, 16, SEQ], BF16, es, name="uT")
        RuT = Reg()

        with ExitStack() as pes:
            gain_sb = c.sb([128, D], F32, pes)
            c.load("sp", gain_sb[:], wd["gain"], [Rc], "c10")
            S.barrier()
            xt = [c.sb([128, D], F32, pes) for _ in range(2)]
            Rxt = [Reg(), Reg()]
            scrs = []
            for _ in range(2):
                sq = c.sb([128, D], BF16, pes)
                ss = c.sb([128, 4], F32, pes)
                u32 = c.sb([128, D], F32, pes)
                scrs.append((sq, Reg(), ss, Reg(), u32, Reg()))
            RuTs = [Reg() for _ in range(16)]
            for tt in range(16):
                b = tt % 2
                c.load("sp", xt[b][:], x_d[tt * 128:(tt + 1) * 128, :], [Rxt[b]], f"x{b}")
                norm_transpose(c, xt[b][:], Rxt[b], gain_sb, Rc, identf, Rid, uT, RuTs[tt], tt * 128, scrs[b])
            S.barrier()

        _chk(c, "A")
        NWB = 3
        wcount = [0]

        def proj_blocks(blocks, wblk, Rw, post):
            def load_w(k):
                s = wcount[0] % NWB
                wcount[0] += 1
                c.load("pool", wblk[s][:], wF_d[blocks[k]], [Rw[s]], f"w{s}")
                return s
            slots = {}
            for k in range(min(2, len(blocks))):
                slots[k] = load_w(k)
            for k, i in enumerate(blocks):
                if k + 2 < len(blocks):
                    slots[k + 2] = load_w(k + 2)
                s = slots[k]
                for tb in range(4):
                    pb, Rp = c.bank()
                    for kc in range(16):
                        c.mm(pb[:], wblk[s][:, kc, :], uT[:, kc, tb * 512:(tb + 1) * 512], kc == 0, kc == 15,
                             [Rw[s], RuT], [Rp], signal=(kc == 15))
                    post(i, tb, pb, Rp)
                post(i, None, None, None)

        for hi, wd in enumerate(wds):
            y_d = y_ds[hi]
            wF_d, wV_d, wAB_d = wd["wF"], wd["wV"], wd["wAB"]
            Rc2 = Reg()
            c.load("sp", convw[:], wd["convw"], [Rc2], "c7")
            c.load("sp", gsc[:], wd["gsc"], [Rc2], "c8")
            c.load("sp", gn[:], wd["gn"], [Rc2], "c9")
            S.barrier()
            with ExitStack() as ses:
                sQT = c.sb([128, 3, SEQ], BF16, ses, name="sQT")
                sKT = c.sb([128, 3, SEQ], BF16, ses, name="sKT")
                sV = c.sb([128, 16, 384], BF16, ses, name="sV")
                mbb = c.sb([128, 4, 512], BF16, ses)
                Rmb = Reg()
                c.load("pool", mbb[:], wd["mb"], [Rmb], "c2")
                RsQ = [Reg() for _ in range(3)]
                RsK = [Reg() for _ in range(3)]
                RsV = Reg()
                with ExitStack() as pes:
                    wblk = [c.sb([128, 16, 128], BF16, pes) for _ in range(NWB)]
                    Rw = [Reg() for _ in range(NWB)]
                    wV = c.sb([128, 16, 384], BF16, pes)
                    RwV = Reg()
                    c.load("pool", wV[:], wV_d, [RwV], "wv")
                    big = c.sb([128, 16 + SEQ], F32, pes, name="big")
                    Rbig = Reg()
                    t1 = c.sb([128, SEQ], F32, pes, name="t1")
                    t2 = c.sb([128, SEQ], F32, pes, name="t2")
                    Rt1, Rt2 = Reg(), Reg()
                    tb16 = c.sb([128, SEQ], BF16, pes, name="tb16")
                    Rtb = Reg()
                    yst = c.sb([128, SEQ], BF16, pes, name="yst")
                    Ryst = Reg()
                    c.S.op("dve", lambda e: e.memset(big[:, 0:16], 0.0), writes=[Rbig])

                    def post_sb(i, tb, pb, Rp):
                        if tb is not None:
                            tsl = slice(tb * 512, (tb + 1) * 512)
                            if i < 4:
                                c.cp("act" if tb % 2 == 0 else "dve", big[:, 16 + tb * 512:16 + (tb + 1) * 512], pb[:], [Rp], [Rbig])
                            elif i < 7:
                                c.cp("act", sQT[:, i - 4, tsl], pb[:], [Rp], [RsQ[i - 4]])
                            else:
                                c.cp("act", sKT[:, i - 7, tsl], pb[:], [Rp], [RsK[i - 7]])
                            return
                        if i >= 4:
                            return
                        g = i
                        p = big[:, 16:16 + SEQ]
                        bufs = [t1, t2]
                        Rb = [Rt1, Rt2]
                        sh = 1
                        for lv in range(g + 1):
                            o = bufs[lv % 2]
                            if lv == 0:
                                c.tt("dve", o[:], p, big[:, 15:15 + SEQ], ALU.add, [Rbig], [Rb[0]])
                            else:
                                prev = bufs[(lv - 1) % 2]
                                c.tt("dve", o[:, sh:], prev[:, sh:], prev[:, 0:SEQ - sh], ALU.add, [Rb[(lv - 1) % 2]], [Rb[lv % 2]])
                                c.cp("dve", o[:, 0:sh], prev[:, 0:sh], [Rb[(lv - 1) % 2]], [Rb[lv % 2]])
                            sh *= 2
                        sw, Rsw = bufs[g % 2], Rb[g % 2]
                        dd, Rdd = bufs[(g + 1) % 2], Rb[(g + 1) % 2]
                        c.stt("dve", dd[:, 16:], sw[:, 16:], poolc[:, g, 1:2], big[:, 32:16 + SEQ], ALU.mult, ALU.subtract,
                              [Rsw, Rbig, Rc], [Rdd])
                        c.tt("dve", dd[:, 0:16], sw[:, 0:16], poolc[:, g, 2:18], ALU.mult, [Rsw, Rc], [Rdd])
                        c.tt("dve", dd[:, 0:16], dd[:, 0:16], big[:, 16:32], ALU.subtract, [Rdd, Rbig], [Rdd])
                        c.cp("act", tb16[:], dd[:], [Rdd], [Rtb])
                        for tb2 in range(4):
                            pb2, Rp2 = c.bank()
                            c.mm(pb2[:], poolw[:, g, :], tb16[:, tb2 * 512:(tb2 + 1) * 512], True, True, [Rtb, Rc], [Rp2])
                            c.ts("dve", yst[:, tb2 * 512:(tb2 + 1) * 512], pb2[:], poolc[:, g, 0:1], None, ALU.mult, None,
                                 [Rp2, Rc], [Ryst])
                        c.store("sp", y_d[g * 128:(g + 1) * 128, :], yst[:], [Ryst], [Ry], "yo")

                    proj_blocks(list(range(10)) if hi == 0 else list(range(4, 10)), wblk, Rw, post_sb)
                    _chk(c, "SBP")
                    for tt in range(16):
                        pb, Rp = c.bank()
                        for kc in range(16):
                            c.mm(pb[:, 0:384], uT[:, kc, tt * 128:(tt + 1) * 128], wV[:, kc, :], kc == 0, kc == 15,
                                 [RwV, RuT], [Rp], signal=(kc == 15))
                        c.cp("act" if tt % 2 else "dve", sV[:, tt, :], pb[:, 0:384], [Rp], [RsV])
                    S.barrier()

                _chk(c, "SBV")
                with ExitStack() as pes:
                    NE = 6
                    Eb = [c.sb([128, 512], F32, pes) for _ in range(NE)]
                    SPb = [c.sb([128, 512], BF16, pes) for _ in range(NE)]
                    Xb = [c.sb([128, 512], F32, pes) for _ in range(NE)]
                    Ab = [c.sb([128, 512], BF16, pes) for _ in range(NE)]
                    RE = [Reg() for _ in range(NE)]
                    RSP = [Reg() for _ in range(NE)]
                    RX = [Reg() for _ in range(NE)]
                    RA = [Reg() for _ in range(NE)]
                    ost = [c.sb([128, 512], BF16, pes) for _ in range(2)]
                    Rost = [Reg(), Reg()]
                    scale = float(128 ** -0.5)
                    items = []
                    for QB in range(4):
                        for kb in range(4 * QB + 3, -1, -1):
                            for h in range(3):
                                items.append((QB, kb, h))
                    n_it = len(items)
                    oc = [0]

                    def stage0(idx):
                        QB, kb, h = items[idx]
                        s = idx % NE
                        zb, Rz = c.pb[idx % 2], c.Rpb[idx % 2]
                        q0 = QB * 512
                        diag = kb >= 4 * QB
                        c.mm(zb[:], sKT[:, h, kb * 128:(kb + 1) * 128], sQT[:, h, q0:q0 + 512], True, not diag,
                             [RsK[h], RsQ[h]], [Rz], signal=not diag)
                        if diag:
                            c.mm(zb[:], identb, mbb[:, kb - 4 * QB, :], False, True, [Rc, Rmb], [Rz])
                        c.act(Eb[s][:], zb[:], AF.Exp, [Rz], [RE[s]], scale=scale)
                        c.act(SPb[s][:], Eb[s][:], AF.Ln, [RE[s]], [RSP[s]], bias=1.0)

                    def stage1(idx):
                        QB, kb, h = items[idx]
                        s = idx % NE
                        cb, Rcb = c.pb[2 + h], c.Rpb[2 + h]
                        first = kb == 4 * QB + 3
                        c.mm(cb[:], TIb, SPb[s][:], first, True, [Rc, RSP[s]], [Rcb])
                        c.act(Xb[s][:], cb[:], AF.Exp, [Rcb], [RX[s]], scale=-1.0)

                    def stage2(idx):
                        QB, kb, h = items[idx]
                        s = idx % NE
                        cb, Rcb = c.pb[2 + h], c.Rpb[2 + h]
                        c.mm(cb[:], TSb, SPb[s][:], False, True, [Rc, RSP[s]], [Rcb])
                        c.tt("dve" if idx % 2 else "pool", Ab[s][:], Eb[s][:], Xb[s][:], ALU.mult, [RE[s], RX[s]], [RA[s]])

                    def stage3(idx):
                        QB, kb, h = items[idx]
                        s = idx % NE
                        ob, Rob = c.pb[5 + h], c.Rpb[5 + h]
                        first = kb == 4 * QB + 3
                        c.mm(ob[:], sV[:, kb, h * 128:(h + 1) * 128], Ab[s][:], first, kb == 0, [RsV, RA[s]], [Rob])
                        if kb == 0:
                            o = oc[0] % 2
                            oc[0] += 1
                            c.cp("dve", ost[o][:], ob[:], [Rob], [Rost[o]])
                            c.store("sp", y_d[512 + h * 128:512 + (h + 1) * 128, QB * 512:(QB + 1) * 512], ost[o][:],
                                    [Rost[o]], [Ry], f"so{o}")

                    for idx in range(n_it + 3):
                        if idx < n_it:
                            stage0(idx)
                        if 1 <= idx < n_it + 1:
                            stage1(idx - 1)
                        if 2 <= idx < n_it + 2:
                            stage2(idx - 2)
                        if 3 <= idx:
                            stage3(idx - 3)
                    S.barrier()

            _chk(c, "SB")
            with ExitStack() as ges:
                tri64 = c64f[:, 0, :]
                id64f = c64f[:, 3, :]
                onesf128 = cstf[0:64, 3, :]
                abc = c.sb([64, 32, 6], F32, ges, name="abc")
                Rabc = Reg()
                la = c.sb([64, 32, 3], F32, ges)
                beta = c.sb([64, 32, 3], F32, ges)
                gcol = c.sb([64, 32, 3], F32, ges)
                glast = c.sb([64, 32, 3], F32, ges)
                egcol = c.sb([64, 32, 3], F32, ges)
                sc_kbg = c.sb([64, 32, 3], F32, ges)
                sc_kdec = c.sb([64, 32, 3], F32, ges)
                decB = c.sb([128, 32, 3], F32, ges)
                tmp = c.sb([64, 32, 3], F32, ges)
                nA = c.sb([64, 32, 3], F32, ges)
                Rs = Reg()
                with ExitStack() as pes:
                    wAB = c.sb([128, 16, 6], BF16, pes)
                    RwAB = Reg()
                    c.load("pool", wAB[:], wAB_d, [RwAB], "wv")
                    pb, Rp = c.bank()
                    for n in range(32):
                        for kc in range(16):
                            c.mm(pb[0:64, n * 6:(n + 1) * 6], uT[:, kc, n * 64:(n + 1) * 64], wAB[:, kc, :], kc == 0, kc == 15,
                                 [RwAB, RuT], [Rp], signal=(kc == 15 and n == 31))
                    c.cp("dve", abc[:].rearrange("p a b -> p (a b)"), pb[0:64, 0:192], [Rp], [Rabc])
                    a_ap = abc[:, :, 0:3]
                    b_ap = abc[:, :, 3:6]
                    c.tt("dve", tmp[:], a_ap, gsc[:, 0, :, :], ALU.add, [Rabc, Rc], [Rs])
                    c.act(tmp[:], tmp[:], AF.Exp, [Rs], [Rs])
                    c.act(tmp[:], tmp[:], AF.Ln, [Rs], [Rs], bias=1.0)
                    c.act(nA[:], gsc[:, 1, :, :], AF.Exp, [Rc, Rs], [Rs])
                    c.stt("dve", la[:], tmp[:], -1.0, nA[:], ALU.mult, ALU.mult, [Rs], [Rs])
                    c.act(tmp[:], b_ap, AF.Exp, [Rabc, Rs], [Rs], scale=-1.0)
                    c.ts("dve", tmp[:], tmp[:], 1.0, None, ALU.add, None, [Rs], [Rs])
                    c.S.op("dve", lambda e: e.reciprocal(out=beta[:], in_=tmp[:]), reads=[Rs], writes=[Rs])
                    la2 = la[:].rearrange("p a b -> p (a b)")
                    pb, Rp = c.bank()
                    c.mm(pb[0:64, 0:96], tri64, la2, True, True, [Rc, Rs], [Rp])
                    c.cp("dve", gcol[:].rearrange("p a b -> p (a b)"), pb[0:64, 0:96], [Rp], [Rs])
                    pb, Rp = c.bank()
                    c.mm(pb[:, 0:96], onesf128, la2, True, True, [Rc, Rs], [Rp])
                    c.cp("dve", glast[:].rearrange("p a b -> p (a b)"), pb[0:64, 0:96], [Rp], [Rs])
                    c.act(decB[:].rearrange("p a b -> p (a b)"), pb[:, 0:96], AF.Exp, [Rp, Rs], [Rs])
                    c.act(egcol[:], gcol[:], AF.Exp, [Rs], [Rs])
                    c.tt("dve", sc_kbg[:], egcol[:], beta[:], ALU.mult, [Rs], [Rs])
                    c.tt("dve", tmp[:], glast[:], gcol[:], ALU.subtract, [Rs], [Rs])
                    c.act(sc_kdec[:], tmp[:], AF.Exp, [Rs], [Rs])
                    S.barrier()

                _chk(c, "GAB")
                for h in range(3):
                    with ExitStack() as hes:
                        gQT = c.sb([128, SEQ], BF16, hes, name=f"gQT{h}")
                        gKT = c.sb([128, SEQ], BF16, hes, name=f"gKT{h}")
                        gKtok = c.sb([64, 32, 128], BF16, hes, name=f"gKtok{h}")
                        gVtok = c.sb([64, 32, 128], BF16, hes, name=f"gVtok{h}")
                        gZ = c.sb([128, SEQ], BF16, hes, name=f"gZ{h}")
                        P = c.sb([64, 32, 64], BF16, hes, name=f"gP{h}")
                        vb = c.sb([64, 32, 128], BF16, hes, name=f"gvb{h}")
                        kdec = c.sb([64, 32, 128], BF16, hes, name=f"gkdec{h}")
                        nwT = c.sb([128, SEQ], BF16, hes, name=f"gnwT{h}")
                        qdT = c.sb([128, SEQ], BF16, hes, name=f"gqdT{h}")
                        QKg = c.sb([64, 32, 64], BF16, hes, name=f"gQKg{h}")
                        RgQ, RgK, RgKt, RgVt, RgZ, RP, Rvb, Rkd, Rnw, Rqd, Rqk = [Reg() for _ in range(11)]
                        with ExitStack() as pes:
                            wblk = [c.sb([128, 16, 128], BF16, pes) for _ in range(NWB)]
                            Rw = [Reg() for _ in range(NWB)]
                            big = c.sb([128, 16 + SEQ], F32, pes, name=f"gbig{h}")
                            Rbig = Reg()
                            t1 = c.sb([128, SEQ], F32, pes, name=f"gt1{h}")
                            t2 = c.sb([128, SEQ], F32, pes, name=f"gt2{h}")
                            Rt1, Rt2 = Reg(), Reg()
                            tb16 = c.sb([128, SEQ], BF16, pes, name=f"gtb16{h}")
                            Rtb = Reg()
                            c.S.op("dve", lambda e: e.memset(big[:, 0:16], 0.0), writes=[Rbig])

                            def post_g(i, tb, pb, Rp):
                                kind = (i - 10) // 3
                                if tb is not None:
                                    tsl = slice(tb * 512, (tb + 1) * 512)
                                    if kind < 3:
                                        c.cp("act" if tb % 2 == 0 else "dve", big[:, 16 + tb * 512:16 + (tb + 1) * 512], pb[:],
                                             [Rp], [Rbig])
                                    else:
                                        c.act(gZ[:, tsl], pb[:], AF.Silu, [Rp], [RgZ])
                                    return
                                if kind == 3:
                                    return
                                j = kind * 3 + h
                                c.ts("dve", t1[:], big[:, 13:13 + SEQ], convw[:, j, 0:1], None, ALU.mult, None, [Rbig, Rc], [Rt1])
                                for ci in range(1, 4):
                                    c.stt("dve", t1[:], big[:, 13 + ci:13 + ci + SEQ], convw[:, j, ci:ci + 1], t1[:],
                                          ALU.mult, ALU.add, [Rbig, Rc, Rt1], [Rt1])
                                c.act(t2[:], t1[:], AF.Silu, [Rt1], [Rt2])
                                if kind < 2:
                                    c.act(tb16[:], t2[:], AF.Square, [Rt2], [Rtb])
                                    for tb2 in range(4):
                                        pb2, Rp2 = c.bank()
                                        c.mm(pb2[:], onesb, tb16[:, tb2 * 512:(tb2 + 1) * 512], True, True, [Rtb, Rc], [Rp2])
                                        c.act(t1[:, tb2 * 512:(tb2 + 1) * 512], pb2[:], AF.Ln, [Rp2], [Rt1], bias=EPS)
                                    c.act(t1[:], t1[:], AF.Exp, [Rt1], [Rt1], scale=-0.5)
                                    if kind == 0:
                                        c.stt("dve", gQT[:], t2[:], float(128 ** -0.5), t1[:], ALU.mult, ALU.mult, [Rt1, Rt2], [RgQ])
                                    else:
                                        c.tt("dve", t2[:], t2[:], t1[:], ALU.mult, [Rt1, Rt2], [Rt2])
                                        c.cp("act", gKT[:], t2[:], [Rt2], [RgK])
                                if kind >= 1:
                                    dst, Rdst = (gKtok, RgKt) if kind == 1 else (gVtok, RgVt)
                                    for grp in range(8):
                                        pb2, Rp2 = c.bank()
                                        for q in range(4):
                                            n = grp * 4 + q
                                            c.tr(pb2[0:64, q * 128:(q + 1) * 128], t2[:, n * 64:(n + 1) * 64], identf, [Rt2, Rid],
                                                 [Rp2], signal=(q == 3))
                                        c.cp("act" if grp % 2 else "dve", dst[:, grp * 4:grp * 4 + 4, :],
                                             pb2[0:64, :].rearrange("p (a b) -> p a b", a=4), [Rp2], [Rdst])

                            proj_blocks([10 + h, 13 + h, 16 + h, 19 + h], wblk, Rw, post_g)
                            S.barrier()

                        _chk(c, "GP")
                        with ExitStack() as pes:
                            labt = c.sb([64, 32, 64], F32, pes)
                            Dm = c.sb([64, 32, 64], F32, pes)
                            gamL = c.sb([64, 32, 64], F32, pes)
                            egB = c.sb([128, SEQ], F32, pes)
                            kbg = c.sb([64, 32, 128], BF16, pes)
                            Mk = [c.sb([64, 32, 64], BF16, pes) for _ in range(2)]
                            Nk = [c.sb([64, 32, 64], BF16, pes) for _ in range(2)]
                            INk = c.sb([64, 32, 64], BF16, pes)
                            Pt = [c.sb([64, 32, 64], BF16, pes) for _ in range(2)]
                            Rl, RD, RgL, ReB, Rkbg, RIN = [Reg() for _ in range(6)]
                            RMk = [[Reg() for _ in range(4)] for _ in range(2)]
                            RNk = [[Reg() for _ in range(4)] for _ in range(2)]
                            RPt = [[Reg() for _ in range(4)] for _ in range(2)]
                            RINg = [Reg() for _ in range(4)]
                            RNfg = [Reg() for _ in range(4)]
                            c.tt("dve", labt[:], la[:, :, h:h + 1].to_broadcast([64, 32, 64]),
                                 c64f[:, 0:1, :].to_broadcast([64, 32, 64]), ALU.mult, [Rs, Rc], [Rl])
                            _chk(c, "G0a")
                            l2 = labt[:].rearrange("p a b -> p (a b)")
                            for tb in range(4):
                                pb, Rp = c.bank()
                                c.mm(pb[:], onesf128, l2[:, tb * 512:(tb + 1) * 512], True, True, [Rc, Rl], [Rp])
                                c.act(egB[:, tb * 512:(tb + 1) * 512], pb[:], AF.Exp, [Rp], [ReB])
                                c.tt("dve", Dm[:, tb * 8:(tb + 1) * 8, :],
                                     gcol[:, tb * 8:(tb + 1) * 8, h:h + 1].to_broadcast([64, 8, 64]),
                                     pb[0:64, :].rearrange("p (a b) -> p a b", a=8), ALU.subtract, [Rp, Rs, ReB], [RD])
                            _chk(c, "G0b")
                            c.ts("dve", gamL[:], Dm[:], 0.0, None, ALU.min, None, [RD], [RgL])
                            c.ts("dve", Dm[:], Dm[:], -1.0, 0.0, ALU.mult, ALU.min, [RD, RgL], [RD])
                            gamU, RgU = Dm, RD
                            c.act(gamL[:], gamL[:], AF.Exp, [RgL], [RgL])
                            c.act(gamU[:], gamU[:], AF.Exp, [RgU], [RgU])
                            c.tt("dve", gamL[:], gamL[:], c64f[:, 1:2, :].to_broadcast([64, 32, 64]), ALU.mult, [RgL, Rc], [RgL])
                            c.tt("dve", gamU[:], gamU[:], c64f[:, 2:3, :].to_broadcast([64, 32, 64]), ALU.mult, [RgU, Rc], [RgU])
                            _chk(c, "G0c")
                            c.tt("dve", qdT[:], gQT[:], egB[:], ALU.mult, [RgQ, ReB], [Rqd])
                            c.tt("dve", kbg[:], gKtok[:], sc_kbg[:, :, h:h + 1].to_broadcast([64, 32, 128]), ALU.mult,
                                 [RgKt, Rs], [Rkbg])
                            c.tt("dve", kdec[:], gKtok[:], sc_kdec[:, :, h:h + 1].to_broadcast([64, 32, 128]), ALU.mult,
                                 [RgKt, Rs], [Rkd])
                            c.tt("dve", vb[:], gVtok[:], beta[:, :, h:h + 1].to_broadcast([64, 32, 128]), ALU.mult,
                                 [RgVt, Rs], [Rvb])
                            _chk(c, "G1")
                            Nf = labt
                            S.wait_all("dve", [Rl])
                            for grp in range(4):
                                pb, Rp = c.bank()
                                for q in range(8):
                                    n = grp * 8 + q
                                    ks = gKT[:, n * 64:(n + 1) * 64]
                                    c.mm(pb[0:64, q * 64:(q + 1) * 64], ks, ks, True, True, [RgK], [Rp], signal=(q == 7))
                                gs = slice(grp * 8, grp * 8 + 8)
                                pv = pb[0:64, :].rearrange("p (a b) -> p a b", a=8)
                                c.tt("dve", Nf[:, gs, :], pv, gamL[:, gs, :], ALU.mult, [Rp, RgL, Rl], [RNfg[grp]])
                                c.tt("dve", Nf[:, gs, :], Nf[:, gs, :], beta[:, gs, h:h + 1].to_broadcast([64, 8, 64]), ALU.mult,
                                     [RNfg[grp], Rs], [RNfg[grp]])
                                c.cp("act", Nk[0][:, gs, :], Nf[:, gs, :], [RNfg[grp]], [RNk[0][grp]])
                                pb2, Rp2 = c.bank()
                                for q in range(8):
                                    n = grp * 8 + q
                                    c.mm(pb2[0:64, q * 64:(q + 1) * 64], gKT[:, n * 64:(n + 1) * 64], gQT[:, n * 64:(n + 1) * 64],
                                         True, True, [RgK, RgQ], [Rp2], signal=(q == 7))
                                c.tt("dve", QKg[:, gs, :], pb2[0:64, :].rearrange("p (a b) -> p a b", a=8), gamU[:, gs, :], ALU.mult,
                                     [Rp2, RgU], [Rqk])
                                pb3, Rp3 = c.bank()
                                for q in range(8):
                                    n = grp * 8 + q
                                    c.tr(pb3[0:64, q * 64:(q + 1) * 64], Nf[:, n, :], id64f, [RNfg[grp], Rc], [Rp3], signal=(q == 7))
                                pv3 = pb3[0:64, :].rearrange("p (a b) -> p a b", a=8)
                                c.cp("act", Mk[0][:, gs, :], pv3, [Rp3], [RMk[0][grp]])
                                c.stt("dve", Pt[0][:, gs, :], pv3, -1.0, c64f[:, 3:4, :].to_broadcast([64, 8, 64]), ALU.mult, ALU.add,
                                      [Rp3, Rc, RMk[0][grp]], [RPt[0][grp]])
                            _chk(c, "G2")
                            for lv in range(5):
                                a, b = lv % 2, (lv + 1) % 2
                                last = lv == 4
                                for grp in range(4):
                                    gs = slice(grp * 8, grp * 8 + 8)
                                    pbN, RpN = c.bank()
                                    for q in range(8):
                                        n = grp * 8 + q
                                        c.mm(pbN[0:64, q * 64:(q + 1) * 64], Mk[a][:, n, :], Nk[a][:, n, :], True, True,
                                             [RMk[a][grp], RNk[a][grp]], [RpN], signal=(q == 7))
                                    pvN = pbN[0:64, :].rearrange("p (a b) -> p a b", a=8)
                                    c.stt("dve", INk[:, gs, :], pvN, 1.0, c64f[:, 3:4, :].to_broadcast([64, 8, 64]), ALU.mult, ALU.add,
                                          [RpN, Rc], [RINg[grp]])
                                    if not last:
                                        c.cp("act", Nk[b][:, gs, :], pvN, [RpN, RINg[grp]], [RNk[b][grp]])
                                        pbM, RpM = c.bank()
                                        for q in range(8):
                                            n = grp * 8 + q
                                            c.mm(pbM[0:64, q * 64:(q + 1) * 64], Nk[a][:, n, :], Mk[a][:, n, :], True, True,
                                                 [RMk[a][grp], RNk[a][grp]], [RpM], signal=(q == 7))
                                        c.cp("act", Mk[b][:, gs, :], pbM[0:64, :].rearrange("p (a b) -> p a b", a=8), [RpM], [RMk[b][grp]])
                                    pbP, RpP = c.bank()
                                    for q in range(8):
                                        n = grp * 8 + q
                                        c.mm(pbP[0:64, q * 64:(q + 1) * 64], INk[:, n, :], Pt[a][:, n, :], True, True,
                                             [RINg[grp], RPt[a][grp]], [RpP], signal=(q == 7))
                                    pvP = pbP[0:64, :].rearrange("p (a b) -> p a b", a=8)
                                    if last:
                                        c.cp("dve", P[:, gs, :], pvP, [RpP], [RP])
                                    else:
                                        c.cp("dve", Pt[b][:, gs, :], pvP, [RpP], [RPt[b][grp]])
                            for grp in range(4):
                                pb, Rp = c.bank()
                                for q in range(8):
                                    n = grp * 8 + q
                                    c.mm(pb[:, q * 64:(q + 1) * 64], kbg[:, n, :], P[:, n, :], True, True, [Rkbg, RP], [Rp],
                                         signal=(q == 7))
                                c.ts("dve", nwT[:, grp * 512:(grp + 1) * 512], pb[:], -1.0, None, ALU.mult, None, [Rp], [Rnw])
                            S.barrier()

                        _chk(c, "GPREP")
                        with ExitStack() as res:
                            Sf = c.sb([128, 128], F32, res)
                            Sb = [c.sb([128, 128], BF16, res) for _ in range(2)]
                            vn = [c.sb([64, 128], BF16, res) for _ in range(2)]
                            oT = c.sb([128, SEQ], F32, res, name=f"goT{h}")
                            sqb = c.sb([128, SEQ], BF16, res)
                            rn = c.sb([128, SEQ], F32, res)
                            yst = c.sb([128, SEQ], BF16, res)
                            RSf, RoT, Rsq, Rrn, Ryst = [Reg() for _ in range(5)]
                            RSb = [Reg(), Reg()]
                            Rvn = [Reg(), Reg()]
                            c.S.op("dve", lambda e: e.memset(Sf[:], 0.0), writes=[RSf])
                            c.S.op("pool", lambda e: e.memset(Sb[0][:], 0.0), writes=[RSb[0]])
                            for n in range(32):
                                a, b = n % 2, (n + 1) % 2
                                vps, Rvps = c.pb[n % 2], c.Rpb[n % 2]
                                ops, Rops = c.pb[2 + (n // 8) % 2], c.Rpb[2 + (n // 8) % 2]
                                dps, Rdps = c.pb[4 + n % 2], c.Rpb[4 + n % 2]
                                q = n % 8
                                c.mm(vps[0:64, 0:128], P[:, n, :], vb[:, n, :], True, False, [RP, Rvb], [Rvps], signal=False)
                                c.mm(vps[0:64, 0:128], nwT[:, n * 64:(n + 1) * 64], Sb[a][:], False, True, [Rnw, RSb[a]], [Rvps])
                                c.cp("act", vn[a][:], vps[0:64, 0:128], [Rvps], [Rvn[a]])
                                c.mm(ops[:, q * 64:(q + 1) * 64], Sb[a][:], qdT[:, n * 64:(n + 1) * 64], True, False,
                                     [RSb[a], Rqd], [Rops], signal=False)
                                c.mm(ops[:, q * 64:(q + 1) * 64], vn[a][:], QKg[:, n, :], False, True, [Rvn[a], Rqk], [Rops])
                                c.mm(dps[:, 0:128], kdec[:, n, :], vn[a][:], True, True, [Rkd, Rvn[a]], [Rdps])
                                c.stt("dve", Sb[b][:], Sf[:], decB[:, n, h:h + 1], dps[:, 0:128], ALU.mult, ALU.add,
                                      [RSf, Rs, Rdps], [RSb[b]])
                                c.stt("dve", Sf[:], Sf[:], decB[:, n, h:h + 1], dps[:, 0:128], ALU.mult, ALU.add,
                                      [RSf, Rs, Rdps], [RSf])
                                if q == 7:
                                    c.cp("act", oT[:, (n - 7) * 64:(n + 1) * 64], ops[:], [Rops], [RoT])
                            c.act(sqb[:], oT[:], AF.Square, [RoT], [Rsq])
                            for tb in range(4):
                                pb, Rp = c.bank()
                                c.mm(pb[:], onesb, sqb[:, tb * 512:(tb + 1) * 512], True, True, [Rc, Rsq], [Rp])
                                c.act(rn[:, tb * 512:(tb + 1) * 512], pb[:], AF.Ln, [Rp], [Rrn], scale=1.0 / 128, bias=EPS)
                            c.act(rn[:], rn[:], AF.Exp, [Rrn], [Rrn], scale=-0.5)
                            c.stt("dve", rn[:], oT[:], gn[:, 0:1], rn[:], ALU.mult, ALU.mult, [RoT, Rrn, Rc], [Rrn])
                            c.tt("dve", yst[:], rn[:], gZ[:], ALU.mult, [Rrn, RgZ], [Ryst])
                            c.store("sp", y_d[896 + h * 128:896 + (h + 1) * 128, :], yst[:], [Ryst], [Ry], "go")
                            S.barrier()
        S.barrier()


OFF_P, OFF_SQ, OFF_SK, OFF_SV = 0, 512, 1280, 2048
OFF_GQ, OFF_GK, OFF_GV, OFF_GZ, OFF_A, OFF_B, OFF_G = 2816, 3584, 4352, 5120, 5888, 5894, 5900


def _blk(w):
    n = w.shape[1]
    return np.ascontiguousarray(w.reshape(16, 128, n).transpose(1, 0, 2))


def p1_consts():
    j = np.arange(128)
    cst = np.zeros((128, 5, 128), np.float32)
    cst[:, 0, :] = np.eye(128)
    cst[:, 1, :] = (j[:, None] >= j[None, :])
    cst[:, 2, :] = (j[:, None] < j[None, :])
    cst[:, 3, :] = 1.0
    mb = np.zeros((128, 4, 512), np.float32)
    q = np.arange(512)
    for jj in range(4):
        mb[:, jj, :] = np.where((128 * jj + j[:, None]) < q[None, :], 0.0, NEG)
    i = np.arange(64)
    c64 = np.zeros((64, 5, 64), np.float32)
    c64[:, 0, :] = (i[:, None] <= i[None, :])
    c64[:, 1, :] = (i[:, None] > i[None, :])
    c64[:, 2, :] = (i[None, :] >= i[:, None])
    c64[:, 3, :] = np.eye(64)
    c64[:, 4, :] = 1.0
    return cst, mb, c64


def p1_inputs(inp, l, hh, consts):
    cst, mb, c64 = consts
    W = inp["w_in"][l]
    hg = [hh * 3 + h for h in range(3)]
    cols = [OFF_P + g * 128 for g in range(4)]
    for off in (OFF_SQ, OFF_SK, OFF_GQ, OFF_GK, OFF_GV, OFF_GZ):
        cols += [off + h * 128 for h in hg]
    wF = np.stack([_blk(W[:, c0:c0 + 128]) for c0 in cols])
    wV = _blk(np.concatenate([W[:, OFF_SV + h * 128:OFF_SV + (h + 1) * 128] for h in hg], axis=1))
    wAB = _blk(np.concatenate([W[:, [OFF_A + h for h in hg]], W[:, [OFF_B + h for h in hg]]], axis=1))
    poolw = np.ascontiguousarray(inp["pool_w"][l].transpose(1, 0, 2))
    poolc = np.zeros((128, 4, 18), np.float32)
    t = np.arange(16)
    for g in range(4):
        w = 2 ** (g + 1)
        poolc[:, g, 0] = inp["pool_scale"][l][g * 128:(g + 1) * 128]
        poolc[:, g, 1] = 1.0 / w
        poolc[:, g, 2:18] = (1.0 / np.minimum(t + 1, w))[None, :]
    convw = np.zeros((128, 9, 4), np.float32)
    for kind in range(3):
        for h in range(3):
            ch0 = kind * 768 + hg[h] * 128
            convw[:, kind * 3 + h, :] = inp["gdn_conv"][l][:, ch0:ch0 + 128].T
    gsc = np.zeros((64, 3, 32, 3), np.float32)
    for h in range(3):
        gsc[:, 0, :, h] = inp["gdn_dt_bias"][l][hg[h]]
        gsc[:, 1, :, h] = inp["gdn_a_log"][l][hg[h]]
    gn = np.ascontiguousarray(inp["gdn_norm"][l][:, None])
    gain = np.ascontiguousarray(np.broadcast_to(inp["attn_norm"][l][None, :], (128, D)))
    return dict(gain=gain, wF=wF, wV=wV, wAB=wAB, cst=cst, mb=mb, c64=c64, poolw=poolw, poolc=poolc,
                convw=convw, gsc=gsc, gn=gn)


TT = 1024
NT = TT // 128


def build_p2(final):
    nc = bass.Bass("TRN2", target_bir_lowering=False)
    dt = lambda name, shape, kind="ExternalInput", dty=F32: nc.dram_tensor(name, shape, dty, kind=kind).ap()
    x_d = dt("x", [TT, D])
    yT_d = dt("yT", [D, TT], dty=BF16)
    o_d = dt("o", [TT, D], kind="ExternalOutput")
    wd = dict(gain1=dt("gain1", [128, D]), gain2=dt("gain2", [128, D]), gainF=dt("gainF", [128, D]),
              ident=dt("ident", [128, 128]),
              wG=dt("wG", [48, 128, 16, 128]), wUp=dt("wUp", [16, 128, 16, 128]), wO=dt("wO", [4, 128, 16, 512]),
              wF1=dt("wF1", [64, 128, 16, 128]), wF2=dt("wF2", [8, 4, 128, 8, 512]))
    with ExitStack() as es:
        S = Sched(nc, es)
        c = Ctx(nc, S, es)
        Ro = Reg()
        emit_p2(c, S, x_d, lambda kc: yT_d[kc * 128:(kc + 1) * 128, :], o_d, wd, Ro, final)
        S.barrier()
        S.wait_all("sp", [Ro])
    return nc


def emit_p2(c, S, x_d, yrows, o_d, wd, Ro, final):
    with ExitStack() as es:
        identf = c.sb([128, 128], F32, es)
        Rc = Reg()
        c.load("sp", identf[:], wd["ident"], [Rc], "c0")
        xs = c.sb([128, NT, D], F32, es, name="xres")
        Rx = [Reg() for _ in range(NT)]
        for tt in range(NT):
            c.load("sp", xs[:, tt, :], x_d[tt * 128:(tt + 1) * 128, :], [Rx[tt]], f"xl{tt % 2}")
        NWB = 3
        wcount = [0]

        def stream_blocks(src_list, wblk, Rw, body):
            nb = len(wblk)
            ahead = nb - 1

            def load_w(k):
                s = wcount[0] % nb
                wcount[0] += 1
                c.load("pool", wblk[s][:], src_list[k], [Rw[s]], f"w{s}")
                return s
            slots = {}
            for k in range(min(ahead, len(src_list))):
                slots[k] = load_w(k)
            for k in range(len(src_list)):
                if k + ahead < len(src_list):
                    slots[k + ahead] = load_w(k + ahead)
                s = slots[k]
                body(k, wblk[s], Rw[s])

        def do_norm(gain_key, uT, RuT, pes, extra=None):
            gain_sb = c.sb([128, D], F32, pes)
            Rg = Reg()
            c.load("sp", gain_sb[:], wd[gain_key], [Rg], "gl")
            scrs = []
            for _ in range(2):
                sq = c.sb([128, D], BF16, pes)
                ss = c.sb([128, 4], F32, pes)
                u32 = c.sb([128, D], F32, pes)
                scrs.append((sq, Reg(), ss, Reg(), u32, Reg()))
            RuTs = [Reg() for _ in range(NT)]
            for tt in range(NT):
                norm_transpose(c, xs[:, tt, :], Rx[tt], gain_sb, Rg, identf, Rc, uT, RuTs[tt], tt * 128, scrs[tt % 2])
            if extra is not None:
                extra()
            S.barrier()

        with ExitStack() as aes:
            uT = c.sb([128, 16, TT], BF16, aes, name="uT2")
            yT = c.sb([128, 16, TT], BF16, aes, name="yT2")
            mT = c.sb([128, 16, TT], BF16, aes, name="mT2")
            RuT, RyT, RmT = Reg(), Reg(), Reg()
            def load_y():
                for kc in range(16):
                    c.load("sp", yT[:, kc, :], yrows(kc), [RyT], f"yl{kc % 2}")
            with ExitStack() as pes:
                do_norm("gain1", uT, RuT, pes, extra=load_y)
            S.barrier()
            with ExitStack() as pes:
                wblk = [c.sb([128, 16, 128], BF16, pes) for _ in range(NWB)]
                Rw = [Reg() for _ in range(NWB)]
                sig = [c.sb([128, 512], F32, pes) for _ in range(2)]
                Rsig = [Reg(), Reg()]
                macc = [c.sb([128, 512], F32, pes) for _ in range(2)]
                Rmacc = [Reg(), Reg()]
                tmpm = c.sb([128, 512], F32, pes)
                Rtmp = Reg()
                srcs = []
                for dc in range(16):
                    srcs += [wd["wG"][dc * 3 + br] for br in range(3)] + [wd["wUp"][dc]]
                kr = [(0, 4), (4, 10), (10, 16)]
                cnt = [0]

                def body(k, wb, Rwb):
                    dc, j = k // 4, k % 4
                    if j < 3:
                        body.gw[j] = (wb, Rwb)
                        return
                    for tb in range(2):
                        tsl = slice(tb * 512, (tb + 1) * 512)
                        mi = cnt[0] % 2
                        cnt[0] += 1
                        for br in range(3):
                            gwb, Rgw = body.gw[br]
                            pg, Rpg = c.bank()
                            for kc in range(16):
                                c.mm(pg[:], gwb[:, kc, :], uT[:, kc, tsl], kc == 0, kc == 15, [Rgw, RuT], [Rpg],
                                     signal=(kc == 15))
                            si = (cnt[0] + br) % 2
                            c.act(sig[si][:], pg[:], AF.Sigmoid, [Rpg], [Rsig[si]])
                            pu, Rpu = c.bank()
                            k0, k1 = kr[br]
                            for kc in range(k0, k1):
                                c.mm(pu[:], wb[:, kc, :], yT[:, kc, tsl], kc == k0, kc == k1 - 1, [Rwb, RyT], [Rpu],
                                     signal=(kc == k1 - 1))
                            if br == 0:
                                c.tt("dve", macc[mi][:], pu[:], sig[si][:], ALU.mult, [Rpu, Rsig[si]], [Rmacc[mi]])
                            else:
                                c.tt("dve", tmpm[:], pu[:], sig[si][:], ALU.mult, [Rpu, Rsig[si]], [Rtmp])
                                if br == 1:
                                    c.tt("pool", macc[mi][:], macc[mi][:], tmpm[:], ALU.add, [Rmacc[mi], Rtmp], [Rmacc[mi]])
                                else:
                                    c.tt("pool", mT[:, dc, tsl], macc[mi][:], tmpm[:], ALU.add, [Rmacc[mi], Rtmp], [RmT])
                body.gw = {}
                wblk5 = wblk + [c.sb([128, 16, 128], BF16, pes) for _ in range(5)]
                Rw5 = Rw + [Reg() for _ in range(5)]
                NW5 = 8
                w5 = [0]

                def load5(k):
                    s = w5[0] % NW5
                    w5[0] += 1
                    c.load("pool", wblk5[s][:], srcs[k], [Rw5[s]], f"v{s}")
                    return s
                slots = {}
                for k in range(4):
                    slots[k] = load5(k)
                for k in range(len(srcs)):
                    if k + 4 < len(srcs):
                        slots[k + 4] = load5(k + 4)
                    body(k, wblk5[slots[k]], Rw5[slots[k]])
                S.barrier()
            with ExitStack() as pes:
                wo = [c.sb([128, 16, 512], BF16, pes) for _ in range(2)]
                Rwo = [Reg(), Reg()]
                c.load("pool", wo[0][:], wd["wO"][0], [Rwo[0]], "wo0")
                for ob in range(4):
                    if ob + 1 < 4:
                        c.load("pool", wo[(ob + 1) % 2][:], wd["wO"][ob + 1], [Rwo[(ob + 1) % 2]], f"wo{(ob + 1) % 2}")
                    for tt in range(NT):
                        pb, Rp = c.bank()
                        for kc in range(16):
                            c.mm(pb[:], mT[:, kc, tt * 128:(tt + 1) * 128], wo[ob % 2][:, kc, :], kc == 0, kc == 15,
                                 [RmT, Rwo[ob % 2]], [Rp], signal=(kc == 15))
                        c.tt("dve", xs[:, tt, ob * 512:(ob + 1) * 512], xs[:, tt, ob * 512:(ob + 1) * 512], pb[:], ALU.add,
                             [Rp, Rx[tt]], [Rx[tt]])
                S.barrier()

        with ExitStack() as mes:
            uT = c.sb([128, 16, TT], BF16, mes, name="u2T")
            RuT = Reg()
            with ExitStack() as pes:
                do_norm("gain2", uT, RuT, pes)
            S.barrier()
            hT = c.sb([128, 8, TT], BF16, mes, name="hT")
            RhT = Reg()
            wblk = [c.sb([128, 16, 128], BF16, mes) for _ in range(6)]
            Rw = [Reg() for _ in range(6)]
            w2 = [c.sb([128, 8, 512], BF16, mes) for _ in range(3)]
            Rw2 = [Reg(), Reg(), Reg()]
            rl = [c.sb([128, 512], F32, mes) for _ in range(2)]
            Rrl = [Reg(), Reg()]
            w2c = [0]

            def load2(g, ob):
                s = w2c[0] % 3
                w2c[0] += 1
                c.load("pool", w2[s][:], wd["wF2"][g, ob], [Rw2[s]], f"w2{s}")
                return s
            rc = [0]
            for g in range(8):
                srcs = [wd["wF1"][g * 8 + cb] for cb in range(8)]

                def body(k, wb, Rwb):
                    for tb in range(2):
                        tsl = slice(tb * 512, (tb + 1) * 512)
                        pb, Rp = c.bank()
                        for kc in range(16):
                            c.mm(pb[:], wb[:, kc, :], uT[:, kc, tsl], kc == 0, kc == 15, [Rwb, RuT], [Rp], signal=(kc == 15))
                        ri = rc[0] % 2
                        rc[0] += 1
                        c.act(rl[ri][:], pb[:], AF.Relu, [Rp], [Rrl[ri]])
                        c.tt("dve" if ri else "pool", hT[:, k, tsl], rl[ri][:], rl[ri][:], ALU.mult, [Rrl[ri]], [RhT])
                stream_blocks(srcs, wblk, Rw, body)
                q2 = [load2(g, 0), load2(g, 1)]
                for ob in range(4):
                    cur = q2.pop(0)
                    if ob + 2 < 4:
                        q2.append(load2(g, ob + 2))
                    for tt in range(NT):
                        pb, Rp = c.bank()
                        for kc in range(8):
                            c.mm(pb[:], hT[:, kc, tt * 128:(tt + 1) * 128], w2[cur][:, kc, :], kc == 0, kc == 7,
                                 [RhT, Rw2[cur]], [Rp], signal=(kc == 7))
                        c.tt("dve", xs[:, tt, ob * 512:(ob + 1) * 512], xs[:, tt, ob * 512:(ob + 1) * 512], pb[:], ALU.add,
                             [Rp, Rx[tt]], [Rx[tt]])
            S.barrier()

        if final:
            with ExitStack() as pes:
                gain_sb = c.sb([128, D], F32, pes)
                Rg = Reg()
                c.load("sp", gain_sb[:], wd["gainF"], [Rg], "gl")
                sq = c.sb([128, D], F32, pes)
                ss = c.sb([128, 4], F32, pes)
                ob_ = [c.sb([128, D], F32, pes) for _ in range(2)]
                Rob = [Reg(), Reg()]
                Rsq, Rss = Reg(), Reg()
                for tt in range(NT):
                    b = tt % 2
                    c.S.op("act", lambda e, tt=tt: e.activation(out=sq[:], in_=xs[:, tt, :], func=AF.Square, accum_out=ss[:, 0:1]),
                           reads=[Rx[tt]], writes=[Rsq, Rss])
                    c.act(ss[:, 1:2], ss[:, 0:1], AF.Ln, [Rss], [Rss], scale=1.0 / D, bias=EPS)
                    c.act(ss[:, 2:3], ss[:, 1:2], AF.Exp, [Rss], [Rss], scale=-0.5)
                    c.stt("dve", ob_[b][:], xs[:, tt, :], ss[:, 2:3], gain_sb[:], ALU.mult, ALU.mult, [Rx[tt], Rss, Rg], [Rob[b]])
                    c.store("sp", o_d[tt * 128:(tt + 1) * 128, :], ob_[b][:], [Rob[b]], [Ro], f"os{b}")
                S.barrier()
        else:
            for tt in range(NT):
                c.store("sp", o_d[tt * 128:(tt + 1) * 128, :], xs[:, tt, :], [Rx[tt]], [Ro], f"os{tt % 2}")
            S.barrier()


def p2_inputs(inp, l):
    W = inp["w_in"][l]
    wG = np.stack([_blk(W[:, OFF_G + br * D + dc * 128:OFF_G + br * D + (dc + 1) * 128]) for dc in range(16) for br in range(3)])
    Wup = np.concatenate([inp["w_pool_up"][l], inp["w_sb_up"][l], inp["w_gdn_up"][l]], axis=0)
    wUp = np.stack([_blk(Wup[:, dc * 128:(dc + 1) * 128]) for dc in range(16)])
    wO = np.stack([_blk(inp["w_out"][l][:, ob * 512:(ob + 1) * 512]) for ob in range(4)])
    wF1 = np.stack([_blk(inp["w_ff1"][l][:, cb * 128:(cb + 1) * 128]) for cb in range(64)])
    W2 = inp["w_ff2"][l]
    wF2 = np.stack([np.stack([np.ascontiguousarray(
        W2[g * 1024:(g + 1) * 1024, ob * 512:(ob + 1) * 512].reshape(8, 128, 512).transpose(1, 0, 2)) for ob in range(4)])
        for g in range(8)])
    bc = lambda v: np.ascontiguousarray(np.broadcast_to(v[None, :], (128, D)))
    return dict(gain1=bc(inp["attn_norm"][l]), gain2=bc(inp["mlp_norm"][l]), gainF=bc(inp["final_norm"]),
                ident=np.eye(128, dtype=np.float32), wG=wG, wUp=wUp, wO=wO, wF1=wF1, wF2=wF2)


def kernel_unfused(**inputs):
    inp = {k: np.asarray(v) for k, v in inputs.items()}
    x = np.ascontiguousarray(inp["x"], dtype=np.float32)
    consts = p1_consts()
    cores = list(range(8))
    depth = inp["w_in"].shape[0]
    for l in range(depth):
        p1h = [p1_inputs(inp, l, hh, consts) for hh in range(2)]
        in_maps = []
        for core in cores:
            m = dict(p1h[core % 2])
            m["x"] = np.ascontiguousarray(x[core // 2])
            in_maps.append(m)
        res = run_bass_kernel_spmd(build_p1(), in_maps, core_ids=cores)
        ys = [np.asarray(res.results[core]["y"]) for core in cores]
        del in_maps, p1h
        base = p2_inputs(inp, l)
        in_maps = []
        for core in cores:
            b, half = core // 2, core % 2
            y0, y1 = ys[2 * b], ys[2 * b + 1]
            tsl = slice(half * TT, (half + 1) * TT)
            yT = np.concatenate([y0[0:512, tsl], y0[512:896, tsl], y1[512:896, tsl], y0[896:1280, tsl], y1[896:1280, tsl]], axis=0)
            m = dict(base)
            m["x"] = np.ascontiguousarray(x[b, tsl, :])
            m["yT"] = np.ascontiguousarray(yT)
            in_maps.append(m)
        res = run_bass_kernel_spmd(build_p2(l == depth - 1), in_maps, core_ids=cores)
        x = np.stack([np.asarray(res.results[core]["o"]) for core in cores]).reshape(NB, SEQ, D)
        del in_maps, base
    return np.ascontiguousarray(x, dtype=np.float32)


P1_KEYS = ("gain", "wF", "wV", "wAB", "poolw", "poolc", "convw", "gsc", "gn")
P1_SHAPES = dict(gain=[128, D], wF=[NFB, 128, 16, 128], wV=[128, 16, 384], wAB=[128, 16, 6], poolw=[128, 4, 128],
                 poolc=[128, 4, 18], convw=[128, 9, 4], gsc=[64, 3, 32, 3], gn=[128, 1])
P2_SHAPES = dict(gain1=[128, D], gain2=[128, D], wG=[48, 128, 16, 128], wUp=[16, 128, 16, 128], wO=[4, 128, 16, 512],
                 wF1=[64, 128, 16, 128], wF2=[8, 4, 128, 8, 512])


def build_fused(depth):
    nc = bass.Bass("TRN2", target_bir_lowering=False)
    dt = lambda name, shape, kind="ExternalInput", dty=F32: nc.dram_tensor(name, shape, dty, kind=kind).ap()
    x_d = dt("x", [SEQ, D])
    o_d = dt("o", [SEQ, D], kind="ExternalOutput")
    shared = dict(cst=dt("cst", [128, 5, 128]), mb=dt("mb", [128, 4, 512]), c64=dt("c64", [64, 5, 64]),
                  ident=dt("ident", [128, 128]), gainF=dt("gainF", [128, D]))
    xs = dt("xs_scratch", [SEQ, D], kind="Internal")
    ys = [dt(f"ys_scratch{hh}", [1280, SEQ], kind="Internal", dty=BF16) for hh in range(2)]
    w1 = {}
    w2 = {}
    for l in range(depth):
        for hh in range(2):
            d1 = {k: dt(f"{k}_{l}_{hh}", P1_SHAPES[k]) for k in P1_KEYS}
            d1.update(cst=shared["cst"], mb=shared["mb"], c64=shared["c64"])
            w1[(l, hh)] = d1
        d2 = {k: dt(f"{k}_{l}", P2_SHAPES[k]) for k in P2_SHAPES}
        d2.update(ident=shared["ident"], gainF=shared["gainF"])
        w2[l] = d2

    def yrows_for(t):
        tsl = slice(t * TT, (t + 1) * TT)

        def yrows(kc):
            if kc < 4:
                return ys[0][kc * 128:(kc + 1) * 128, tsl]
            if kc < 7:
                return ys[0][512 + (kc - 4) * 128:512 + (kc - 3) * 128, tsl]
            if kc < 10:
                return ys[1][512 + (kc - 7) * 128:512 + (kc - 6) * 128, tsl]
            if kc < 13:
                return ys[0][896 + (kc - 10) * 128:896 + (kc - 9) * 128, tsl]
            return ys[1][896 + (kc - 13) * 128:896 + (kc - 12) * 128, tsl]
        return yrows

    with ExitStack() as es:
        S = Sched(nc, es)
        c = Ctx(nc, S, es)
        Ro = Reg()
        for l in range(depth):
            src = x_d if l == 0 else xs
            last = l == depth - 1
            dst = o_d if last else xs
            Ry = Reg()
            emit_p1(c, S, es, src, ys, [w1[(l, 0)], w1[(l, 1)]], Ry)
            S.barrier()
            for t in range(2):
                emit_p2(c, S, src[t * TT:(t + 1) * TT, :], yrows_for(t), dst[t * TT:(t + 1) * TT, :], w2[l], Ro, last)
                S.barrier()
        S.barrier()
        S.wait_all("sp", [Ro])
        print("sem counts", S.count, max(S.dcount.values()))
    return nc


def kernel(**inputs):
    inp = {k: np.asarray(v) for k, v in inputs.items()}
    x = np.ascontiguousarray(inp["x"], dtype=np.float32)
    depth = inp["w_in"].shape[0]
    cst, mb, c64 = p1_consts()
    base = dict(cst=cst, mb=mb, c64=c64, ident=np.eye(128, dtype=np.float32))
    for l in range(depth):
        for hh in range(2):
            m = p1_inputs(inp, l, hh, (cst, mb, c64))
            for k in P1_KEYS:
                base[f"{k}_{l}_{hh}"] = m[k]
        m = p2_inputs(inp, l)
        for k in P2_SHAPES:
            base[f"{k}_{l}"] = m[k]
        base["gainF"] = m["gainF"]
    cores = list(range(NB))
    in_maps = []
    for b in cores:
        m = dict(base)
        m["x"] = np.ascontiguousarray(x[b])
        in_maps.append(m)
    res = run_bass_kernel_spmd(build_fused(depth), in_maps, core_ids=cores)
    out = np.stack([np.asarray(res.results[b]["o"]) for b in cores])
    return np.ascontiguousarray(out, dtype=np.float32)
```
